# Optimizing a Trainium2 kernel written in Bass

```python
import jax, jax.numpy as jnp
from jax import lax
import numpy as np

D_MODEL = 1024
BATCH = 2
SEQ = 8192
DEPTH = 1

N_HEADS = 8
HEAD_DIM = 64
ATTN_WIDTH = N_HEADS * HEAD_DIM
CONV_WIDTH = D_MODEL // 2
CONV_K = 3
FFN_CONV_K = 3
D_FF = 2816
Q_BLOCK = 128
N_MOD = 6
RMS_EPS = 1e-6
NEG_INF = -1e30
IN_SPLITS = [CONV_WIDTH, CONV_WIDTH, CONV_WIDTH,
             ATTN_WIDTH, ATTN_WIDTH, ATTN_WIDTH,
             N_HEADS,
             D_MODEL, D_MODEL]
IN_WIDTH = sum(IN_SPLITS)
IN_OFFSETS = list(np.cumsum(IN_SPLITS)[:-1])

kernel_name = "hybrid_shortconv_fox_convffn_adaln"


def rmsnorm(x, g):
    xf = x.astype(jnp.float32)
    inv = lax.rsqrt(jnp.mean(xf * xf, axis=-1, keepdims=True) + RMS_EPS)
    return (xf * inv).astype(x.dtype) * g


def causal_dwconv(u, w):
    K = w.shape[0]
    S = u.shape[1]
    up = jnp.pad(u, ((0, 0), (K - 1, 0), (0, 0)))
    out = up[:, 0:S, :] * w[0]
    for k in range(1, K):
        out = out + up[:, k:k + S, :] * w[k]
    return out


def forgetting_attention(q, k, v, log_f):
    B, S, H, hd = q.shape
    nb = S // Q_BLOCK
    scale = 1.0 / np.sqrt(hd)
    F = jnp.cumsum(log_f, axis=1)
    qb = q.reshape(B, nb, Q_BLOCK, H, hd).transpose(1, 0, 3, 2, 4)
    Fq = F.reshape(B, nb, Q_BLOCK, H).transpose(1, 0, 3, 2)
    kh = k.transpose(0, 2, 1, 3)
    vh = v.transpose(0, 2, 1, 3)
    Fk = F.transpose(0, 2, 1)
    kpos = jnp.arange(S)

    def one_block(args):
        i, qi, Fqi = args
        qpos = i * Q_BLOCK + jnp.arange(Q_BLOCK)
        s = jnp.einsum('bhqd,bhkd->bhqk', qi, kh).astype(jnp.float32) * scale
        s = s + (Fqi[..., None] - Fk[:, :, None, :])
        s = jnp.where(kpos[None, :] <= qpos[:, None], s, NEG_INF)
        p = jax.nn.softmax(s, axis=-1)
        return jnp.einsum('bhqk,bhkd->bhqd', p.astype(vh.dtype), vh)

    o = lax.map(one_block, (jnp.arange(nb), qb, Fq))
    return o.transpose(1, 0, 3, 2, 4).reshape(B, S, H * hd)


def setup_inputs(seed: int = 0) -> dict:
    key = jax.random.key(seed)
    ks = jax.random.split(key, 17)
    L = DEPTH

    def nrm(k, shape, s):
        return jax.random.normal(k, shape, jnp.float32) * s

    return {
        "x": nrm(ks[0], (BATCH, SEQ, D_MODEL), 1.0),
        "c": nrm(ks[1], (BATCH, D_MODEL), 1.0),
        "w_ada": nrm(ks[2], (L, D_MODEL, N_MOD * D_MODEL), 0.5 * D_MODEL ** -0.5),
        "b_ada": nrm(ks[3], (L, N_MOD * D_MODEL), 0.02),
        "norm1_g": 1.0 + nrm(ks[4], (L, D_MODEL), 0.02),
        "w_in": nrm(ks[5], (L, D_MODEL, IN_WIDTH), D_MODEL ** -0.5),
        "b_f": 3.0 + nrm(ks[6], (L, N_HEADS), 0.5),
        "conv_a_w": nrm(ks[7], (L, CONV_K, CONV_WIDTH), CONV_K ** -0.5),
        "q_norm_g": 1.0 + nrm(ks[8], (L, HEAD_DIM), 0.02),
        "k_norm_g": 1.0 + nrm(ks[9], (L, HEAD_DIM), 0.02),
        "w_branch_a": nrm(ks[10], (L, CONV_WIDTH, D_MODEL), CONV_WIDTH ** -0.5),
        "w_branch_b": nrm(ks[11], (L, ATTN_WIDTH, D_MODEL), ATTN_WIDTH ** -0.5),
        "w_out": nrm(ks[12], (L, D_MODEL, D_MODEL), D_MODEL ** -0.5),
        "norm2_g": 1.0 + nrm(ks[13], (L, D_MODEL), 0.02),
        "w_up": nrm(ks[14], (L, D_MODEL, 2 * D_FF), D_MODEL ** -0.5),
        "conv_ffn_w": nrm(ks[15], (L, FFN_CONV_K, 2 * D_FF), FFN_CONV_K ** -0.5),
        "w_down": nrm(ks[16], (L, D_FF, D_MODEL), D_FF ** -0.5),
    }


def reference(x, c, w_ada, b_ada, norm1_g, w_in, b_f, conv_a_w, q_norm_g, k_norm_g,
              w_branch_a, w_branch_b, w_out, norm2_g, w_up, conv_ffn_w, w_down):
    B, S, _ = x.shape
    for l in range(DEPTH):
        mod = jnp.einsum('bd,de->be', jax.nn.silu(c), w_ada[l]) + b_ada[l]
        sh1, sc1, g1, sh2, sc2, g2 = jnp.split(mod, N_MOD, axis=-1)

        h = rmsnorm(x, norm1_g[l]) * (1.0 + sc1[:, None, :]) + sh1[:, None, :]
        proj = jnp.einsum('bsd,de->bse', h, w_in[l])
        cb, cc, cv, q, k, v, f_logit, ga, gb = jnp.split(proj, IN_OFFSETS, axis=-1)

        ya = cb * causal_dwconv(cc * cv, conv_a_w[l])

        q = rmsnorm(q.reshape(B, S, N_HEADS, HEAD_DIM), q_norm_g[l])
        k = rmsnorm(k.reshape(B, S, N_HEADS, HEAD_DIM), k_norm_g[l])
        v = v.reshape(B, S, N_HEADS, HEAD_DIM)
        log_f = jax.nn.log_sigmoid((f_logit + b_f[l]).astype(jnp.float32))
        yb = forgetting_attention(q, k, v, log_f)

        ya = jnp.einsum('bsc,cd->bsd', ya, w_branch_a[l])
        yb = jnp.einsum('bsc,cd->bsd', yb, w_branch_b[l])
        merged = jax.nn.sigmoid(ga) * ya + jax.nn.sigmoid(gb) * yb
        mix = jnp.einsum('bsd,de->bse', merged, w_out[l])
        x = x + g1[:, None, :] * mix

        h2 = rmsnorm(x, norm2_g[l]) * (1.0 + sc2[:, None, :]) + sh2[:, None, :]
        u = jnp.einsum('bsd,df->bsf', h2, w_up[l])
        u = causal_dwconv(u, conv_ffn_w[l])
        gate, val = jnp.split(u, 2, axis=-1)
        ff = jnp.einsum('bsf,fd->bsd', jax.nn.silu(gate) * val, w_down[l])
        x = x + g2[:, None, :] * ff
    return x
```

```python
import numpy as np
import ml_dtypes
import concourse.bass as bass
import concourse.mybir as mybir
from concourse.bass_utils import run_bass_kernel_spmd

F32 = mybir.dt.float32
BF16 = mybir.dt.bfloat16
AF = mybir.ActivationFunctionType
ALU = mybir.AluOpType

D = 1024
KC = 8
S = 8192
NT = 64
HALO0 = 47 * 128
NOWN = 2176
W1C = 3080
DFF = 2816
NEG = -30000.0
EPS = 1e-6
OWN_GROUPS = [(0, 128), (128, 512), (640, 512), (1152, 512), (1664, 512)]
PRE_GROUPS = [(i * 512, 512) for i in range(11)] + [(5632, 384)]

ENGS = ["pe", "act", "dve", "pool", "sp"]
NDSEM = 24


class Sched:
    def __init__(self, nc, sems, dsems):
        self.nc = nc
        self.sems = sems
        self.dsems = dsems
        self.streams = {e: [] for e in ENGS}
        self.cnt = {e: 0 for e in ENGS}
        self.lastw = {}
        self.readers = {}
        self.waited = {e: {} for e in ENGS}
        self.ndma = 0
        self.out_tokens = []

    def semof(self, tok):
        if tok[0] == "c":
            return (("c", tok[1]), tok[2])
        k = tok[1]
        return (("d", k % NDSEM), 16 * (k // NDSEM + 1))

    def _need(self, eng, sv, waits):
        key, val = sv
        if self.waited[eng].get(key, 0) >= val:
            return
        self.waited[eng][key] = val
        waits.append(sv)

    def add(self, eng, fn, reads=(), writes=(), dma=False, is_out=False):
        deps = set()
        for r in reads:
            t = self.lastw.get(r)
            if t is not None:
                deps.add(t)
        for w in writes:
            t = self.lastw.get(w)
            if t is not None:
                deps.add(t)
            for t in self.readers.get(w, ()):
                deps.add(t)
        waits = []
        for tok in deps:
            if tok[0] == "c" and tok[1] == eng and not dma:
                if eng == "pe":
                    continue
                if self.cnt[eng] - tok[2] > 3:
                    continue
            self._need(eng, self.semof(tok), waits)
        if dma:
            k = self.ndma
            self.ndma += 1
            if k >= NDSEM:
                self._need(eng, (("d", k % NDSEM), 16 * (k // NDSEM)), waits)
            tok = ("d", k)
            inc = (("d", k % NDSEM), 16)
            if is_out:
                self.out_tokens.append(tok)
        else:
            self.cnt[eng] += 1
            tok = ("c", eng, self.cnt[eng])
            inc = (("c", eng), 1)
        self.streams[eng].append((waits, fn, inc))
        for r in reads:
            self.readers.setdefault(r, []).append(tok)
        for w in writes:
            self.lastw[w] = tok
            self.readers[w] = []

    def barrier(self):
        for e in ENGS:
            waits = []
            for e2 in ENGS:
                if e2 != e and self.cnt[e2] > 0:
                    self._need(e, (("c", e2), self.cnt[e2]), waits)
            lo = max(0, self.ndma - NDSEM)
            for k in range(lo, self.ndma):
                self._need(e, self.semof(("d", k)), waits)
            if waits:
                self.streams[e].append((waits, None, None))
        self.lastw = {}
        self.readers = {}

    def finish(self):
        waits = []
        for t in self.out_tokens:
            self._need("sp", self.semof(t), waits)
        lo = max(0, self.ndma - NDSEM)
        for k in range(lo, self.ndma):
            self._need("sp", self.semof(("d", k)), waits)
        self.streams["sp"].append((waits, None, None))

    def _sem(self, key):
        return self.sems[key[1]] if key[0] == "c" else self.dsems[key[1]]

    def emit(self, eng, e):
        for waits, fn, inc in self.streams[eng]:
            for key, val in waits:
                e.wait_ge(self._sem(key), val)
            if fn is not None:
                ins = fn(e)
                if inc is not None:
                    ins.then_inc(self._sem(inc[0]), inc[1])


class Arena:
    def __init__(self, t16, t32, nbytes):
        self.t16, self.t32, self.nb = t16, t32, nbytes
        self.off = 0
        self.n16 = nbytes // 2
        self.n32 = nbytes // 4

    def alloc(self, nbytes):
        o = self.off
        self.off = (o + nbytes + 63) // 64 * 64
        assert self.off <= self.nb, ("arena overflow", self.off)
        return o

    def f32(self, cols):
        return self.alloc(cols * 4) // 4

    def b16(self, cols):
        return self.alloc(cols * 2) // 2

    def a32(self, off, dims, p0=0, np_=128):
        return bass.AP(self.t32, p0 * self.n32 + off, [[self.n32, np_]] + [list(d) for d in dims])

    def a16(self, off, dims, p0=0, np_=128):
        return bass.AP(self.t16, p0 * self.n16 + off, [[self.n16, np_]] + [list(d) for d in dims])


def build_nc():
    nc = bass.Bass("TRN2", target_bir_lowering=False)
    dt = nc.dram_tensor

    def din(name, shape, dtype=F32):
        return dt(name, list(shape), dtype, kind="ExternalInput").ap()

    xT = din("xT", [D, S])
    validT = din("validT", [8, S])
    kmask = din("kmask", [128, NT])
    flag = din("flag", [128, 1])
    cT = din("cT", [128, 8])
    w_ada = din("w_ada", [D, 6144])
    b_adaT = din("b_adaT", [128, 48])
    g1T = din("g1T", [128, 8])
    g2T = din("g2T", [128, 8])
    w_in = din("w_in", [D, 5128])
    bfT = din("bfT", [8, 1])
    convA = din("convA", [128, 12])
    gqT = din("gqT", [128, 1])
    gkT = din("gkT", [128, 1])
    w_a = din("w_a", [512, D])
    w_b = din("w_b", [512, D])
    w_o = din("w_o", [D, D])
    w_up = din("w_up", [D, 2 * DFF])
    convF = din("convF", [128, 132])
    w_dn = din("w_dn", [DFF, D])
    cst = din("cst", [128, 512])
    maskb_d = din("maskb", [128, 2048 + 128], BF16)
    outT = dt("outT", [D, 2048], F32, kind="ExternalOutput").ap()
    kT_d = dt("kT_d", [512, S], BF16).ap()
    v_w = dt("v_d", [8 * 128, NT * 65], BF16)
    v_d = v_w.ap()
    qT_d = dt("qT_d", [512, NOWN], BF16).ap()
    fq_w = dt("fq_d", [8, 3 * NOWN], BF16)
    fq_d = fq_w.ap()
    x1_d = dt("x1_d", [D, NOWN], F32).ap()

    NB = 210944
    import contextlib
    with contextlib.ExitStack() as es:
        arena_t = es.enter_context(nc.sbuf_tensor("arena", [128, NB // 2], BF16))
        arena32 = arena_t.bitcast(F32)
        pp = [es.enter_context(nc.psum_tensor("pp%d" % i, [128, 1024], F32)) for i in range(4)]
        sems = {e: es.enter_context(nc.semaphore("s_" + e)) for e in ENGS}
        dsems = [es.enter_context(nc.semaphore("d%d" % i)) for i in range(NDSEM)]
        es.enter_context(nc.allow_low_precision("bf16 matmul operands, fp32 accumulate"))
        A = Arena(arena_t, arena32, NB)
        sc = Sched(nc, sems, dsems)

        def P(i, n=512, p0=0, np_=128, c0=0):
            b0 = (i % 2) * 512 + c0
            return pp[i // 2][p0:p0 + np_, b0:b0 + n]

        def PAP(i, off, dims):
            return bass.AP(pp[i // 2], (i % 2) * 512 + off, [[1024, 128]] + [list(d) for d in dims])

        def mm(out, lhsT, rhs, start, stop, reads, writes):
            sc.add("pe", lambda e: e.matmul(out, lhsT, rhs, start=start, stop=stop), reads, writes)

        def act(out, in_, func, reads, writes, bias=None, scale=None):
            kw = {}
            if bias is not None:
                kw["bias"] = bias
            if scale is not None:
                kw["scale"] = scale
            sc.add("act", lambda e: e.activation(out, in_, func, **kw), reads, writes)

        def tt(out, in0, in1, op, reads, writes, eng="dve"):
            sc.add(eng, lambda e: e.tensor_tensor(out, in0, in1, op), reads, writes)

        def ts(out, in0, s1, op0, reads, writes, s2=None, op1=None, eng="dve"):
            if op1 is None:
                sc.add(eng, lambda e: e.tensor_scalar(out, in0, s1, None, op0), reads, writes)
            else:
                sc.add(eng, lambda e: e.tensor_scalar(out, in0, s1, s2, op0, op1), reads, writes)

        def stt(out, in0, scalar, in1, op0, op1, reads, writes):
            sc.add("dve", lambda e: e.scalar_tensor_tensor(out, in0, scalar, in1, op0, op1), reads, writes)

        def recip(out, in_, reads, writes):
            sc.add("dve", lambda e: e.reciprocal(out, in_), reads, writes)

        def cp(out, in_, reads, writes, eng="dve"):
            sc.add(eng, lambda e: e.tensor_copy(out, in_), reads, writes)

        def memset(ap, val, writes, eng="pool"):
            sc.add(eng, lambda e: e.memset(ap, val), (), writes)

        def dma(out, in_, reads, writes, q="sp", is_out=False):
            sc.add(q, lambda e: e.dma_start(out=out, in_=in_), reads, writes, dma=True, is_out=is_out)

        stg = {"offs": None, "i": 0}

        def load_w(dst_off, src_ap_fn, ncols, wkey, np_=128, engs=("act", "pool")):
            for c0 in range(0, ncols, 1024):
                n = min(1024, ncols - c0)
                i = stg["i"] % len(stg["offs"])
                stg["i"] += 1
                so = stg["offs"][i]
                dma(A.a32(so, [[1, n]], 0, np_), src_ap_fn(c0, n), (), ["stg%d" % i])
                eng = engs[stg["i"] % len(engs)]
                if eng in ("pool", "dve"):
                    cp(A.a16(dst_off + c0, [[1, n]], 0, np_), A.a32(so, [[1, n]], 0, np_), ["stg%d" % i], [wkey], eng=eng)
                else:
                    act(A.a16(dst_off + c0, [[1, n]], 0, np_), A.a32(so, [[1, n]], 0, np_), AF.Copy, ["stg%d" % i], [wkey])

        o_cst = A.f32(512)
        o_maskb = A.b16(2048 + 128)
        o_small = A.f32(512)
        o_biasK = A.f32(512)
        o_ucar = A.f32(8)
        o_b1b = A.b16(8)
        o_b2b = A.b16(8)
        small_mark = A.off
        o_yaT = A.b16(4 * NOWN)
        mark_ya = A.off
        o_zT = A.f32(S)
        persist_mark = A.off

        tri = A.a32(o_cst, [[1, 128]])
        ones = A.a32(o_cst + 128, [[1, 128]])
        ident = A.a32(o_cst + 256, [[1, 128]])
        blockones = A.a32(o_cst + 384, [[1, 128]])
        identb = A.a16(o_maskb + 2048, [[1, 128]])

        def small(c0, n=1, p0=0, np_=128):
            return A.a32(o_small + c0, [[1, n]], p0, np_)
        C_MOD = 0
        C_A1 = 48
        C_A2 = 56
        C_CB = 64
        C_CBV = 88
        C_CF = 96
        C_QS = 97
        C_G1 = 98
        C_G2 = 106
        C_BADA = 114
        C_CT = 162
        C_CVA = 170
        C_CVF = 182
        C_FLAG = 314
        C_KMASK = 315
        C_CBG = 379
        C_CB2 = 395
        C_CBW = 439
        C_BF = 483
        C_GQ = 484
        C_GK = 485
        C_TMP = 486
        C_FREF = 494
        C_FRB = 495

        dma(A.a32(o_cst, [[1, 512]]), cst[:, :], (), ["cst"])
        dma(A.a16(o_maskb, [[1, 2176]]), maskb_d[:, :], (), ["maskb"])
        dma(small(C_BADA, 48), b_adaT[:, :], (), ["bada"])
        dma(small(C_CT, 8), cT[:, :], (), ["ct"])
        dma(small(C_G1, 8), g1T[:, :], (), ["g1"])
        dma(small(C_G2, 8), g2T[:, :], (), ["g2"])
        dma(small(C_CVA, 12), convA[:, :], (), ["cva"])
        dma(small(C_CVF, 132), convF[:, :], (), ["cvf"])
        dma(small(C_FLAG, 1), flag[:, :], (), ["flag"])
        dma(small(C_KMASK, 64), kmask[:, :], (), ["kmask"])
        dma(small(C_BF, 1, 0, 8), bfT[:, :], (), ["bf"])
        dma(small(C_GQ, 1), gqT[:, :], (), ["gq"])
        dma(small(C_GK, 1), gkT[:, :], (), ["gk"])

        o_w1 = A.b16(KC * W1C)
        o_scb = A.b16(8)
        o_wp = [A.b16(8 * 512) for _ in range(2)]
        stg["offs"] = [A.f32(1024) for _ in range(3)]
        scb = A.a16(o_scb, [[1, 8]])
        act(scb, small(C_CT, 8), AF.Silu, ["ct"], ["scb"])
        for pj in range(12):
            b = pj % 2
            for k in range(KC):
                load_w(o_wp[b] + k * 512,
                       lambda c0, n, k=k, pj=pj: w_ada[k * 128:(k + 1) * 128, pj * 512 + c0:pj * 512 + c0 + n],
                       512, "wp%d_%d" % (b, k), engs=("dve", "act", "pool"))
            for jj in range(4):
                j = pj * 4 + jj
                for k in range(KC):
                    mm(P(7, 1, c0=j), A.a16(o_wp[b] + k * 512 + jj * 128, [[1, 128]]),
                       A.a16(o_scb + k, [[1, 1]]), k == 0, k == KC - 1,
                       ["wp%d_%d" % (b, k), "scb"], ["ps7"])
        tt(small(C_MOD, 48), P(7, 48), small(C_BADA, 48), ALU.add, ["ps7", "bada"], ["mod"])
        stt(small(C_A1, 8), small(C_MOD + 8, 8), 1.0, small(C_G1, 8), ALU.add, ALU.mult, ["mod", "g1"], ["A1t"])
        ts(small(C_A1, 8), small(C_A1, 8), 32.0, ALU.mult, ["A1t"], ["A1"])
        stt(small(C_A2, 8), small(C_MOD + 32, 8), 1.0, small(C_G2, 8), ALU.add, ALU.mult, ["mod", "g2"], ["A2t"])
        ts(small(C_A2, 8), small(C_A2, 8), 32.0, ALU.mult, ["A2t"], ["A2"])
        cp(A.a16(o_b1b, [[1, 8]]), small(C_MOD + 0, 8), ["mod"], ["b1b"])
        cp(A.a16(o_b2b, [[1, 8]]), small(C_MOD + 24, 8), ["mod"], ["b2b"])
        stt(small(C_QS), small(C_GQ), 8.0, small(C_GK), ALU.mult, ALU.mult, ["gq", "gk"], ["qs"])
        for k in range(KC):
            load_w(o_w1 + k * W1C, lambda c0, n, k=k: w_in[k * 128:(k + 1) * 128, c0:c0 + n], W1C, "w1_%d" % k,
                   engs=("dve", "act", "pool"))
        sc.barrier()
        A.off = persist_mark

        o_w1 = A.b16(KC * W1C)
        o_xg = [A.f32(KC * 512) for _ in range(3)]
        o_sq = [A.f32(512) for _ in range(2)]
        o_acc = A.f32(512)
        o_hn = [A.b16(KC * 512) for _ in range(2)]
        o_inv = A.f32(512)
        o_rt = A.f32(512)
        o_rt2 = A.f32(512)
        o_kb = [A.f32(512) for _ in range(2)]
        o_ksq = [A.f32(512) for _ in range(2)]
        o_invk = A.f32(512)
        o_kst = [A.b16(512) for _ in range(2)]
        o_vst = A.b16(8 * 4 * 65)
        o_ccs = A.f32(512)
        o_u = A.f32(516)
        o_y = A.f32(512)

        def w1(k, c0, n):
            return A.a16(o_w1 + k * W1C + c0, [[1, n]])

        W1K = ["w1_%d" % k for k in range(KC)]
        b1b = A.a16(o_b1b, [[1, 8]])
        for j in range(24):
            if 20 <= j < 24:
                continue
            for k in range(KC):
                mm(P(7, 1, c0=j), w1(k, j * 128, 128), A.a16(o_b1b + k, [[1, 1]]), k == 0, k == KC - 1,
                   W1K + ["b1b"], ["ps7"])
        for h in range(8):
            for k in range(KC):
                mm(P(7, 1, 0, 64, c0=24 + h), w1(k, 2560 + h * 64, 64), A.a16(o_b1b + k, [[1, 1]]),
                   k == 0, k == KC - 1, W1K + ["b1b"], ["ps7"])
        for k in range(KC):
            mm(P(7, 1, 0, 8, c0=32), w1(k, 3072, 8), A.a16(o_b1b + k, [[1, 1]]), k == 0, k == KC - 1,
               W1K + ["b1b"], ["ps7"])
        cp(small(C_CB, 20), P(7, 20), ["ps7"], ["cb"])
        cp(small(C_CBV, 8, 0, 64), P(7, 8, 0, 64, c0=24), ["ps7"], ["cbv"])
        tt(small(C_CF, 1, 0, 8), P(7, 1, 0, 8, c0=32), small(C_BF, 1, 0, 8), ALU.add, ["ps7", "bf"], ["cf"])
        memset(A.a32(o_u, [[1, 2]]), 0.0, ["u"])
        memset(A.a16(o_vst, [[1, 8 * 4 * 65]]), 1.0, ["vst"])
        memset(small(C_TMP), EPS * D, ["tmpc"], eng="dve")
        memset(small(C_TMP + 1), EPS * 64, ["tmpc"], eng="dve")
        memset(small(C_TMP + 2), 1.0, ["tmpc"], eng="dve")

        def load_group(src_fn, n, xb):
            for k in range(KC):
                dma(A.a32(o_xg[xb] + k * 512, [[1, n]]), src_fn(k), (), ["xg%d_%d" % (xb, k)])

        def norm_group(src_fn, n, Acol, gi, tag, nbuf=2, xb=None, hb=None, do_load=True):
            b = gi % nbuf if xb is None else xb
            hb = b if hb is None else hb
            if do_load:
                load_group(src_fn, n, b)
            for k in range(KC):
                if k == 0:
                    act(A.a32(o_acc, [[1, n]]), A.a32(o_xg[b] + k * 512, [[1, n]]), AF.Square,
                        ["xg%d_%d" % (b, k)], ["acc"])
                else:
                    sb = k % 2
                    act(A.a32(o_sq[sb], [[1, n]]), A.a32(o_xg[b] + k * 512, [[1, n]]), AF.Square,
                        ["xg%d_%d" % (b, k)], ["sq%d" % sb])
                    tt(A.a32(o_acc, [[1, n]]), A.a32(o_acc, [[1, n]]), A.a32(o_sq[sb], [[1, n]]), ALU.add,
                       ["acc", "sq%d" % sb], ["acc"], eng="dve")
            mm(P(2, n), ones, A.a32(o_acc, [[1, n]]), True, True, ["acc", "cst"], ["ps2"])
            act(A.a32(o_rt, [[1, n]]), P(2, n), AF.Ln, ["ps2", "tmpc"], ["rt"], bias=small(C_TMP), scale=1.0)
            act(A.a32(o_inv, [[1, n]]), A.a32(o_rt, [[1, n]]), AF.Exp, ["rt"], ["inv"], scale=-0.5)
            for k in range(KC):
                stt(A.a16(o_hn[hb] + k * 512, [[1, n]]), A.a32(o_xg[b] + k * 512, [[1, n]]),
                    small(Acol + k), A.a32(o_inv, [[1, n]]), ALU.mult, ALU.mult,
                    ["xg%d_%d" % (b, k), "inv", "A1", "A2"], ["hn%d" % hb])
            return b

        hstate = {"i": 0, "pending": None}

        def headnorm_p1(psi, n, cbcol):
            i = hstate["i"] % 2
            hstate["i"] += 1
            act(A.a32(o_kb[i], [[1, n]]), P(psi, n), AF.Identity, ["ps%d" % psi, "cb"], ["kb%d" % i], bias=small(cbcol), scale=1.0)
            act(A.a32(o_ksq[i], [[1, n]]), P(psi, n), AF.Square, ["ps%d" % psi, "cb"], ["ksq%d" % i], bias=small(cbcol), scale=1.0)
            return i

        def headnorm_p2(i, n, scale_ap, kbuf, dst_fn):
            mm(P(3, n), blockones, A.a32(o_ksq[i], [[1, n]]), True, True, ["ksq%d" % i, "cst"], ["ps3"])
            act(A.a32(o_rt2, [[1, n]]), P(3, n), AF.Ln, ["ps3", "tmpc"], ["rt2"], bias=small(C_TMP + 1), scale=1.0)
            act(A.a32(o_invk, [[1, n]]), A.a32(o_rt2, [[1, n]]), AF.Exp, ["rt2"], ["invk"], scale=-0.5)
            out16 = A.a16(o_kst[kbuf], [[1, n]])
            if scale_ap is None:
                tt(out16, A.a32(o_kb[i], [[1, n]]), A.a32(o_invk, [[1, n]]), ALU.mult, ["kb%d" % i, "invk"], ["kst%d" % kbuf])
            else:
                stt(out16, A.a32(o_kb[i], [[1, n]]), scale_ap, A.a32(o_invk, [[1, n]]), ALU.mult, ALU.mult,
                    ["kb%d" % i, "invk", "qs"], ["kst%d" % kbuf])
            dst_fn(out16, kbuf)

        def flush_pending():
            if hstate["pending"] is not None:
                hstate["pending"]()
                hstate["pending"] = None

        groups = ([(c, n, False, 0) for (c, n) in PRE_GROUPS] +
                  [(HALO0 + c, n, True, c) for (c, n) in OWN_GROUPS])
        psr = 0
        PA_BANKS = [0, 1, 4, 5, 6]
        kcount = 0

        def gsrc(gi):
            col0, n, own, oc0 = groups[gi]
            return (lambda k: xT[k * 128:(k + 1) * 128, col0:col0 + n]), n

        def gload(gi):
            f_, n_ = gsrc(gi)
            load_group(f_, n_, gi % 3)

        def stage1(gi):
            f_, n_ = gsrc(gi)
            norm_group(f_, n_, C_A1, gi, "a", xb=gi % 3, hb=gi % 2, do_load=False)

        gload(0)
        gload(1)
        stage1(0)
        for gi, (col0, n, own, oc0) in enumerate(groups):
            if gi + 2 < len(groups):
                gload(gi + 2)
            if gi + 1 < len(groups):
                stage1(gi + 1)
            b = gi % 2
            HN = ["hn%d" % b]

            def hn(k, c0=0, nn=None):
                return A.a16(o_hn[b] + k * 512 + c0, [[1, nn if nn is not None else n]])

            def proj(psi, wc0, wn, np_=128):
                for k in range(KC):
                    mm(P(psi, n, 0, np_), w1(k, wc0, wn), hn(k), k == 0, k == KC - 1, W1K + HN, ["ps%d" % psi])
            for c in range(4):
                psi = PA_BANKS[psr % 5]
                psr += 1
                proj(psi, 2048 + c * 128, 128)
                i = headnorm_p1(psi, n, C_CB + 16 + c)
                flush_pending()
                kbuf = kcount % 2
                kcount += 1

                def fin(i=i, n=n, kbuf=kbuf, c=c, col0=col0):
                    headnorm_p2(i, n, None, kbuf,
                                lambda o16, kb_: dma(kT_d[c * 128:(c + 1) * 128, col0:col0 + n], o16,
                                                     ["kst%d" % kb_], ["kT_d"]))
                hstate["pending"] = fin
            nt = n // 128
            for t in range(nt):
                psi = PA_BANKS[psr % 5]
                psr += 1
                for k in range(KC):
                    mm(P(psi, 512), hn(k, t * 128, 128), w1(k, 2560, 512), k == 0, k == KC - 1, W1K + HN, ["ps%d" % psi])
                if t == 0:
                    flush_pending()
                sc.add("act", lambda e, psi=psi, t=t: e.activation(
                    A.a16(o_vst + t * 65, [[260, 8], [1, 64]]),
                    PAP(psi, 0, [[64, 8], [1, 64]]), AF.Copy), ["ps%d" % psi], ["vst"])
            dma(bass.AP(v_w, (col0 // 128) * 65, [[NT * 65, 128], [128 * NT * 65, 8], [1, nt * 65]]),
                A.a16(o_vst, [[260, 8], [1, nt * 65]]), ["vst"], ["v_d"])
            psi = PA_BANKS[psr % 5]
            psr += 1
            proj(psi, 3072, 8, np_=8)
            act(A.a32(o_zT + col0, [[1, n]], 0, 8), P(psi, n, 0, 8), AF.Identity, ["ps%d" % psi, "cf"], ["zT"],
                bias=small(C_CF, 1, 0, 8), scale=1.0)
            if own:
                for c in range(4):
                    psi = PA_BANKS[psr % 5]
                    psr += 1
                    proj(psi, 1536 + c * 128, 128)
                    i = headnorm_p1(psi, n, C_CB + 12 + c)
                    flush_pending()
                    kbuf = kcount % 2
                    kcount += 1

                    def fin(i=i, n=n, kbuf=kbuf, c=c, oc0=oc0):
                        headnorm_p2(i, n, small(C_QS), kbuf,
                                    lambda o16, kb_: dma(qT_d[c * 128:(c + 1) * 128, oc0:oc0 + n], o16,
                                                         ["kst%d" % kb_], ["qT_d"]))
                    hstate["pending"] = fin
                for c in range(4):
                    p_cc = PA_BANKS[psr % 5]
                    psr += 1
                    proj(p_cc, 512 + c * 128, 128)
                    if c == 0:
                        flush_pending()
                    act(A.a32(o_ccs, [[1, n]]), P(p_cc, n), AF.Identity, ["ps%d" % p_cc, "cb"], ["ccs"],
                        bias=small(C_CB + 4 + c), scale=1.0)
                    p_cv = PA_BANKS[psr % 5]
                    psr += 1
                    proj(p_cv, 1024 + c * 128, 128)
                    stt(A.a32(o_u + 2, [[1, n]]), P(p_cv, n), small(C_CB + 8 + c), A.a32(o_ccs, [[1, n]]),
                        ALU.add, ALU.mult, ["ps%d" % p_cv, "ccs", "cb", "ucar%d" % c], ["u"])
                    if oc0 == 0:
                        memset(A.a32(o_u, [[1, 2]]), 0.0, ["u"], eng="dve")
                        ts(A.a32(o_u + 2, [[1, n]]), A.a32(o_u + 2, [[1, n]]), small(C_FLAG), ALU.mult, ["u", "flag"], ["u"])
                    else:
                        cp(A.a32(o_u, [[1, 2]]), A.a32(o_ucar + c * 2, [[1, 2]]), ["ucar%d" % c], ["u"])
                    ts(A.a32(o_y, [[1, n]]), A.a32(o_u + 2, [[1, n]]), small(C_CVA + c * 3 + 2), ALU.mult, ["u", "cva"], ["y"])
                    stt(A.a32(o_y, [[1, n]]), A.a32(o_u + 1, [[1, n]]), small(C_CVA + c * 3 + 1), A.a32(o_y, [[1, n]]),
                        ALU.mult, ALU.add, ["u", "y", "cva"], ["y"])
                    stt(A.a32(o_y, [[1, n]]), A.a32(o_u, [[1, n]]), small(C_CVA + c * 3 + 0), A.a32(o_y, [[1, n]]),
                        ALU.mult, ALU.add, ["u", "y", "cva"], ["y"])
                    cp(A.a32(o_ucar + c * 2, [[1, 2]]), A.a32(o_u + n, [[1, 2]]), ["u"], ["ucar%d" % c])
                    p_cb = PA_BANKS[psr % 5]
                    psr += 1
                    proj(p_cb, c * 128, 128)
                    stt(A.a16(o_yaT + c * NOWN + oc0, [[1, n]]), P(p_cb, n), small(C_CB + c), A.a32(o_y, [[1, n]]),
                        ALU.add, ALU.mult, ["ps%d" % p_cb, "y", "cb"], ["yaT"])
        flush_pending()
        sc.barrier()
        A.off = persist_mark

        o_e = A.f32(2048)
        o_val = A.f32(2048)
        o_one8 = A.f32(2048)
        o_fqf = A.f32(NOWN)
        o_fr = A.f32(NOWN)
        o_fq16 = A.b16(3 * NOWN)
        o_diag = A.f32(8)

        def r8(off, n, c0=0):
            return A.a32(off + c0, [[1, n]], 0, 8)

        memset(r8(o_one8, 2048), 1.0, ["one8"], eng="dve")
        for pc in range(4):
            c0 = pc * 2048
            dma(r8(o_val, 2048), validT[:, c0:c0 + 2048], (), ["val"])
            act(r8(o_e, 2048), r8(o_zT, 2048, c0), AF.Exp, ["zT"], ["e"], scale=-1.0)
            act(r8(o_e, 2048), r8(o_e, 2048), AF.Ln, ["e", "tmpc"], ["e"], bias=small(C_TMP + 2, 1, 0, 8), scale=1.0)
            stt(r8(o_e, 2048), r8(o_e, 2048), -1.0, r8(o_val, 2048), ALU.mult, ALU.mult, ["e", "val"], ["e"])
            init = 0.0 if pc == 0 else r8(o_zT, 1, c0 - 1)
            sc.add("dve", lambda e, c0=c0, init=init: e.tensor_tensor_scan(
                r8(o_zT, 2048, c0), r8(o_one8, 2048), r8(o_e, 2048), init, ALU.mult, ALU.add),
                ["e", "one8", "zT"], ["zT"])
        cp(small(C_FREF, 1, 0, 8), r8(o_zT, 1, S - 1), ["zT"], ["fref"])
        ts(r8(o_fqf, NOWN), r8(o_zT, NOWN, HALO0), small(C_FREF, 1, 0, 8), ALU.subtract, ["zT", "fref"], ["fqf"])
        cp(A.a16(o_fq16, [[1, NOWN]], 0, 8), r8(o_fqf, NOWN), ["fqf"], ["fq16a"])
        tt(r8(o_fr, NOWN), r8(o_fqf, NOWN), A.a16(o_fq16, [[1, NOWN]], 0, 8), ALU.subtract, ["fqf", "fq16a"], ["fr"])
        cp(A.a16(o_fq16 + NOWN, [[1, NOWN]], 0, 8), r8(o_fr, NOWN), ["fr"], ["fq16b"])
        tt(r8(o_fqf, NOWN), r8(o_fr, NOWN), A.a16(o_fq16 + NOWN, [[1, NOWN]], 0, 8), ALU.subtract, ["fr", "fq16b"], ["fqf"])
        cp(A.a16(o_fq16 + 2 * NOWN, [[1, NOWN]], 0, 8), r8(o_fqf, NOWN), ["fqf"], ["fq16c"])
        dma(fq_d[:, :], A.a16(o_fq16, [[1, 3 * NOWN]], 0, 8), ["fq16a", "fq16b", "fq16c"], ["fq_d"])
        for t in range(NT):
            mm(P(6, 8, c0=t * 8), r8(o_zT, 128, t * 128), A.a32(o_cst + 256, [[1, 8]], 0, 8), True, True,
               ["zT", "cst"], ["ps6"])
        ts(A.a32(o_diag, [[1, 8]], 0, 8), A.a32(o_cst + 256, [[1, 8]], 0, 8), small(C_FREF, 1, 0, 8), ALU.mult,
           ["cst", "fref"], ["diag"])
        mm(P(7, 8), A.a32(o_cst + 128, [[1, 128]], 0, 8), A.a32(o_diag, [[1, 8]], 0, 8), True, True, ["diag", "cst"], ["ps7"])
        cp(small(C_FRB, 8), P(7, 8), ["ps7"], ["frb"])
        tt(A.a32(o_biasK, [[8, NT], [1, 8]]), A.a32(o_small + C_FRB, [[0, NT], [1, 8]]),
           PAP(6, 0, [[8, NT], [1, 8]]), ALU.subtract, ["frb", "ps6"], ["biasK"])
        tt(A.a32(o_biasK, [[8, NT], [1, 8]]), A.a32(o_biasK, [[8, NT], [1, 8]]),
           A.a32(o_small + C_KMASK, [[1, NT], [0, 8]]), ALU.add, ["biasK", "kmask"], ["biasK"])
        sc.barrier()
        A.off = mark_ya
        o_ybT = A.b16(8 * NOWN)
        persist_mark = A.off

        o_wg = A.b16(KC * 2048)
        o_wo = A.b16(KC * 1024)
        mark_c = A.off
        stg["offs"] = [A.f32(1024) for _ in range(2)]
        o_k = [A.b16(S) for _ in range(2)]
        o_v = [A.b16(NT * 65) for _ in range(2)]
        o_q = [A.b16(NOWN) for _ in range(2)]
        NPT = 3
        o_pT = [A.b16(1024) for _ in range(NPT)]
        o_rec = A.f32(512)
        o_osb = A.f32(512)
        o_tmp = A.f32(512)
        for b in range(2):
            memset(A.a16(o_k[b], [[1, S]], 64, 3), 1.0, ["k%d" % b], eng="pool")

        def load_head(h):
            b = h % 2
            KB, VB, QB = "k%d" % b, "v%d" % b, "q%d" % b
            for half in range(2):
                dma(A.a16(o_k[b] + half * 4096, [[1, 4096]], 0, 64),
                    kT_d[h * 64:(h + 1) * 64, half * 4096:(half + 1) * 4096], ["kT_d"], [KB])
            dma(A.a16(o_v[b], [[1, NT * 65]]), v_d[h * 128:(h + 1) * 128, :], ["v_d"], [VB])
            dma(A.a16(o_q[b], [[1, NOWN]], 0, 64), qT_d[h * 64:(h + 1) * 64, :], ["qT_d"], [QB])
            dma(A.a16(o_q[b], [[1, NOWN]], 64, 3),
                bass.AP(fq_w, h * 3 * NOWN, [[NOWN, 3], [1, NOWN]]), ["fq_d"], [QB])

        PAIRS = [[(0, 128, 384), (128, 512, 512)], [(640, 512, 0), (1152, 512, 512)], [(1664, 512, 0)]]
        units = []
        blk = 0
        for h in range(8):
            for pair in PAIRS:
                blks = []
                for (q0, nq, slot) in pair:
                    blks.append((q0, nq, slot, 47 + q0 // 128, 47 + (q0 + nq) // 128, blk))
                    blk += 1
                nv_max = max(bb[4] for bb in blks)
                for kt in range(nv_max):
                    vis = []
                    for (q0, nq, slot, qt0, nvis, bk) in blks:
                        if kt < nvis:
                            vis.append((q0, nq, slot, kt == 0, kt == nvis - 1, kt >= qt0, kt - qt0, bk))
                    units.append((h, kt, vis, kt == 0 and pair is PAIRS[0]))

        def emit_S(idx):
            h, kt, vis, _ = units[idx]
            b = h % 2
            KB, QB = "k%d" % b, "q%d" % b
            sp_ = idx % 2
            pb = idx % NPT
            SK = "pps%d" % sp_
            for (q0, nq, slot, first, last, partial, jm, bk) in vis:
                out = pp[sp_][:, slot:slot + nq]
                mm(out, A.a16(o_k[b] + kt * 128, [[1, 128]], 0, 67), A.a16(o_q[b] + q0, [[1, nq]], 0, 67),
                   True, not partial, [KB, QB], [SK])
                if partial:
                    mm(out, identb, A.a16(o_maskb + jm * 512, [[1, nq]]), False, True, ["maskb"], [SK])
            c_lo = min(v[2] for v in vis)
            c_hi = max(v[2] + v[1] for v in vis)
            act(A.a16(o_pT[pb] + c_lo, [[1, c_hi - c_lo]]), pp[sp_][:, c_lo:c_hi], AF.Exp, [SK, "biasK"], ["pT%d" % pb],
                bias=A.a32(o_biasK + kt * 8 + h, [[1, 1]]), scale=1.0)

        def emit_PV(idx):
            h, kt, vis, _ = units[idx]
            b = h % 2
            VB = "v%d" % b
            pb = idx % NPT
            for (q0, nq, slot, first, last, partial, jm, bk) in vis:
                ob = 4 + bk % 3
                OK_ = "ps%d" % ob
                mm(P(ob, nq, 0, 65), A.a16(o_v[b] + kt * 65, [[1, 65]]), A.a16(o_pT[pb] + slot, [[1, nq]]),
                   first, last, [VB, "pT%d" % pb], [OK_])
                if last:
                    ts(A.a32(o_rec, [[1, nq]], 64, 1), P(ob, nq, 64, 1), 1e-30, ALU.add, [OK_], ["rec"])
                    recip(A.a32(o_rec, [[1, nq]], 64, 1), A.a32(o_rec, [[1, nq]], 64, 1), ["rec"], ["rec"])
                    mm(P(7, nq, 0, 64), A.a32(o_cst + 128, [[1, 64]], 64, 1), A.a32(o_rec, [[1, nq]], 64, 1), True, True,
                       ["rec", "cst"], ["ps7"])
                    cp(A.a32(o_osb, [[1, nq]], 0, 64), P(ob, nq, 0, 64), [OK_], ["osb"])
                    tt(A.a32(o_tmp, [[1, nq]], 0, 64), A.a32(o_osb, [[1, nq]], 0, 64), P(7, nq, 0, 64), ALU.mult,
                       ["osb", "ps7"], ["tmp"])
                    ts(A.a16(o_ybT + h * NOWN + q0, [[1, nq]], 0, 64), A.a32(o_tmp, [[1, nq]], 0, 64),
                       small(C_CBV + h, 1, 0, 64), ALU.add, ["tmp", "cbv"], ["ybT"])

        LA = 1
        load_head(0)
        load_head(1)
        for k in range(KC):
            load_w(o_wg + k * 2048, lambda c0, n, k=k: w_in[k * 128:(k + 1) * 128, 3080 + c0:3080 + c0 + n], 2048, "wg",
                   engs=("pool",))
            load_w(o_wo + k * 1024, lambda c0, n, k=k: w_o[k * 128:(k + 1) * 128, c0:c0 + n], 1024, "wo", engs=("pool",))
        for idx in range(len(units) + LA):
            if idx < len(units):
                u = units[idx]
                if u[1] == LA + 2 and u[3] is False and False:
                    pass
                emit_S(idx)
            if idx - LA >= 0:
                emit_PV(idx - LA)
                u2 = units[idx - LA]
                if idx - LA + 1 < len(units) and units[idx - LA + 1][0] != u2[0] and u2[0] + 2 < 8:
                    load_head(u2[0] + 2)
        sc.barrier()
        A.off = mark_c

        o_wa = A.b16(4 * 1024)
        o_wb = A.b16(8 * 1024)
        o_xg = [A.f32(KC * 512)]
        o_sq = [A.f32(512) for _ in range(2)]
        o_acc = A.f32(512)
        o_hn = [A.b16(KC * 512)]
        o_inv = A.f32(512)
        o_rt = A.f32(512)
        o_sga = A.f32(512)
        o_sgb = A.f32(512)
        o_m1 = A.f32(512)
        o_m2 = A.f32(512)
        o_mg = A.b16(KC * 512)
        stg["offs"] = [o_xg[0], o_xg[0] + 1024, o_xg[0] + 2048]
        for c in range(4):
            load_w(o_wa + c * 1024, lambda c0, n, c=c: w_a[c * 128:(c + 1) * 128, c0:c0 + n], 1024, "wa")
        for h in range(8):
            load_w(o_wb + h * 1024, lambda c0, n, h=h: w_b[h * 64:(h + 1) * 64, c0:c0 + n], 1024, "wb", np_=64)
        sc.barrier()
        for j in range(16):
            for k in range(KC):
                mm(P(7, 1, c0=j), A.a16(o_wg + k * 2048 + j * 128, [[1, 128]]), A.a16(o_b1b + k, [[1, 1]]),
                   k == 0, k == KC - 1, ["wg", "b1b"], ["ps7"])
        cp(small(C_CBG, 16), P(7, 16), ["ps7"], ["cbg"])
        for gi2, (oc0, n) in enumerate(OWN_GROUPS):
            col0 = HALO0 + oc0
            norm_group(lambda k: xT[k * 128:(k + 1) * 128, col0:col0 + n], n, C_A1, 0, "c", nbuf=1)
            for e_ in range(KC):
                pa_, pb_ = [0, 6][e_ % 2], [1, 7][e_ % 2]
                for k in range(KC):
                    mm(P(pa_, n), A.a16(o_wg + k * 2048 + e_ * 128, [[1, 128]]), A.a16(o_hn[0] + k * 512, [[1, n]]),
                       k == 0, k == KC - 1, ["wg", "hn0"], ["ps%d" % pa_])
                act(A.a32(o_sga, [[1, n]]), P(pa_, n), AF.Sigmoid, ["ps%d" % pa_, "cbg"], ["sga"], bias=small(C_CBG + e_), scale=1.0)
                for k in range(KC):
                    mm(P(pb_, n), A.a16(o_wg + k * 2048 + 1024 + e_ * 128, [[1, 128]]), A.a16(o_hn[0] + k * 512, [[1, n]]),
                       k == 0, k == KC - 1, ["wg", "hn0"], ["ps%d" % pb_])
                act(A.a32(o_sgb, [[1, n]]), P(pb_, n), AF.Sigmoid, ["ps%d" % pb_, "cbg"], ["sgb"], bias=small(C_CBG + 8 + e_), scale=1.0)
                for c in range(4):
                    mm(P(4, n), A.a16(o_wa + c * 1024 + e_ * 128, [[1, 128]]), A.a16(o_yaT + c * NOWN + oc0, [[1, n]]),
                       c == 0, c == 3, ["wa", "yaT"], ["ps4"])
                tt(A.a32(o_m1, [[1, n]]), A.a32(o_sga, [[1, n]]), P(4, n), ALU.mult, ["sga", "ps4"], ["m1"])
                for h in range(8):
                    mm(P(5, n), A.a16(o_wb + h * 1024 + e_ * 128, [[1, 128]], 0, 64),
                       A.a16(o_ybT + h * NOWN + oc0, [[1, n]], 0, 64), h == 0, h == 7, ["wb", "ybT"], ["ps5"])
                tt(A.a32(o_m2, [[1, n]]), A.a32(o_sgb, [[1, n]]), P(5, n), ALU.mult, ["sgb", "ps5"], ["m2"])
                tt(A.a16(o_mg + e_ * 512, [[1, n]]), A.a32(o_m1, [[1, n]]), A.a32(o_m2, [[1, n]]), ALU.add,
                   ["m1", "m2"], ["mg%d" % e_], eng="pool")
            MG = ["mg%d" % i for i in range(KC)]
            for e_ in range(KC):
                psi = 2 + e_ % 2
                for k in range(KC):
                    mm(P(psi, n), A.a16(o_wo + k * 1024 + e_ * 128, [[1, 128]]), A.a16(o_mg + k * 512, [[1, n]]),
                       k == 0, k == KC - 1, ["wo"] + MG, ["ps%d" % psi])
                stt(A.a32(o_xg[0] + e_ * 512, [[1, n]]), P(psi, n), small(C_MOD + 16 + e_),
                    A.a32(o_xg[0] + e_ * 512, [[1, n]]), ALU.mult, ALU.add,
                    ["ps%d" % psi, "mod", "xg0_%d" % e_], ["xg0_%d" % e_])
                dma(x1_d[e_ * 128:(e_ + 1) * 128, oc0:oc0 + n], A.a32(o_xg[0] + e_ * 512, [[1, n]]),
                    ["xg0_%d" % e_], ["x1_d"])
        sc.barrier()
        A.off = small_mark

        NF = 2 * DFF
        o_wup = A.b16(KC * NF)
        o_wdn = A.b16(22 * 1024)
        o_xg = [A.f32(KC * 512)]
        o_sq = [A.f32(512) for _ in range(2)]
        o_acc = A.f32(512)
        o_hn = [A.b16(KC * 512)]
        o_inv = A.f32(512)
        o_rt = A.f32(512)
        o_yg = A.f32(512)
        o_yv = A.f32(512)
        o_sg = A.f32(512)
        o_act = A.b16(22 * 512)
        o_ws = A.f32(44)
        stg["offs"] = [o_xg[0], o_xg[0] + 1024, o_xg[0] + 2048]
        for k in range(KC):
            load_w(o_wup + k * NF, lambda c0, n, k=k: w_up[k * 128:(k + 1) * 128, c0:c0 + n], NF, "wup", engs=("dve", "act"))
        for c in range(22):
            load_w(o_wdn + c * 1024, lambda c0, n, c=c: w_dn[c * 128:(c + 1) * 128, c0:c0 + n], 1024, "wdn", engs=("dve", "act"))
        sc.barrier()
        for fc in range(44):
            for k in range(KC):
                mm(P(7, 1, c0=fc), A.a16(o_wup + k * NF + fc * 128, [[1, 128]]), A.a16(o_b2b + k, [[1, 1]]),
                   k == 0, k == KC - 1, ["wup", "b2b"], ["ps7"])
        cp(small(C_CB2, 44), P(7, 44), ["ps7"], ["cb2"])
        tt(A.a32(o_ws, [[1, 44]]), A.a32(o_small + C_CVF, [[3, 44]]), A.a32(o_small + C_CVF + 1, [[3, 44]]), ALU.add,
           ["cvf"], ["ws"])
        tt(A.a32(o_ws, [[1, 44]]), A.a32(o_ws, [[1, 44]]), A.a32(o_small + C_CVF + 2, [[3, 44]]), ALU.add, ["ws", "cvf"], ["ws"])
        tt(small(C_CBW, 44), small(C_CB2, 44), A.a32(o_ws, [[1, 44]]), ALU.mult, ["ws", "cb2"], ["cbw"])
        ts(small(C_TMP + 3), small(C_FLAG), -1.0, ALU.add, ["flag"], ["fm1"])
        o_tc = A.f32(2)
        pcount = 0
        for g in range(5):
            w0c = 126 + 510 * g
            n = min(512, NOWN - w0c)
            m = n - 2
            norm_group(lambda k: x1_d[k * 128:(k + 1) * 128, w0c:w0c + n], n, C_A2, 0, "d", nbuf=1)
            for c in range(22):
                for which, fc in ((0, c), (1, 22 + c)):
                    psi = [0, 1, 6, 7][pcount % 4]
                    pcount += 1
                    oy = o_yg if which == 0 else o_yv
                    YK = "yg" if which == 0 else "yv"
                    PK = "ps%d" % psi
                    for k in range(KC):
                        mm(P(psi, n), A.a16(o_wup + k * NF + fc * 128, [[1, 128]]), A.a16(o_hn[0] + k * 512, [[1, n]]),
                           k == 0, k == KC - 1, ["wup", "hn0"], [PK])
                    act(A.a32(oy, [[1, m]]), P(psi, m, c0=2), AF.Identity, [PK, "cvf", "cbw"], [YK],
                        bias=small(C_CBW + fc), scale=small(C_CVF + fc * 3 + 2))
                    stt(A.a32(oy, [[1, m]]), P(psi, m, c0=1), small(C_CVF + fc * 3 + 1), A.a32(oy, [[1, m]]),
                        ALU.mult, ALU.add, [PK, YK, "cvf"], [YK])
                    stt(A.a32(oy, [[1, m]]), P(psi, m, c0=0), small(C_CVF + fc * 3 + 0), A.a32(oy, [[1, m]]),
                        ALU.mult, ALU.add, [PK, YK, "cvf"], [YK])
                    if g == 0:
                        ts(A.a32(o_tc, [[1, 2]]), P(psi, 2), small(C_CB2 + fc), ALU.add, [PK, "cb2", "fm1"], ["tc"],
                           s2=small(C_TMP + 3), op1=ALU.mult)
                        stt(A.a32(oy, [[1, 2]]), A.a32(o_tc, [[1, 2]]), small(C_CVF + fc * 3 + 0), A.a32(oy, [[1, 2]]),
                            ALU.mult, ALU.add, ["tc", YK, "cvf"], [YK])
                        stt(A.a32(oy, [[1, 1]]), A.a32(o_tc + 1, [[1, 1]]), small(C_CVF + fc * 3 + 1), A.a32(oy, [[1, 1]]),
                            ALU.mult, ALU.add, ["tc", YK, "cvf"], [YK])
                act(A.a32(o_sg, [[1, m]]), A.a32(o_yg, [[1, m]]), AF.Silu, ["yg"], ["sg"])
                tt(A.a16(o_act + c * 512, [[1, m]]), A.a32(o_sg, [[1, m]]), A.a32(o_yv, [[1, m]]), ALU.mult,
                   ["sg", "yv"], ["act%d" % c], eng="pool")
            ACTK = ["act%d" % c for c in range(22)]
            for e_ in range(KC):
                psi = 4 + e_ % 2
                for c in range(22):
                    mm(P(psi, m), A.a16(o_wdn + c * 1024 + e_ * 128, [[1, 128]]), A.a16(o_act + c * 512, [[1, m]]),
                       c == 0, c == 21, ["wdn"] + ACTK, ["ps%d" % psi])
                stt(A.a32(o_xg[0] + e_ * 512 + 2, [[1, m]]), P(psi, m), small(C_MOD + 40 + e_),
                    A.a32(o_xg[0] + e_ * 512 + 2, [[1, m]]), ALU.mult, ALU.add,
                    ["ps%d" % psi, "mod", "xg0_%d" % e_], ["xg0_%d" % e_])
                dma(outT[e_ * 128:(e_ + 1) * 128, w0c + 2 - 128:w0c + n - 128], A.a32(o_xg[0] + e_ * 512 + 2, [[1, m]]),
                    ["xg0_%d" % e_], ["outT"], is_out=True)
        sc.finish()

        with nc.Block() as block:
            @block.tensor
            def _(e):
                sc.emit("pe", e)

            @block.scalar
            def _(e):
                sc.emit("act", e)

            @block.vector
            def _(e):
                sc.emit("dve", e)

            @block.gpsimd
            def _(e):
                sc.emit("pool", e)

            @block.sync
            def _(e):
                sc.emit("sp", e)
    return nc


_NC_CACHE = {}


def _consts():
    p = np.arange(128)
    tri = (p[:, None] <= p[None, :]).astype(np.float32)
    ones = np.ones((128, 128), np.float32)
    ident = np.eye(128, dtype=np.float32)
    blk = ((p[:, None] // 64) == (p[None, :] // 64)).astype(np.float32)
    cst = np.concatenate([tri, ones, ident, blk], axis=1)
    c = np.arange(512)
    masks = []
    for jm in range(4):
        masks.append(np.where(c[None, :] >= 128 * jm + p[:, None], 0.0, NEG).astype(np.float32))
    maskb = np.concatenate(masks + [ident], axis=1).astype(ml_dtypes.bfloat16)
    return cst, maskb


def prep_inputs(x, c, w_ada, b_ada, norm1_g, w_in, b_f, conv_a_w, q_norm_g, k_norm_g,
                w_branch_a, w_branch_b, w_out, norm2_g, w_up, conv_ffn_w, w_down):
    f = lambda a: np.ascontiguousarray(np.asarray(a, dtype=np.float32))
    x = f(x); c = f(c)
    cst, maskb = _consts()
    shared = {
        "w_ada": f(w_ada[0]),
        "b_adaT": f(np.asarray(b_ada[0]).reshape(48, 128).T),
        "g1T": f(np.asarray(norm1_g[0]).reshape(8, 128).T),
        "g2T": f(np.asarray(norm2_g[0]).reshape(8, 128).T),
        "w_in": f(w_in[0]),
        "bfT": f(np.asarray(b_f[0])[:, None]),
        "convA": f(np.asarray(conv_a_w[0]).reshape(3, 4, 128).transpose(2, 1, 0).reshape(128, 12)),
        "gqT": f(np.tile(np.asarray(q_norm_g[0]), 2)[:, None]),
        "gkT": f(np.tile(np.asarray(k_norm_g[0]), 2)[:, None]),
        "w_a": f(w_branch_a[0]),
        "w_b": f(w_branch_b[0]),
        "w_o": f(w_out[0]),
        "w_up": f(w_up[0]),
        "convF": f(np.asarray(conv_ffn_w[0]).reshape(3, 44, 128).transpose(2, 1, 0).reshape(128, 132)),
        "w_dn": f(w_down[0]),
        "cst": cst,
        "maskb": maskb,
    }
    in_maps = []
    pidx = np.arange(128)[:, None] + 128 * np.arange(NT)[None, :]
    for j in range(8):
        b, i = j // 4, j % 4
        own_end = 2048 * (i + 1)
        ws = own_end - S
        xTw = np.zeros((D, S), np.float32)
        t0 = max(ws, 0)
        xTw[:, t0 - ws:] = x[b, t0:own_end, :].T
        valid = ((np.arange(S) + ws) >= 0).astype(np.float32)
        m = dict(shared)
        m["xT"] = xTw
        m["validT"] = np.ascontiguousarray(np.broadcast_to(valid[None, :], (8, S)))
        m["kmask"] = np.where((pidx + ws) >= 0, 0.0, NEG).astype(np.float32)
        m["flag"] = np.full((128, 1), 1.0 if i > 0 else 0.0, np.float32)
        m["cT"] = f(c[b].reshape(8, 128).T)
        in_maps.append(m)
    return in_maps


def kernel(**inputs):
    in_maps = prep_inputs(**inputs)
    if "nc" not in _NC_CACHE:
        _NC_CACHE["nc"] = build_nc()
    nc = _NC_CACHE["nc"]
    res = run_bass_kernel_spmd(nc, in_maps, core_ids=list(range(8)))
    out = np.empty((2, S, D), np.float32)
    for j in range(8):
        b, i = j // 4, j % 4
        out[b, 2048 * i:2048 * (i + 1), :] = np.asarray(res.results[j]["outT"]).T
    return out
```

```python
import numpy as np
import ml_dtypes
import concourse.bass as bass
import concourse.mybir as mybir
from concourse.bass_utils import run_bass_kernel_spmd

F32 = mybir.dt.float32
BF16 = mybir.dt.bfloat16
AF = mybir.ActivationFunctionType
ALU = mybir.AluOpType

D = 1024
KC = 8
S = 8192
NT = 64
HALO0 = 47 * 128
NOWN = 2176
W1C = 3080
DFF = 2816
NEG = -30000.0
EPS = 1e-6
OWN_GROUPS = [(0, 128), (128, 512), (640, 512), (1152, 512), (1664, 512)]
PRE_GROUPS = [(i * 512, 512) for i in range(11)] + [(5632, 384)]

ENGS = ["pe", "act", "dve", "pool", "sp"]
NDSEM = 24


class Sched:
    def __init__(self, nc, sems, dsems):
        self.nc = nc
        self.sems = sems
        self.dsems = dsems
        self.streams = {e: [] for e in ENGS}
        self.cnt = {e: 0 for e in ENGS}
        self.lastw = {}
        self.readers = {}
        self.waited = {e: {} for e in ENGS}
        self.ndma = 0
        self.out_tokens = []

    def semof(self, tok):
        if tok[0] == "c":
            return (("c", tok[1]), tok[2])
        k = tok[1]
        return (("d", k % NDSEM), 16 * (k // NDSEM + 1))

    def _need(self, eng, sv, waits):
        key, val = sv
        if self.waited[eng].get(key, 0) >= val:
            return
        self.waited[eng][key] = val
        waits.append(sv)

    def add(self, eng, fn, reads=(), writes=(), dma=False, is_out=False):
        deps = set()
        for r in reads:
            t = self.lastw.get(r)
            if t is not None:
                deps.add(t)
        for w in writes:
            t = self.lastw.get(w)
            if t is not None:
                deps.add(t)
            for t in self.readers.get(w, ()):
                deps.add(t)
        waits = []
        for tok in deps:
            if tok[0] == "c" and tok[1] == eng and not dma:
                if eng == "pe":
                    continue
                if self.cnt[eng] - tok[2] > 3:
                    continue
            self._need(eng, self.semof(tok), waits)
        if dma:
            k = self.ndma
            self.ndma += 1
            if k >= NDSEM:
                self._need(eng, (("d", k % NDSEM), 16 * (k // NDSEM)), waits)
            tok = ("d", k)
            inc = (("d", k % NDSEM), 16)
            if is_out:
                self.out_tokens.append(tok)
        else:
            self.cnt[eng] += 1
            tok = ("c", eng, self.cnt[eng])
            inc = (("c", eng), 1)
        self.streams[eng].append((waits, fn, inc))
        for r in reads:
            self.readers.setdefault(r, []).append(tok)
        for w in writes:
            self.lastw[w] = tok
            self.readers[w] = []

    def barrier(self):
        for e in ENGS:
            waits = []
            for e2 in ENGS:
                if e2 != e and self.cnt[e2] > 0:
                    self._need(e, (("c", e2), self.cnt[e2]), waits)
            lo = max(0, self.ndma - NDSEM)
            for k in range(lo, self.ndma):
                self._need(e, self.semof(("d", k)), waits)
            if waits:
                self.streams[e].append((waits, None, None))
        self.lastw = {}
        self.readers = {}

    def finish(self):
        waits = []
        for t in self.out_tokens:
            self._need("sp", self.semof(t), waits)
        lo = max(0, self.ndma - NDSEM)
        for k in range(lo, self.ndma):
            self._need("sp", self.semof(("d", k)), waits)
        self.streams["sp"].append((waits, None, None))

    def _sem(self, key):
        return self.sems[key[1]] if key[0] == "c" else self.dsems[key[1]]

    def emit(self, eng, e):
        for waits, fn, inc in self.streams[eng]:
            for key, val in waits:
                e.wait_ge(self._sem(key), val)
            if fn is not None:
                ins = fn(e)
                if inc is not None:
                    ins.then_inc(self._sem(inc[0]), inc[1])


class Arena:
    def __init__(self, t16, t32, nbytes):
        self.t16, self.t32, self.nb = t16, t32, nbytes
        self.off = 0
        self.n16 = nbytes // 2
        self.n32 = nbytes // 4

    def alloc(self, nbytes):
        o = self.off
        self.off = (o + nbytes + 63) // 64 * 64
        assert self.off <= self.nb, ("arena overflow", self.off)
        return o

    def f32(self, cols):
        return self.alloc(cols * 4) // 4

    def b16(self, cols):
        return self.alloc(cols * 2) // 2

    def a32(self, off, dims, p0=0, np_=128):
        return bass.AP(self.t32, p0 * self.n32 + off, [[self.n32, np_]] + [list(d) for d in dims])

    def a16(self, off, dims, p0=0, np_=128):
        return bass.AP(self.t16, p0 * self.n16 + off, [[self.n16, np_]] + [list(d) for d in dims])


def build_nc():
    nc = bass.Bass("TRN2", target_bir_lowering=False)
    dt = nc.dram_tensor

    def din(name, shape, dtype=F32):
        return dt(name, list(shape), dtype, kind="ExternalInput").ap()

    xT = din("xT", [D, S])
    validT = din("validT", [8, S])
    kmask = din("kmask", [128, NT])
    flag = din("flag", [128, 1])
    cT = din("cT", [128, 8])
    w_ada = din("w_ada", [D, 6144])
    b_adaT = din("b_adaT", [128, 48])
    g1T = din("g1T", [128, 8])
    g2T = din("g2T", [128, 8])
    w_in = din("w_in", [D, 5128])
    bfT = din("bfT", [8, 1])
    convA = din("convA", [128, 12])
    gqT = din("gqT", [128, 1])
    gkT = din("gkT", [128, 1])
    w_a = din("w_a", [512, D])
    w_b = din("w_b", [512, D])
    w_o = din("w_o", [D, D])
    w_up = din("w_up", [D, 2 * DFF])
    convF = din("convF", [128, 132])
    w_dn = din("w_dn", [DFF, D])
    cst = din("cst", [128, 512])
    maskb_d = din("maskb", [128, 2048 + 128], BF16)
    outT = dt("outT", [D, 2048], F32, kind="ExternalOutput").ap()
    kT_d = dt("kT_d", [512, S], BF16).ap()
    v_w = dt("v_d", [8 * 128, NT * 65], BF16)
    v_d = v_w.ap()
    qT_d = dt("qT_d", [512, NOWN], BF16).ap()
    fq_w = dt("fq_d", [8, 3 * NOWN], BF16)
    fq_d = fq_w.ap()
    x1_d = dt("x1_d", [D, NOWN], F32).ap()

    NB = 210944
    import contextlib
    with contextlib.ExitStack() as es:
        arena_t = es.enter_context(nc.sbuf_tensor("arena", [128, NB // 2], BF16))
        arena32 = arena_t.bitcast(F32)
        ps = [es.enter_context(nc.psum_tensor("ps%d" % i, [128, 512], F32)) for i in range(8)]
        sems = {e: es.enter_context(nc.semaphore("s_" + e)) for e in ENGS}
        dsems = [es.enter_context(nc.semaphore("d%d" % i)) for i in range(NDSEM)]
        es.enter_context(nc.allow_low_precision("bf16 matmul operands, fp32 accumulate"))
        A = Arena(arena_t, arena32, NB)
        sc = Sched(nc, sems, dsems)

        def P(i, n=512, p0=0, np_=128, c0=0):
            return ps[i][p0:p0 + np_, c0:c0 + n]

        def mm(out, lhsT, rhs, start, stop, reads, writes):
            sc.add("pe", lambda e: e.matmul(out, lhsT, rhs, start=start, stop=stop), reads, writes)

        def act(out, in_, func, reads, writes, bias=None, scale=None):
            kw = {}
            if bias is not None:
                kw["bias"] = bias
            if scale is not None:
                kw["scale"] = scale
            sc.add("act", lambda e: e.activation(out, in_, func, **kw), reads, writes)

        def tt(out, in0, in1, op, reads, writes, eng="dve"):
            sc.add(eng, lambda e: e.tensor_tensor(out, in0, in1, op), reads, writes)

        def ts(out, in0, s1, op0, reads, writes, s2=None, op1=None, eng="dve"):
            if op1 is None:
                sc.add(eng, lambda e: e.tensor_scalar(out, in0, s1, None, op0), reads, writes)
            else:
                sc.add(eng, lambda e: e.tensor_scalar(out, in0, s1, s2, op0, op1), reads, writes)

        def stt(out, in0, scalar, in1, op0, op1, reads, writes):
            sc.add("dve", lambda e: e.scalar_tensor_tensor(out, in0, scalar, in1, op0, op1), reads, writes)

        def recip(out, in_, reads, writes):
            sc.add("dve", lambda e: e.reciprocal(out, in_), reads, writes)

        def cp(out, in_, reads, writes, eng="dve"):
            sc.add(eng, lambda e: e.tensor_copy(out, in_), reads, writes)

        def memset(ap, val, writes, eng="pool"):
            sc.add(eng, lambda e: e.memset(ap, val), (), writes)

        def dma(out, in_, reads, writes, q="sp", is_out=False):
            sc.add(q, lambda e: e.dma_start(out=out, in_=in_), reads, writes, dma=True, is_out=is_out)

        stg = {"offs": None, "i": 0}

        def load_w(dst_off, src_ap_fn, ncols, wkey, np_=128, engs=("act", "pool")):
            for c0 in range(0, ncols, 1024):
                n = min(1024, ncols - c0)
                i = stg["i"] % len(stg["offs"])
                stg["i"] += 1
                so = stg["offs"][i]
                dma(A.a32(so, [[1, n]], 0, np_), src_ap_fn(c0, n), (), ["stg%d" % i])
                eng = engs[stg["i"] % len(engs)]
                if eng in ("pool", "dve"):
                    cp(A.a16(dst_off + c0, [[1, n]], 0, np_), A.a32(so, [[1, n]], 0, np_), ["stg%d" % i], [wkey], eng=eng)
                else:
                    act(A.a16(dst_off + c0, [[1, n]], 0, np_), A.a32(so, [[1, n]], 0, np_), AF.Copy, ["stg%d" % i], [wkey])

        o_cst = A.f32(512)
        o_maskb = A.b16(2048 + 128)
        o_small = A.f32(512)
        o_biasK = A.f32(512)
        o_ucar = A.f32(8)
        o_b1b = A.b16(8)
        o_b2b = A.b16(8)
        small_mark = A.off
        o_yaT = A.b16(4 * NOWN)
        mark_ya = A.off
        o_zT = A.f32(S)
        persist_mark = A.off

        tri = A.a32(o_cst, [[1, 128]])
        ones = A.a32(o_cst + 128, [[1, 128]])
        ident = A.a32(o_cst + 256, [[1, 128]])
        blockones = A.a32(o_cst + 384, [[1, 128]])
        identb = A.a16(o_maskb + 2048, [[1, 128]])

        def small(c0, n=1, p0=0, np_=128):
            return A.a32(o_small + c0, [[1, n]], p0, np_)
        C_MOD = 0
        C_A1 = 48
        C_A2 = 56
        C_CB = 64
        C_CBV = 88
        C_CF = 96
        C_QS = 97
        C_G1 = 98
        C_G2 = 106
        C_BADA = 114
        C_CT = 162
        C_CVA = 170
        C_CVF = 182
        C_FLAG = 314
        C_KMASK = 315
        C_CBG = 379
        C_CB2 = 395
        C_CBW = 439
        C_BF = 483
        C_GQ = 484
        C_GK = 485
        C_TMP = 486
        C_FREF = 494
        C_FRB = 495

        dma(A.a32(o_cst, [[1, 512]]), cst[:, :], (), ["cst"])
        dma(A.a16(o_maskb, [[1, 2176]]), maskb_d[:, :], (), ["maskb"])
        dma(small(C_BADA, 48), b_adaT[:, :], (), ["bada"])
        dma(small(C_CT, 8), cT[:, :], (), ["ct"])
        dma(small(C_G1, 8), g1T[:, :], (), ["g1"])
        dma(small(C_G2, 8), g2T[:, :], (), ["g2"])
        dma(small(C_CVA, 12), convA[:, :], (), ["cva"])
        dma(small(C_CVF, 132), convF[:, :], (), ["cvf"])
        dma(small(C_FLAG, 1), flag[:, :], (), ["flag"])
        dma(small(C_KMASK, 64), kmask[:, :], (), ["kmask"])
        dma(small(C_BF, 1, 0, 8), bfT[:, :], (), ["bf"])
        dma(small(C_GQ, 1), gqT[:, :], (), ["gq"])
        dma(small(C_GK, 1), gkT[:, :], (), ["gk"])

        o_w1 = A.b16(KC * W1C)
        o_scb = A.b16(8)
        o_wp = [A.b16(8 * 512) for _ in range(2)]
        stg["offs"] = [A.f32(1024) for _ in range(4)]
        scb = A.a16(o_scb, [[1, 8]])
        act(scb, small(C_CT, 8), AF.Silu, ["ct"], ["scb"])
        for pj in range(12):
            b = pj % 2
            for k in range(KC):
                load_w(o_wp[b] + k * 512,
                       lambda c0, n, k=k, pj=pj: w_ada[k * 128:(k + 1) * 128, pj * 512 + c0:pj * 512 + c0 + n],
                       512, "wp%d_%d" % (b, k), engs=("dve", "act"))
            for jj in range(4):
                j = pj * 4 + jj
                for k in range(KC):
                    mm(P(7, 1, c0=j), A.a16(o_wp[b] + k * 512 + jj * 128, [[1, 128]]),
                       A.a16(o_scb + k, [[1, 1]]), k == 0, k == KC - 1,
                       ["wp%d_%d" % (b, k), "scb"], ["ps7"])
        tt(small(C_MOD, 48), P(7, 48), small(C_BADA, 48), ALU.add, ["ps7", "bada"], ["mod"])
        stt(small(C_A1, 8), small(C_MOD + 8, 8), 1.0, small(C_G1, 8), ALU.add, ALU.mult, ["mod", "g1"], ["A1t"])
        ts(small(C_A1, 8), small(C_A1, 8), 32.0, ALU.mult, ["A1t"], ["A1"])
        stt(small(C_A2, 8), small(C_MOD + 32, 8), 1.0, small(C_G2, 8), ALU.add, ALU.mult, ["mod", "g2"], ["A2t"])
        ts(small(C_A2, 8), small(C_A2, 8), 32.0, ALU.mult, ["A2t"], ["A2"])
        cp(A.a16(o_b1b, [[1, 8]]), small(C_MOD + 0, 8), ["mod"], ["b1b"])
        cp(A.a16(o_b2b, [[1, 8]]), small(C_MOD + 24, 8), ["mod"], ["b2b"])
        stt(small(C_QS), small(C_GQ), 8.0, small(C_GK), ALU.mult, ALU.mult, ["gq", "gk"], ["qs"])
        for k in range(KC):
            load_w(o_w1 + k * W1C, lambda c0, n, k=k: w_in[k * 128:(k + 1) * 128, c0:c0 + n], W1C, "w1_%d" % k,
                   engs=("dve", "act"))
        sc.barrier()
        A.off = persist_mark

        o_w1 = A.b16(KC * W1C)
        o_xg = [A.f32(KC * 512) for _ in range(3)]
        o_sq = [A.f32(512) for _ in range(2)]
        o_acc = A.f32(512)
        o_hn = [A.b16(KC * 512) for _ in range(2)]
        o_inv = A.f32(512)
        o_rt = A.f32(512)
        o_rt2 = A.f32(512)
        o_kb = [A.f32(512) for _ in range(2)]
        o_ksq = [A.f32(512) for _ in range(2)]
        o_invk = A.f32(512)
        o_kst = [A.b16(512) for _ in range(2)]
        o_vst = A.b16(8 * 4 * 65)
        o_ccs = A.f32(512)
        o_u = A.f32(516)
        o_y = A.f32(512)

        def w1(k, c0, n):
            return A.a16(o_w1 + k * W1C + c0, [[1, n]])

        W1K = ["w1_%d" % k for k in range(KC)]
        b1b = A.a16(o_b1b, [[1, 8]])
        for j in range(24):
            if 20 <= j < 24:
                continue
            for k in range(KC):
                mm(P(7, 1, c0=j), w1(k, j * 128, 128), A.a16(o_b1b + k, [[1, 1]]), k == 0, k == KC - 1,
                   W1K + ["b1b"], ["ps7"])
        for h in range(8):
            for k in range(KC):
                mm(P(7, 1, 0, 64, c0=24 + h), w1(k, 2560 + h * 64, 64), A.a16(o_b1b + k, [[1, 1]]),
                   k == 0, k == KC - 1, W1K + ["b1b"], ["ps7"])
        for k in range(KC):
            mm(P(7, 1, 0, 8, c0=32), w1(k, 3072, 8), A.a16(o_b1b + k, [[1, 1]]), k == 0, k == KC - 1,
               W1K + ["b1b"], ["ps7"])
        cp(small(C_CB, 20), P(7, 20), ["ps7"], ["cb"])
        cp(small(C_CBV, 8, 0, 64), P(7, 8, 0, 64, c0=24), ["ps7"], ["cbv"])
        tt(small(C_CF, 1, 0, 8), P(7, 1, 0, 8, c0=32), small(C_BF, 1, 0, 8), ALU.add, ["ps7", "bf"], ["cf"])
        memset(A.a32(o_u, [[1, 2]]), 0.0, ["u"])
        memset(A.a16(o_vst, [[1, 8 * 4 * 65]]), 1.0, ["vst"])
        memset(small(C_TMP), EPS * D, ["tmpc"], eng="dve")
        memset(small(C_TMP + 1), EPS * 64, ["tmpc"], eng="dve")
        memset(small(C_TMP + 2), 1.0, ["tmpc"], eng="dve")

        def load_group(src_fn, n, xb):
            for k in range(KC):
                dma(A.a32(o_xg[xb] + k * 512, [[1, n]]), src_fn(k), (), ["xg%d_%d" % (xb, k)])

        def norm_group(src_fn, n, Acol, gi, tag, nbuf=2, xb=None, hb=None, do_load=True):
            b = gi % nbuf if xb is None else xb
            hb = b if hb is None else hb
            if do_load:
                load_group(src_fn, n, b)
            for k in range(KC):
                if k == 0:
                    act(A.a32(o_acc, [[1, n]]), A.a32(o_xg[b] + k * 512, [[1, n]]), AF.Square,
                        ["xg%d_%d" % (b, k)], ["acc"])
                else:
                    sb = k % 2
                    act(A.a32(o_sq[sb], [[1, n]]), A.a32(o_xg[b] + k * 512, [[1, n]]), AF.Square,
                        ["xg%d_%d" % (b, k)], ["sq%d" % sb])
                    tt(A.a32(o_acc, [[1, n]]), A.a32(o_acc, [[1, n]]), A.a32(o_sq[sb], [[1, n]]), ALU.add,
                       ["acc", "sq%d" % sb], ["acc"], eng="dve")
            mm(P(2, n), ones, A.a32(o_acc, [[1, n]]), True, True, ["acc", "cst"], ["ps2"])
            act(A.a32(o_rt, [[1, n]]), P(2, n), AF.Ln, ["ps2", "tmpc"], ["rt"], bias=small(C_TMP), scale=1.0)
            act(A.a32(o_inv, [[1, n]]), A.a32(o_rt, [[1, n]]), AF.Exp, ["rt"], ["inv"], scale=-0.5)
            for k in range(KC):
                stt(A.a16(o_hn[hb] + k * 512, [[1, n]]), A.a32(o_xg[b] + k * 512, [[1, n]]),
                    small(Acol + k), A.a32(o_inv, [[1, n]]), ALU.mult, ALU.mult,
                    ["xg%d_%d" % (b, k), "inv", "A1", "A2"], ["hn%d" % hb])
            return b

        hstate = {"i": 0, "pending": None}

        def headnorm_p1(psi, n, cbcol):
            i = hstate["i"] % 2
            hstate["i"] += 1
            act(A.a32(o_kb[i], [[1, n]]), P(psi, n), AF.Identity, ["ps%d" % psi, "cb"], ["kb%d" % i], bias=small(cbcol), scale=1.0)
            act(A.a32(o_ksq[i], [[1, n]]), P(psi, n), AF.Square, ["ps%d" % psi, "cb"], ["ksq%d" % i], bias=small(cbcol), scale=1.0)
            return i

        def headnorm_p2(i, n, scale_ap, kbuf, dst_fn):
            mm(P(3, n), blockones, A.a32(o_ksq[i], [[1, n]]), True, True, ["ksq%d" % i, "cst"], ["ps3"])
            act(A.a32(o_rt2, [[1, n]]), P(3, n), AF.Ln, ["ps3", "tmpc"], ["rt2"], bias=small(C_TMP + 1), scale=1.0)
            act(A.a32(o_invk, [[1, n]]), A.a32(o_rt2, [[1, n]]), AF.Exp, ["rt2"], ["invk"], scale=-0.5)
            out16 = A.a16(o_kst[kbuf], [[1, n]])
            if scale_ap is None:
                tt(out16, A.a32(o_kb[i], [[1, n]]), A.a32(o_invk, [[1, n]]), ALU.mult, ["kb%d" % i, "invk"], ["kst%d" % kbuf])
            else:
                stt(out16, A.a32(o_kb[i], [[1, n]]), scale_ap, A.a32(o_invk, [[1, n]]), ALU.mult, ALU.mult,
                    ["kb%d" % i, "invk", "qs"], ["kst%d" % kbuf])
            dst_fn(out16, kbuf)

        def flush_pending():
            if hstate["pending"] is not None:
                hstate["pending"]()
                hstate["pending"] = None

        groups = ([(c, n, False, 0) for (c, n) in PRE_GROUPS] +
                  [(HALO0 + c, n, True, c) for (c, n) in OWN_GROUPS])
        psr = 0
        PA_BANKS = [0, 1, 4, 5, 6]
        kcount = 0

        def gsrc(gi):
            col0, n, own, oc0 = groups[gi]
            return (lambda k: xT[k * 128:(k + 1) * 128, col0:col0 + n]), n

        def gload(gi):
            f_, n_ = gsrc(gi)
            load_group(f_, n_, gi % 3)

        def stage1(gi):
            f_, n_ = gsrc(gi)
            norm_group(f_, n_, C_A1, gi, "a", xb=gi % 3, hb=gi % 2, do_load=False)

        gload(0)
        gload(1)
        stage1(0)
        for gi, (col0, n, own, oc0) in enumerate(groups):
            if gi + 2 < len(groups):
                gload(gi + 2)
            if gi + 1 < len(groups):
                stage1(gi + 1)
            b = gi % 2
            HN = ["hn%d" % b]

            def hn(k, c0=0, nn=None):
                return A.a16(o_hn[b] + k * 512 + c0, [[1, nn if nn is not None else n]])

            def proj(psi, wc0, wn, np_=128):
                for k in range(KC):
                    mm(P(psi, n, 0, np_), w1(k, wc0, wn), hn(k), k == 0, k == KC - 1, W1K + HN, ["ps%d" % psi])
            for c in range(4):
                psi = PA_BANKS[psr % 5]
                psr += 1
                proj(psi, 2048 + c * 128, 128)
                i = headnorm_p1(psi, n, C_CB + 16 + c)
                flush_pending()
                kbuf = kcount % 2
                kcount += 1

                def fin(i=i, n=n, kbuf=kbuf, c=c, col0=col0):
                    headnorm_p2(i, n, None, kbuf,
                                lambda o16, kb_: dma(kT_d[c * 128:(c + 1) * 128, col0:col0 + n], o16,
                                                     ["kst%d" % kb_], ["kT_d"]))
                hstate["pending"] = fin
            nt = n // 128
            for t in range(nt):
                psi = PA_BANKS[psr % 5]
                psr += 1
                for k in range(KC):
                    mm(P(psi, 512), hn(k, t * 128, 128), w1(k, 2560, 512), k == 0, k == KC - 1, W1K + HN, ["ps%d" % psi])
                if t == 0:
                    flush_pending()
                sc.add("act", lambda e, psi=psi, t=t: e.activation(
                    A.a16(o_vst + t * 65, [[260, 8], [1, 64]]),
                    bass.AP(ps[psi], 0, [[512, 128], [64, 8], [1, 64]]), AF.Copy), ["ps%d" % psi], ["vst"])
            dma(bass.AP(v_w, (col0 // 128) * 65, [[NT * 65, 128], [128 * NT * 65, 8], [1, nt * 65]]),
                A.a16(o_vst, [[260, 8], [1, nt * 65]]), ["vst"], ["v_d"])
            psi = PA_BANKS[psr % 5]
            psr += 1
            proj(psi, 3072, 8, np_=8)
            act(A.a32(o_zT + col0, [[1, n]], 0, 8), P(psi, n, 0, 8), AF.Identity, ["ps%d" % psi, "cf"], ["zT"],
                bias=small(C_CF, 1, 0, 8), scale=1.0)
            if own:
                for c in range(4):
                    psi = PA_BANKS[psr % 5]
                    psr += 1
                    proj(psi, 1536 + c * 128, 128)
                    i = headnorm_p1(psi, n, C_CB + 12 + c)
                    flush_pending()
                    kbuf = kcount % 2
                    kcount += 1

                    def fin(i=i, n=n, kbuf=kbuf, c=c, oc0=oc0):
                        headnorm_p2(i, n, small(C_QS), kbuf,
                                    lambda o16, kb_: dma(qT_d[c * 128:(c + 1) * 128, oc0:oc0 + n], o16,
                                                         ["kst%d" % kb_], ["qT_d"]))
                    hstate["pending"] = fin
                for c in range(4):
                    p_cc = PA_BANKS[psr % 5]
                    psr += 1
                    proj(p_cc, 512 + c * 128, 128)
                    if c == 0:
                        flush_pending()
                    act(A.a32(o_ccs, [[1, n]]), P(p_cc, n), AF.Identity, ["ps%d" % p_cc, "cb"], ["ccs"],
                        bias=small(C_CB + 4 + c), scale=1.0)
                    p_cv = PA_BANKS[psr % 5]
                    psr += 1
                    proj(p_cv, 1024 + c * 128, 128)
                    stt(A.a32(o_u + 2, [[1, n]]), P(p_cv, n), small(C_CB + 8 + c), A.a32(o_ccs, [[1, n]]),
                        ALU.add, ALU.mult, ["ps%d" % p_cv, "ccs", "cb", "ucar%d" % c], ["u"])
                    if oc0 == 0:
                        memset(A.a32(o_u, [[1, 2]]), 0.0, ["u"], eng="dve")
                        ts(A.a32(o_u + 2, [[1, n]]), A.a32(o_u + 2, [[1, n]]), small(C_FLAG), ALU.mult, ["u", "flag"], ["u"])
                    else:
                        cp(A.a32(o_u, [[1, 2]]), A.a32(o_ucar + c * 2, [[1, 2]]), ["ucar%d" % c], ["u"])
                    ts(A.a32(o_y, [[1, n]]), A.a32(o_u + 2, [[1, n]]), small(C_CVA + c * 3 + 2), ALU.mult, ["u", "cva"], ["y"])
                    stt(A.a32(o_y, [[1, n]]), A.a32(o_u + 1, [[1, n]]), small(C_CVA + c * 3 + 1), A.a32(o_y, [[1, n]]),
                        ALU.mult, ALU.add, ["u", "y", "cva"], ["y"])
                    stt(A.a32(o_y, [[1, n]]), A.a32(o_u, [[1, n]]), small(C_CVA + c * 3 + 0), A.a32(o_y, [[1, n]]),
                        ALU.mult, ALU.add, ["u", "y", "cva"], ["y"])
                    cp(A.a32(o_ucar + c * 2, [[1, 2]]), A.a32(o_u + n, [[1, 2]]), ["u"], ["ucar%d" % c])
                    p_cb = PA_BANKS[psr % 5]
                    psr += 1
                    proj(p_cb, c * 128, 128)
                    stt(A.a16(o_yaT + c * NOWN + oc0, [[1, n]]), P(p_cb, n), small(C_CB + c), A.a32(o_y, [[1, n]]),
                        ALU.add, ALU.mult, ["ps%d" % p_cb, "y", "cb"], ["yaT"])
        flush_pending()
        sc.barrier()
        A.off = persist_mark

        o_e = A.f32(2048)
        o_val = A.f32(2048)
        o_one8 = A.f32(2048)
        o_fqf = A.f32(NOWN)
        o_fr = A.f32(NOWN)
        o_fq16 = A.b16(3 * NOWN)
        o_diag = A.f32(8)

        def r8(off, n, c0=0):
            return A.a32(off + c0, [[1, n]], 0, 8)

        memset(r8(o_one8, 2048), 1.0, ["one8"], eng="dve")
        for pc in range(4):
            c0 = pc * 2048
            dma(r8(o_val, 2048), validT[:, c0:c0 + 2048], (), ["val"])
            act(r8(o_e, 2048), r8(o_zT, 2048, c0), AF.Exp, ["zT"], ["e"], scale=-1.0)
            act(r8(o_e, 2048), r8(o_e, 2048), AF.Ln, ["e", "tmpc"], ["e"], bias=small(C_TMP + 2, 1, 0, 8), scale=1.0)
            stt(r8(o_e, 2048), r8(o_e, 2048), -1.0, r8(o_val, 2048), ALU.mult, ALU.mult, ["e", "val"], ["e"])
            init = 0.0 if pc == 0 else r8(o_zT, 1, c0 - 1)
            sc.add("dve", lambda e, c0=c0, init=init: e.tensor_tensor_scan(
                r8(o_zT, 2048, c0), r8(o_one8, 2048), r8(o_e, 2048), init, ALU.mult, ALU.add),
                ["e", "one8", "zT"], ["zT"])
        cp(small(C_FREF, 1, 0, 8), r8(o_zT, 1, S - 1), ["zT"], ["fref"])
        ts(r8(o_fqf, NOWN), r8(o_zT, NOWN, HALO0), small(C_FREF, 1, 0, 8), ALU.subtract, ["zT", "fref"], ["fqf"])
        cp(A.a16(o_fq16, [[1, NOWN]], 0, 8), r8(o_fqf, NOWN), ["fqf"], ["fq16a"])
        tt(r8(o_fr, NOWN), r8(o_fqf, NOWN), A.a16(o_fq16, [[1, NOWN]], 0, 8), ALU.subtract, ["fqf", "fq16a"], ["fr"])
        cp(A.a16(o_fq16 + NOWN, [[1, NOWN]], 0, 8), r8(o_fr, NOWN), ["fr"], ["fq16b"])
        tt(r8(o_fqf, NOWN), r8(o_fr, NOWN), A.a16(o_fq16 + NOWN, [[1, NOWN]], 0, 8), ALU.subtract, ["fr", "fq16b"], ["fqf"])
        cp(A.a16(o_fq16 + 2 * NOWN, [[1, NOWN]], 0, 8), r8(o_fqf, NOWN), ["fqf"], ["fq16c"])
        dma(fq_d[:, :], A.a16(o_fq16, [[1, 3 * NOWN]], 0, 8), ["fq16a", "fq16b", "fq16c"], ["fq_d"])
        for t in range(NT):
            mm(P(6, 8, c0=t * 8), r8(o_zT, 128, t * 128), A.a32(o_cst + 256, [[1, 8]], 0, 8), True, True,
               ["zT", "cst"], ["ps6"])
        ts(A.a32(o_diag, [[1, 8]], 0, 8), A.a32(o_cst + 256, [[1, 8]], 0, 8), small(C_FREF, 1, 0, 8), ALU.mult,
           ["cst", "fref"], ["diag"])
        mm(P(7, 8), A.a32(o_cst + 128, [[1, 128]], 0, 8), A.a32(o_diag, [[1, 8]], 0, 8), True, True, ["diag", "cst"], ["ps7"])
        cp(small(C_FRB, 8), P(7, 8), ["ps7"], ["frb"])
        tt(A.a32(o_biasK, [[8, NT], [1, 8]]), A.a32(o_small + C_FRB, [[0, NT], [1, 8]]),
           bass.AP(ps[6], 0, [[512, 128], [8, NT], [1, 8]]), ALU.subtract, ["frb", "ps6"], ["biasK"])
        tt(A.a32(o_biasK, [[8, NT], [1, 8]]), A.a32(o_biasK, [[8, NT], [1, 8]]),
           A.a32(o_small + C_KMASK, [[1, NT], [0, 8]]), ALU.add, ["biasK", "kmask"], ["biasK"])
        sc.barrier()
        A.off = mark_ya
        o_ybT = A.b16(8 * NOWN)
        persist_mark = A.off

        o_wg = A.b16(KC * 2048)
        o_wo = A.b16(KC * 1024)
        mark_c = A.off
        stg["offs"] = [A.f32(1024) for _ in range(2)]
        o_k = [A.b16(S) for _ in range(2)]
        o_v = [A.b16(NT * 65) for _ in range(2)]
        o_q = [A.b16(NOWN) for _ in range(2)]
        NPT = 4
        o_pT = [A.b16(512) for _ in range(NPT)]
        o_rec = A.f32(512)
        o_osb = A.f32(512)
        o_tmp = A.f32(512)
        for b in range(2):
            memset(A.a16(o_k[b], [[1, S]], 64, 3), 1.0, ["k%d" % b], eng="pool")

        def load_head(h):
            b = h % 2
            KB, VB, QB = "k%d" % b, "v%d" % b, "q%d" % b
            for half in range(2):
                dma(A.a16(o_k[b] + half * 4096, [[1, 4096]], 0, 64),
                    kT_d[h * 64:(h + 1) * 64, half * 4096:(half + 1) * 4096], ["kT_d"], [KB])
            dma(A.a16(o_v[b], [[1, NT * 65]]), v_d[h * 128:(h + 1) * 128, :], ["v_d"], [VB])
            dma(A.a16(o_q[b], [[1, NOWN]], 0, 64), qT_d[h * 64:(h + 1) * 64, :], ["qT_d"], [QB])
            dma(A.a16(o_q[b], [[1, NOWN]], 64, 3),
                bass.AP(fq_w, h * 3 * NOWN, [[NOWN, 3], [1, NOWN]]), ["fq_d"], [QB])

        units = []
        blk = 0
        for h in range(8):
            for (q0, nq) in OWN_GROUPS:
                qt0 = 47 + q0 // 128
                nvis = 47 + (q0 + nq) // 128
                for kt in range(nvis):
                    units.append((h, q0, nq, kt, kt == 0, kt == nvis - 1, kt >= qt0, kt - qt0, blk))
                blk += 1

        def emit_S(idx):
            h, q0, nq, kt, first, last, partial, jm, blk = units[idx]
            b = h % 2
            KB, QB = "k%d" % b, "q%d" % b
            sb = 3 + idx % 3
            pb = idx % NPT
            mm(P(sb, nq), A.a16(o_k[b] + kt * 128, [[1, 128]], 0, 67), A.a16(o_q[b] + q0, [[1, nq]], 0, 67),
               True, not partial, [KB, QB], ["ps%d" % sb])
            if partial:
                mm(P(sb, nq), identb, A.a16(o_maskb + jm * 512, [[1, nq]]), False, True, ["maskb"], ["ps%d" % sb])
            act(A.a16(o_pT[pb], [[1, nq]]), P(sb, nq), AF.Exp, ["ps%d" % sb, "biasK"], ["pT%d" % pb],
                bias=A.a32(o_biasK + kt * 8 + h, [[1, 1]]), scale=1.0)

        def emit_PV(idx):
            h, q0, nq, kt, first, last, partial, jm, blk = units[idx]
            b = h % 2
            VB = "v%d" % b
            pb = idx % NPT
            ob = 6 + blk % 2
            OK_ = "ps%d" % ob
            mm(P(ob, nq, 0, 65), A.a16(o_v[b] + kt * 65, [[1, 65]]), A.a16(o_pT[pb], [[1, nq]]),
               first, last, [VB, "pT%d" % pb], [OK_])
            if last:
                ts(A.a32(o_rec, [[1, nq]], 64, 1), P(ob, nq, 64, 1), 1e-30, ALU.add, [OK_], ["rec"])
                recip(A.a32(o_rec, [[1, nq]], 64, 1), A.a32(o_rec, [[1, nq]], 64, 1), ["rec"], ["rec"])
                mm(P(2, nq, 0, 64), A.a32(o_cst + 128, [[1, 64]], 64, 1), A.a32(o_rec, [[1, nq]], 64, 1), True, True,
                   ["rec", "cst"], ["ps2"])
                cp(A.a32(o_osb, [[1, nq]], 0, 64), P(ob, nq, 0, 64), [OK_], ["osb"])
                tt(A.a32(o_tmp, [[1, nq]], 0, 64), A.a32(o_osb, [[1, nq]], 0, 64), P(2, nq, 0, 64), ALU.mult,
                   ["osb", "ps2"], ["tmp"])
                ts(A.a16(o_ybT + h * NOWN + q0, [[1, nq]], 0, 64), A.a32(o_tmp, [[1, nq]], 0, 64),
                   small(C_CBV + h, 1, 0, 64), ALU.add, ["tmp", "cbv"], ["ybT"])

        LA = 2
        load_head(0)
        load_head(1)
        for k in range(KC):
            load_w(o_wg + k * 2048, lambda c0, n, k=k: w_in[k * 128:(k + 1) * 128, 3080 + c0:3080 + c0 + n], 2048, "wg",
                   engs=("pool",))
            load_w(o_wo + k * 1024, lambda c0, n, k=k: w_o[k * 128:(k + 1) * 128, c0:c0 + n], 1024, "wo", engs=("pool",))
        for idx in range(len(units) + LA):
            if idx < len(units):
                u = units[idx]
                if u[3] == LA + 2 and u[1] == 0 and 1 <= u[0] and u[0] + 1 < 8:
                    load_head(u[0] + 1)
                emit_S(idx)
            if idx - LA >= 0:
                emit_PV(idx - LA)
        sc.barrier()
        A.off = mark_c

        o_wa = A.b16(4 * 1024)
        o_wb = A.b16(8 * 1024)
        o_xg = [A.f32(KC * 512)]
        o_sq = [A.f32(512) for _ in range(2)]
        o_acc = A.f32(512)
        o_hn = [A.b16(KC * 512)]
        o_inv = A.f32(512)
        o_rt = A.f32(512)
        o_sga = A.f32(512)
        o_sgb = A.f32(512)
        o_m1 = A.f32(512)
        o_m2 = A.f32(512)
        o_mg = A.b16(KC * 512)
        stg["offs"] = [o_xg[0], o_xg[0] + 1024, o_xg[0] + 2048]
        for c in range(4):
            load_w(o_wa + c * 1024, lambda c0, n, c=c: w_a[c * 128:(c + 1) * 128, c0:c0 + n], 1024, "wa", engs=("dve", "act"))
        for h in range(8):
            load_w(o_wb + h * 1024, lambda c0, n, h=h: w_b[h * 64:(h + 1) * 64, c0:c0 + n], 1024, "wb", np_=64, engs=("dve", "act"))
        sc.barrier()
        for j in range(16):
            for k in range(KC):
                mm(P(7, 1, c0=j), A.a16(o_wg + k * 2048 + j * 128, [[1, 128]]), A.a16(o_b1b + k, [[1, 1]]),
                   k == 0, k == KC - 1, ["wg", "b1b"], ["ps7"])
        cp(small(C_CBG, 16), P(7, 16), ["ps7"], ["cbg"])
        for gi2, (oc0, n) in enumerate(OWN_GROUPS):
            col0 = HALO0 + oc0
            norm_group(lambda k: xT[k * 128:(k + 1) * 128, col0:col0 + n], n, C_A1, 0, "c", nbuf=1)
            for e_ in range(KC):
                pa_, pb_ = [0, 6][e_ % 2], [1, 7][e_ % 2]
                for k in range(KC):
                    mm(P(pa_, n), A.a16(o_wg + k * 2048 + e_ * 128, [[1, 128]]), A.a16(o_hn[0] + k * 512, [[1, n]]),
                       k == 0, k == KC - 1, ["wg", "hn0"], ["ps%d" % pa_])
                act(A.a32(o_sga, [[1, n]]), P(pa_, n), AF.Sigmoid, ["ps%d" % pa_, "cbg"], ["sga"], bias=small(C_CBG + e_), scale=1.0)
                for k in range(KC):
                    mm(P(pb_, n), A.a16(o_wg + k * 2048 + 1024 + e_ * 128, [[1, 128]]), A.a16(o_hn[0] + k * 512, [[1, n]]),
                       k == 0, k == KC - 1, ["wg", "hn0"], ["ps%d" % pb_])
                act(A.a32(o_sgb, [[1, n]]), P(pb_, n), AF.Sigmoid, ["ps%d" % pb_, "cbg"], ["sgb"], bias=small(C_CBG + 8 + e_), scale=1.0)
                for c in range(4):
                    mm(P(4, n), A.a16(o_wa + c * 1024 + e_ * 128, [[1, 128]]), A.a16(o_yaT + c * NOWN + oc0, [[1, n]]),
                       c == 0, c == 3, ["wa", "yaT"], ["ps4"])
                tt(A.a32(o_m1, [[1, n]]), A.a32(o_sga, [[1, n]]), P(4, n), ALU.mult, ["sga", "ps4"], ["m1"])
                for h in range(8):
                    mm(P(5, n), A.a16(o_wb + h * 1024 + e_ * 128, [[1, 128]], 0, 64),
                       A.a16(o_ybT + h * NOWN + oc0, [[1, n]], 0, 64), h == 0, h == 7, ["wb", "ybT"], ["ps5"])
                tt(A.a32(o_m2, [[1, n]]), A.a32(o_sgb, [[1, n]]), P(5, n), ALU.mult, ["sgb", "ps5"], ["m2"])
                tt(A.a16(o_mg + e_ * 512, [[1, n]]), A.a32(o_m1, [[1, n]]), A.a32(o_m2, [[1, n]]), ALU.add,
                   ["m1", "m2"], ["mg%d" % e_], eng="pool")
            MG = ["mg%d" % i for i in range(KC)]
            for e_ in range(KC):
                psi = 2 + e_ % 2
                for k in range(KC):
                    mm(P(psi, n), A.a16(o_wo + k * 1024 + e_ * 128, [[1, 128]]), A.a16(o_mg + k * 512, [[1, n]]),
                       k == 0, k == KC - 1, ["wo"] + MG, ["ps%d" % psi])
                stt(A.a32(o_xg[0] + e_ * 512, [[1, n]]), P(psi, n), small(C_MOD + 16 + e_),
                    A.a32(o_xg[0] + e_ * 512, [[1, n]]), ALU.mult, ALU.add,
                    ["ps%d" % psi, "mod", "xg0_%d" % e_], ["xg0_%d" % e_])
                dma(x1_d[e_ * 128:(e_ + 1) * 128, oc0:oc0 + n], A.a32(o_xg[0] + e_ * 512, [[1, n]]),
                    ["xg0_%d" % e_], ["x1_d"])
        sc.barrier()
        A.off = small_mark

        NF = 2 * DFF
        o_wup = A.b16(KC * NF)
        o_wdn = A.b16(22 * 1024)
        o_xg = [A.f32(KC * 512)]
        o_sq = [A.f32(512) for _ in range(2)]
        o_acc = A.f32(512)
        o_hn = [A.b16(KC * 512)]
        o_inv = A.f32(512)
        o_rt = A.f32(512)
        o_yg = A.f32(512)
        o_yv = A.f32(512)
        o_sg = A.f32(512)
        o_act = A.b16(22 * 512)
        o_ws = A.f32(44)
        stg["offs"] = [o_xg[0], o_xg[0] + 1024, o_xg[0] + 2048]
        for k in range(KC):
            load_w(o_wup + k * NF, lambda c0, n, k=k: w_up[k * 128:(k + 1) * 128, c0:c0 + n], NF, "wup", engs=("dve", "act"))
        for c in range(22):
            load_w(o_wdn + c * 1024, lambda c0, n, c=c: w_dn[c * 128:(c + 1) * 128, c0:c0 + n], 1024, "wdn", engs=("dve", "act"))
        sc.barrier()
        for fc in range(44):
            for k in range(KC):
                mm(P(7, 1, c0=fc), A.a16(o_wup + k * NF + fc * 128, [[1, 128]]), A.a16(o_b2b + k, [[1, 1]]),
                   k == 0, k == KC - 1, ["wup", "b2b"], ["ps7"])
        cp(small(C_CB2, 44), P(7, 44), ["ps7"], ["cb2"])
        tt(A.a32(o_ws, [[1, 44]]), A.a32(o_small + C_CVF, [[3, 44]]), A.a32(o_small + C_CVF + 1, [[3, 44]]), ALU.add,
           ["cvf"], ["ws"])
        tt(A.a32(o_ws, [[1, 44]]), A.a32(o_ws, [[1, 44]]), A.a32(o_small + C_CVF + 2, [[3, 44]]), ALU.add, ["ws", "cvf"], ["ws"])
        tt(small(C_CBW, 44), small(C_CB2, 44), A.a32(o_ws, [[1, 44]]), ALU.mult, ["ws", "cb2"], ["cbw"])
        ts(small(C_TMP + 3), small(C_FLAG), -1.0, ALU.add, ["flag"], ["fm1"])
        o_tc = A.f32(2)
        pcount = 0
        for g in range(5):
            w0c = 126 + 510 * g
            n = min(512, NOWN - w0c)
            m = n - 2
            norm_group(lambda k: x1_d[k * 128:(k + 1) * 128, w0c:w0c + n], n, C_A2, 0, "d", nbuf=1)
            for c in range(22):
                for which, fc in ((0, c), (1, 22 + c)):
                    psi = [0, 1, 6, 7][pcount % 4]
                    pcount += 1
                    oy = o_yg if which == 0 else o_yv
                    YK = "yg" if which == 0 else "yv"
                    PK = "ps%d" % psi
                    for k in range(KC):
                        mm(P(psi, n), A.a16(o_wup + k * NF + fc * 128, [[1, 128]]), A.a16(o_hn[0] + k * 512, [[1, n]]),
                           k == 0, k == KC - 1, ["wup", "hn0"], [PK])
                    act(A.a32(oy, [[1, m]]), P(psi, m, c0=2), AF.Identity, [PK, "cvf", "cbw"], [YK],
                        bias=small(C_CBW + fc), scale=small(C_CVF + fc * 3 + 2))
                    stt(A.a32(oy, [[1, m]]), P(psi, m, c0=1), small(C_CVF + fc * 3 + 1), A.a32(oy, [[1, m]]),
                        ALU.mult, ALU.add, [PK, YK, "cvf"], [YK])
                    stt(A.a32(oy, [[1, m]]), P(psi, m, c0=0), small(C_CVF + fc * 3 + 0), A.a32(oy, [[1, m]]),
                        ALU.mult, ALU.add, [PK, YK, "cvf"], [YK])
                    if g == 0:
                        ts(A.a32(o_tc, [[1, 2]]), P(psi, 2), small(C_CB2 + fc), ALU.add, [PK, "cb2", "fm1"], ["tc"],
                           s2=small(C_TMP + 3), op1=ALU.mult)
                        stt(A.a32(oy, [[1, 2]]), A.a32(o_tc, [[1, 2]]), small(C_CVF + fc * 3 + 0), A.a32(oy, [[1, 2]]),
                            ALU.mult, ALU.add, ["tc", YK, "cvf"], [YK])
                        stt(A.a32(oy, [[1, 1]]), A.a32(o_tc + 1, [[1, 1]]), small(C_CVF + fc * 3 + 1), A.a32(oy, [[1, 1]]),
                            ALU.mult, ALU.add, ["tc", YK, "cvf"], [YK])
                act(A.a32(o_sg, [[1, m]]), A.a32(o_yg, [[1, m]]), AF.Silu, ["yg"], ["sg"])
                tt(A.a16(o_act + c * 512, [[1, m]]), A.a32(o_sg, [[1, m]]), A.a32(o_yv, [[1, m]]), ALU.mult,
                   ["sg", "yv"], ["act%d" % c], eng="pool")
            ACTK = ["act%d" % c for c in range(22)]
            for e_ in range(KC):
                psi = 4 + e_ % 2
                for c in range(22):
                    mm(P(psi, m), A.a16(o_wdn + c * 1024 + e_ * 128, [[1, 128]]), A.a16(o_act + c * 512, [[1, m]]),
                       c == 0, c == 21, ["wdn"] + ACTK, ["ps%d" % psi])
                stt(A.a32(o_xg[0] + e_ * 512 + 2, [[1, m]]), P(psi, m), small(C_MOD + 40 + e_),
                    A.a32(o_xg[0] + e_ * 512 + 2, [[1, m]]), ALU.mult, ALU.add,
                    ["ps%d" % psi, "mod", "xg0_%d" % e_], ["xg0_%d" % e_])
                dma(outT[e_ * 128:(e_ + 1) * 128, w0c + 2 - 128:w0c + n - 128], A.a32(o_xg[0] + e_ * 512 + 2, [[1, m]]),
                    ["xg0_%d" % e_], ["outT"], is_out=True)
        sc.finish()

        with nc.Block() as block:
            @block.tensor
            def _(e):
                sc.emit("pe", e)

            @block.scalar
            def _(e):
                sc.emit("act", e)

            @block.vector
            def _(e):
                sc.emit("dve", e)

            @block.gpsimd
            def _(e):
                sc.emit("pool", e)

            @block.sync
            def _(e):
                sc.emit("sp", e)
    return nc


_NC_CACHE = {}


def _consts():
    p = np.arange(128)
    tri = (p[:, None] <= p[None, :]).astype(np.float32)
    ones = np.ones((128, 128), np.float32)
    ident = np.eye(128, dtype=np.float32)
    blk = ((p[:, None] // 64) == (p[None, :] // 64)).astype(np.float32)
    cst = np.concatenate([tri, ones, ident, blk], axis=1)
    c = np.arange(512)
    masks = []
    for jm in range(4):
        masks.append(np.where(c[None, :] >= 128 * jm + p[:, None], 0.0, NEG).astype(np.float32))
    maskb = np.concatenate(masks + [ident], axis=1).astype(ml_dtypes.bfloat16)
    return cst, maskb


def prep_inputs(x, c, w_ada, b_ada, norm1_g, w_in, b_f, conv_a_w, q_norm_g, k_norm_g,
                w_branch_a, w_branch_b, w_out, norm2_g, w_up, conv_ffn_w, w_down):
    f = lambda a: np.ascontiguousarray(np.asarray(a, dtype=np.float32))
    x = f(x); c = f(c)
    cst, maskb = _consts()
    shared = {
        "w_ada": f(w_ada[0]),
        "b_adaT": f(np.asarray(b_ada[0]).reshape(48, 128).T),
        "g1T": f(np.asarray(norm1_g[0]).reshape(8, 128).T),
        "g2T": f(np.asarray(norm2_g[0]).reshape(8, 128).T),
        "w_in": f(w_in[0]),
        "bfT": f(np.asarray(b_f[0])[:, None]),
        "convA": f(np.asarray(conv_a_w[0]).reshape(3, 4, 128).transpose(2, 1, 0).reshape(128, 12)),
        "gqT": f(np.tile(np.asarray(q_norm_g[0]), 2)[:, None]),
        "gkT": f(np.tile(np.asarray(k_norm_g[0]), 2)[:, None]),
        "w_a": f(w_branch_a[0]),
        "w_b": f(w_branch_b[0]),
        "w_o": f(w_out[0]),
        "w_up": f(w_up[0]),
        "convF": f(np.asarray(conv_ffn_w[0]).reshape(3, 44, 128).transpose(2, 1, 0).reshape(128, 132)),
        "w_dn": f(w_down[0]),
        "cst": cst,
        "maskb": maskb,
    }
    in_maps = []
    pidx = np.arange(128)[:, None] + 128 * np.arange(NT)[None, :]
    for j in range(8):
        b, i = j // 4, j % 4
        own_end = 2048 * (i + 1)
        ws = own_end - S
        xTw = np.zeros((D, S), np.float32)
        t0 = max(ws, 0)
        xTw[:, t0 - ws:] = x[b, t0:own_end, :].T
        valid = ((np.arange(S) + ws) >= 0).astype(np.float32)
        m = dict(shared)
        m["xT"] = xTw
        m["validT"] = np.ascontiguousarray(np.broadcast_to(valid[None, :], (8, S)))
        m["kmask"] = np.where((pidx + ws) >= 0, 0.0, NEG).astype(np.float32)
        m["flag"] = np.full((128, 1), 1.0 if i > 0 else 0.0, np.float32)
        m["cT"] = f(c[b].reshape(8, 128).T)
        in_maps.append(m)
    return in_maps


def kernel(**inputs):
    in_maps = prep_inputs(**inputs)
    if "nc" not in _NC_CACHE:
        _NC_CACHE["nc"] = build_nc()
    nc = _NC_CACHE["nc"]
    res = run_bass_kernel_spmd(nc, in_maps, core_ids=list(range(8)))
    out = np.empty((2, S, D), np.float32)
    for j in range(8):
        b, i = j // 4, j % 4
        out[b, 2048 * i:2048 * (i + 1), :] = np.asarray(res.results[j]["outT"]).T
    return out
```

```python
import numpy as np
import ml_dtypes
import concourse.bass as bass
import concourse.mybir as mybir
from concourse.bass_utils import run_bass_kernel_spmd

F32 = mybir.dt.float32
BF16 = mybir.dt.bfloat16
AF = mybir.ActivationFunctionType
ALU = mybir.AluOpType

D = 1024
KC = 8
S = 8192
NT = 64
HALO0 = 47 * 128
NOWN = 2176
W1C = 3080
DFF = 2816
NEG = -30000.0
EPS = 1e-6
OWN_GROUPS = [(0, 128), (128, 512), (640, 512), (1152, 512), (1664, 512)]
PRE_GROUPS = [(i * 512, 512) for i in range(11)] + [(5632, 384)]

ENGS = ["pe", "act", "dve", "pool", "sp"]
NDSEM = 24


class Sched:
    def __init__(self, nc, sems, dsems):
        self.nc = nc
        self.sems = sems
        self.dsems = dsems
        self.streams = {e: [] for e in ENGS}
        self.cnt = {e: 0 for e in ENGS}
        self.lastw = {}
        self.readers = {}
        self.waited = {e: {} for e in ENGS}
        self.ndma = 0
        self.out_tokens = []

    def semof(self, tok):
        if tok[0] == "c":
            return (("c", tok[1]), tok[2])
        k = tok[1]
        return (("d", k % NDSEM), 16 * (k // NDSEM + 1))

    def _need(self, eng, sv, waits):
        key, val = sv
        if self.waited[eng].get(key, 0) >= val:
            return
        self.waited[eng][key] = val
        waits.append(sv)

    def add(self, eng, fn, reads=(), writes=(), dma=False, is_out=False):
        deps = set()
        for r in reads:
            t = self.lastw.get(r)
            if t is not None:
                deps.add(t)
        for w in writes:
            t = self.lastw.get(w)
            if t is not None:
                deps.add(t)
            for t in self.readers.get(w, ()):
                deps.add(t)
        waits = []
        for tok in deps:
            if tok[0] == "c" and tok[1] == eng and not dma:
                if eng == "pe":
                    continue
                if self.cnt[eng] - tok[2] > 3:
                    continue
            self._need(eng, self.semof(tok), waits)
        if dma:
            k = self.ndma
            self.ndma += 1
            if k >= NDSEM:
                self._need(eng, (("d", k % NDSEM), 16 * (k // NDSEM)), waits)
            tok = ("d", k)
            inc = (("d", k % NDSEM), 16)
            if is_out:
                self.out_tokens.append(tok)
        else:
            self.cnt[eng] += 1
            tok = ("c", eng, self.cnt[eng])
            inc = (("c", eng), 1)
        self.streams[eng].append((waits, fn, inc))
        for r in reads:
            self.readers.setdefault(r, []).append(tok)
        for w in writes:
            self.lastw[w] = tok
            self.readers[w] = []

    def barrier(self):
        for e in ENGS:
            waits = []
            for e2 in ENGS:
                if e2 != e and self.cnt[e2] > 0:
                    self._need(e, (("c", e2), self.cnt[e2]), waits)
            lo = max(0, self.ndma - NDSEM)
            for k in range(lo, self.ndma):
                self._need(e, self.semof(("d", k)), waits)
            if waits:
                self.streams[e].append((waits, None, None))
        self.lastw = {}
        self.readers = {}

    def finish(self):
        waits = []
        for t in self.out_tokens:
            self._need("sp", self.semof(t), waits)
        lo = max(0, self.ndma - NDSEM)
        for k in range(lo, self.ndma):
            self._need("sp", self.semof(("d", k)), waits)
        self.streams["sp"].append((waits, None, None))

    def _sem(self, key):
        return self.sems[key[1]] if key[0] == "c" else self.dsems[key[1]]

    def emit(self, eng, e):
        for waits, fn, inc in self.streams[eng]:
            for key, val in waits:
                e.wait_ge(self._sem(key), val)
            if fn is not None:
                ins = fn(e)
                if inc is not None:
                    ins.then_inc(self._sem(inc[0]), inc[1])


class Arena:
    def __init__(self, t16, t32, nbytes):
        self.t16, self.t32, self.nb = t16, t32, nbytes
        self.off = 0
        self.n16 = nbytes // 2
        self.n32 = nbytes // 4

    def alloc(self, nbytes):
        o = self.off
        self.off = (o + nbytes + 63) // 64 * 64
        assert self.off <= self.nb, ("arena overflow", self.off)
        return o

    def f32(self, cols):
        return self.alloc(cols * 4) // 4

    def b16(self, cols):
        return self.alloc(cols * 2) // 2

    def a32(self, off, dims, p0=0, np_=128):
        return bass.AP(self.t32, p0 * self.n32 + off, [[self.n32, np_]] + [list(d) for d in dims])

    def a16(self, off, dims, p0=0, np_=128):
        return bass.AP(self.t16, p0 * self.n16 + off, [[self.n16, np_]] + [list(d) for d in dims])


def build_nc():
    nc = bass.Bass("TRN2", target_bir_lowering=False)
    dt = nc.dram_tensor

    def din(name, shape, dtype=F32):
        return dt(name, list(shape), dtype, kind="ExternalInput").ap()

    xT = din("xT", [D, S])
    validT = din("validT", [8, S])
    kmask = din("kmask", [128, NT])
    flag = din("flag", [128, 1])
    cT = din("cT", [128, 8])
    w_ada = din("w_ada", [D, 6144])
    b_adaT = din("b_adaT", [128, 48])
    g1T = din("g1T", [128, 8])
    g2T = din("g2T", [128, 8])
    w_in = din("w_in", [D, 5128])
    bfT = din("bfT", [8, 1])
    convA = din("convA", [128, 12])
    gqT = din("gqT", [128, 1])
    gkT = din("gkT", [128, 1])
    w_a = din("w_a", [512, D])
    w_b = din("w_b", [512, D])
    w_o = din("w_o", [D, D])
    w_up = din("w_up", [D, 2 * DFF])
    convF = din("convF", [128, 132])
    w_dn = din("w_dn", [DFF, D])
    cst = din("cst", [128, 512])
    maskb_d = din("maskb", [128, 2048 + 128], BF16)
    outT = dt("outT", [D, 2048], F32, kind="ExternalOutput").ap()
    kT_d = dt("kT_d", [512, S], BF16).ap()
    v_w = dt("v_d", [8 * 128, NT * 65], BF16)
    v_d = v_w.ap()
    qT_d = dt("qT_d", [512, NOWN], BF16).ap()
    fq_w = dt("fq_d", [8, 3 * NOWN], BF16)
    fq_d = fq_w.ap()
    x1_d = dt("x1_d", [D, NOWN], F32).ap()

    NB = 210944
    import contextlib
    with contextlib.ExitStack() as es:
        arena_t = es.enter_context(nc.sbuf_tensor("arena", [128, NB // 2], BF16))
        arena32 = arena_t.bitcast(F32)
        ps = [es.enter_context(nc.psum_tensor("ps%d" % i, [128, 512], F32)) for i in range(8)]
        sems = {e: es.enter_context(nc.semaphore("s_" + e)) for e in ENGS}
        dsems = [es.enter_context(nc.semaphore("d%d" % i)) for i in range(NDSEM)]
        es.enter_context(nc.allow_low_precision("bf16 matmul operands, fp32 accumulate"))
        A = Arena(arena_t, arena32, NB)
        sc = Sched(nc, sems, dsems)

        def P(i, n=512, p0=0, np_=128, c0=0):
            return ps[i][p0:p0 + np_, c0:c0 + n]

        def mm(out, lhsT, rhs, start, stop, reads, writes):
            sc.add("pe", lambda e: e.matmul(out, lhsT, rhs, start=start, stop=stop), reads, writes)

        def act(out, in_, func, reads, writes, bias=None, scale=None):
            kw = {}
            if bias is not None:
                kw["bias"] = bias
            if scale is not None:
                kw["scale"] = scale
            sc.add("act", lambda e: e.activation(out, in_, func, **kw), reads, writes)

        def tt(out, in0, in1, op, reads, writes, eng="dve"):
            sc.add(eng, lambda e: e.tensor_tensor(out, in0, in1, op), reads, writes)

        def ts(out, in0, s1, op0, reads, writes, s2=None, op1=None, eng="dve"):
            if op1 is None:
                sc.add(eng, lambda e: e.tensor_scalar(out, in0, s1, None, op0), reads, writes)
            else:
                sc.add(eng, lambda e: e.tensor_scalar(out, in0, s1, s2, op0, op1), reads, writes)

        def stt(out, in0, scalar, in1, op0, op1, reads, writes):
            sc.add("dve", lambda e: e.scalar_tensor_tensor(out, in0, scalar, in1, op0, op1), reads, writes)

        def recip(out, in_, reads, writes):
            sc.add("dve", lambda e: e.reciprocal(out, in_), reads, writes)

        def cp(out, in_, reads, writes, eng="dve"):
            sc.add(eng, lambda e: e.tensor_copy(out, in_), reads, writes)

        def memset(ap, val, writes, eng="pool"):
            sc.add(eng, lambda e: e.memset(ap, val), (), writes)

        def dma(out, in_, reads, writes, q="sp", is_out=False):
            sc.add(q, lambda e: e.dma_start(out=out, in_=in_), reads, writes, dma=True, is_out=is_out)

        stg = {"offs": None, "i": 0}

        def load_w(dst_off, src_ap_fn, ncols, wkey, np_=128, engs=("act", "pool")):
            for c0 in range(0, ncols, 1024):
                n = min(1024, ncols - c0)
                i = stg["i"] % len(stg["offs"])
                stg["i"] += 1
                so = stg["offs"][i]
                dma(A.a32(so, [[1, n]], 0, np_), src_ap_fn(c0, n), (), ["stg%d" % i])
                eng = engs[stg["i"] % len(engs)]
                if eng in ("pool", "dve"):
                    cp(A.a16(dst_off + c0, [[1, n]], 0, np_), A.a32(so, [[1, n]], 0, np_), ["stg%d" % i], [wkey], eng=eng)
                else:
                    act(A.a16(dst_off + c0, [[1, n]], 0, np_), A.a32(so, [[1, n]], 0, np_), AF.Copy, ["stg%d" % i], [wkey])

        o_cst = A.f32(512)
        o_maskb = A.b16(2048 + 128)
        o_small = A.f32(512)
        o_biasK = A.f32(512)
        o_ucar = A.f32(8)
        o_b1b = A.b16(8)
        o_b2b = A.b16(8)
        small_mark = A.off
        o_yaT = A.b16(4 * NOWN)
        mark_ya = A.off
        o_zT = A.f32(S)
        persist_mark = A.off

        tri = A.a32(o_cst, [[1, 128]])
        ones = A.a32(o_cst + 128, [[1, 128]])
        ident = A.a32(o_cst + 256, [[1, 128]])
        blockones = A.a32(o_cst + 384, [[1, 128]])
        identb = A.a16(o_maskb + 2048, [[1, 128]])

        def small(c0, n=1, p0=0, np_=128):
            return A.a32(o_small + c0, [[1, n]], p0, np_)
        C_MOD = 0
        C_A1 = 48
        C_A2 = 56
        C_CB = 64
        C_CBV = 88
        C_CF = 96
        C_QS = 97
        C_G1 = 98
        C_G2 = 106
        C_BADA = 114
        C_CT = 162
        C_CVA = 170
        C_CVF = 182
        C_FLAG = 314
        C_KMASK = 315
        C_CBG = 379
        C_CB2 = 395
        C_CBW = 439
        C_BF = 483
        C_GQ = 484
        C_GK = 485
        C_TMP = 486
        C_FREF = 494
        C_FRB = 495

        dma(A.a32(o_cst, [[1, 512]]), cst[:, :], (), ["cst"])
        dma(A.a16(o_maskb, [[1, 2176]]), maskb_d[:, :], (), ["maskb"])
        dma(small(C_BADA, 48), b_adaT[:, :], (), ["bada"])
        dma(small(C_CT, 8), cT[:, :], (), ["ct"])
        dma(small(C_G1, 8), g1T[:, :], (), ["g1"])
        dma(small(C_G2, 8), g2T[:, :], (), ["g2"])
        dma(small(C_CVA, 12), convA[:, :], (), ["cva"])
        dma(small(C_CVF, 132), convF[:, :], (), ["cvf"])
        dma(small(C_FLAG, 1), flag[:, :], (), ["flag"])
        dma(small(C_KMASK, 64), kmask[:, :], (), ["kmask"])
        dma(small(C_BF, 1, 0, 8), bfT[:, :], (), ["bf"])
        dma(small(C_GQ, 1), gqT[:, :], (), ["gq"])
        dma(small(C_GK, 1), gkT[:, :], (), ["gk"])

        o_w1 = A.b16(KC * W1C)
        o_scb = A.b16(8)
        o_wp = [A.b16(8 * 1024) for _ in range(2)]
        stg["offs"] = [A.f32(1024) for _ in range(4)]
        scb = A.a16(o_scb, [[1, 8]])
        act(scb, small(C_CT, 8), AF.Silu, ["ct"], ["scb"])
        for pj in range(6):
            b = pj % 2
            for k in range(KC):
                load_w(o_wp[b] + k * 1024,
                       lambda c0, n, k=k, pj=pj: w_ada[k * 128:(k + 1) * 128, pj * 1024 + c0:pj * 1024 + c0 + n],
                       1024, "wp%d_%d" % (b, k), engs=("dve", "act"))
            for jj in range(8):
                j = pj * 8 + jj
                for k in range(KC):
                    mm(P(7, 1, c0=j), A.a16(o_wp[b] + k * 1024 + jj * 128, [[1, 128]]),
                       A.a16(o_scb + k, [[1, 1]]), k == 0, k == KC - 1,
                       ["wp%d_%d" % (b, k), "scb"], ["ps7"])
        tt(small(C_MOD, 48), P(7, 48), small(C_BADA, 48), ALU.add, ["ps7", "bada"], ["mod"])
        stt(small(C_A1, 8), small(C_MOD + 8, 8), 1.0, small(C_G1, 8), ALU.add, ALU.mult, ["mod", "g1"], ["A1t"])
        ts(small(C_A1, 8), small(C_A1, 8), 32.0, ALU.mult, ["A1t"], ["A1"])
        stt(small(C_A2, 8), small(C_MOD + 32, 8), 1.0, small(C_G2, 8), ALU.add, ALU.mult, ["mod", "g2"], ["A2t"])
        ts(small(C_A2, 8), small(C_A2, 8), 32.0, ALU.mult, ["A2t"], ["A2"])
        cp(A.a16(o_b1b, [[1, 8]]), small(C_MOD + 0, 8), ["mod"], ["b1b"])
        cp(A.a16(o_b2b, [[1, 8]]), small(C_MOD + 24, 8), ["mod"], ["b2b"])
        stt(small(C_QS), small(C_GQ), 8.0, small(C_GK), ALU.mult, ALU.mult, ["gq", "gk"], ["qs"])
        for k in range(KC):
            load_w(o_w1 + k * W1C, lambda c0, n, k=k: w_in[k * 128:(k + 1) * 128, c0:c0 + n], W1C, "w1_%d" % k,
                   engs=("dve", "act"))
        sc.barrier()
        A.off = persist_mark

        o_w1 = A.b16(KC * W1C)
        o_xg = [A.f32(KC * 512) for _ in range(3)]
        o_sq = [A.f32(512) for _ in range(2)]
        o_acc = A.f32(512)
        o_hn = [A.b16(KC * 512) for _ in range(2)]
        o_inv = A.f32(512)
        o_rt = A.f32(512)
        o_rt2 = A.f32(512)
        o_kb = [A.f32(512) for _ in range(2)]
        o_ksq = [A.f32(512) for _ in range(2)]
        o_invk = A.f32(512)
        o_kst = [A.b16(512) for _ in range(2)]
        o_vst = A.b16(8 * 4 * 65)
        o_ccs = A.f32(512)
        o_u = A.f32(516)
        o_y = A.f32(512)

        def w1(k, c0, n):
            return A.a16(o_w1 + k * W1C + c0, [[1, n]])

        W1K = ["w1_%d" % k for k in range(KC)]
        b1b = A.a16(o_b1b, [[1, 8]])
        for j in range(24):
            if 20 <= j < 24:
                continue
            for k in range(KC):
                mm(P(7, 1, c0=j), w1(k, j * 128, 128), A.a16(o_b1b + k, [[1, 1]]), k == 0, k == KC - 1,
                   W1K + ["b1b"], ["ps7"])
        for h in range(8):
            for k in range(KC):
                mm(P(7, 1, 0, 64, c0=24 + h), w1(k, 2560 + h * 64, 64), A.a16(o_b1b + k, [[1, 1]]),
                   k == 0, k == KC - 1, W1K + ["b1b"], ["ps7"])
        for k in range(KC):
            mm(P(7, 1, 0, 8, c0=32), w1(k, 3072, 8), A.a16(o_b1b + k, [[1, 1]]), k == 0, k == KC - 1,
               W1K + ["b1b"], ["ps7"])
        cp(small(C_CB, 20), P(7, 20), ["ps7"], ["cb"])
        cp(small(C_CBV, 8, 0, 64), P(7, 8, 0, 64, c0=24), ["ps7"], ["cbv"])
        tt(small(C_CF, 1, 0, 8), P(7, 1, 0, 8, c0=32), small(C_BF, 1, 0, 8), ALU.add, ["ps7", "bf"], ["cf"])
        memset(A.a32(o_u, [[1, 2]]), 0.0, ["u"])
        memset(A.a16(o_vst, [[1, 8 * 4 * 65]]), 1.0, ["vst"])
        memset(small(C_TMP), EPS * D, ["tmpc"], eng="dve")
        memset(small(C_TMP + 1), EPS * 64, ["tmpc"], eng="dve")
        memset(small(C_TMP + 2), 1.0, ["tmpc"], eng="dve")

        def load_group(src_fn, n, xb):
            for k in range(KC):
                dma(A.a32(o_xg[xb] + k * 512, [[1, n]]), src_fn(k), (), ["xg%d_%d" % (xb, k)])

        def norm_group(src_fn, n, Acol, gi, tag, nbuf=2, xb=None, hb=None, do_load=True):
            b = gi % nbuf if xb is None else xb
            hb = b if hb is None else hb
            if do_load:
                load_group(src_fn, n, b)
            for k in range(KC):
                if k == 0:
                    act(A.a32(o_acc, [[1, n]]), A.a32(o_xg[b] + k * 512, [[1, n]]), AF.Square,
                        ["xg%d_%d" % (b, k)], ["acc"])
                else:
                    sb = k % 2
                    act(A.a32(o_sq[sb], [[1, n]]), A.a32(o_xg[b] + k * 512, [[1, n]]), AF.Square,
                        ["xg%d_%d" % (b, k)], ["sq%d" % sb])
                    tt(A.a32(o_acc, [[1, n]]), A.a32(o_acc, [[1, n]]), A.a32(o_sq[sb], [[1, n]]), ALU.add,
                       ["acc", "sq%d" % sb], ["acc"], eng="dve")
            mm(P(2, n), ones, A.a32(o_acc, [[1, n]]), True, True, ["acc", "cst"], ["ps2"])
            act(A.a32(o_rt, [[1, n]]), P(2, n), AF.Ln, ["ps2", "tmpc"], ["rt"], bias=small(C_TMP), scale=1.0)
            act(A.a32(o_inv, [[1, n]]), A.a32(o_rt, [[1, n]]), AF.Exp, ["rt"], ["inv"], scale=-0.5)
            for k in range(KC):
                stt(A.a16(o_hn[hb] + k * 512, [[1, n]]), A.a32(o_xg[b] + k * 512, [[1, n]]),
                    small(Acol + k), A.a32(o_inv, [[1, n]]), ALU.mult, ALU.mult,
                    ["xg%d_%d" % (b, k), "inv", "A1", "A2"], ["hn%d" % hb])
            return b

        hstate = {"i": 0, "pending": None}

        def headnorm_p1(psi, n, cbcol):
            i = hstate["i"] % 2
            hstate["i"] += 1
            act(A.a32(o_kb[i], [[1, n]]), P(psi, n), AF.Identity, ["ps%d" % psi, "cb"], ["kb%d" % i], bias=small(cbcol), scale=1.0)
            act(A.a32(o_ksq[i], [[1, n]]), P(psi, n), AF.Square, ["ps%d" % psi, "cb"], ["ksq%d" % i], bias=small(cbcol), scale=1.0)
            return i

        def headnorm_p2(i, n, scale_ap, kbuf, dst_fn):
            mm(P(3, n), blockones, A.a32(o_ksq[i], [[1, n]]), True, True, ["ksq%d" % i, "cst"], ["ps3"])
            act(A.a32(o_rt2, [[1, n]]), P(3, n), AF.Ln, ["ps3", "tmpc"], ["rt2"], bias=small(C_TMP + 1), scale=1.0)
            act(A.a32(o_invk, [[1, n]]), A.a32(o_rt2, [[1, n]]), AF.Exp, ["rt2"], ["invk"], scale=-0.5)
            out16 = A.a16(o_kst[kbuf], [[1, n]])
            if scale_ap is None:
                tt(out16, A.a32(o_kb[i], [[1, n]]), A.a32(o_invk, [[1, n]]), ALU.mult, ["kb%d" % i, "invk"], ["kst%d" % kbuf])
            else:
                stt(out16, A.a32(o_kb[i], [[1, n]]), scale_ap, A.a32(o_invk, [[1, n]]), ALU.mult, ALU.mult,
                    ["kb%d" % i, "invk", "qs"], ["kst%d" % kbuf])
            dst_fn(out16, kbuf)

        def flush_pending():
            if hstate["pending"] is not None:
                hstate["pending"]()
                hstate["pending"] = None

        groups = ([(c, n, False, 0) for (c, n) in PRE_GROUPS] +
                  [(HALO0 + c, n, True, c) for (c, n) in OWN_GROUPS])
        psr = 0
        PA_BANKS = [0, 1, 4, 5, 6]
        kcount = 0

        def gsrc(gi):
            col0, n, own, oc0 = groups[gi]
            return (lambda k: xT[k * 128:(k + 1) * 128, col0:col0 + n]), n

        def gload(gi):
            f_, n_ = gsrc(gi)
            load_group(f_, n_, gi % 3)

        def stage1(gi):
            f_, n_ = gsrc(gi)
            norm_group(f_, n_, C_A1, gi, "a", xb=gi % 3, hb=gi % 2, do_load=False)

        gload(0)
        gload(1)
        stage1(0)
        for gi, (col0, n, own, oc0) in enumerate(groups):
            if gi + 2 < len(groups):
                gload(gi + 2)
            if gi + 1 < len(groups):
                stage1(gi + 1)
            b = gi % 2
            HN = ["hn%d" % b]

            def hn(k, c0=0, nn=None):
                return A.a16(o_hn[b] + k * 512 + c0, [[1, nn if nn is not None else n]])

            def proj(psi, wc0, wn, np_=128):
                for k in range(KC):
                    mm(P(psi, n, 0, np_), w1(k, wc0, wn), hn(k), k == 0, k == KC - 1, W1K + HN, ["ps%d" % psi])
            for c in range(4):
                psi = PA_BANKS[psr % 5]
                psr += 1
                proj(psi, 2048 + c * 128, 128)
                i = headnorm_p1(psi, n, C_CB + 16 + c)
                flush_pending()
                kbuf = kcount % 2
                kcount += 1

                def fin(i=i, n=n, kbuf=kbuf, c=c, col0=col0):
                    headnorm_p2(i, n, None, kbuf,
                                lambda o16, kb_: dma(kT_d[c * 128:(c + 1) * 128, col0:col0 + n], o16,
                                                     ["kst%d" % kb_], ["kT_d"]))
                hstate["pending"] = fin
            nt = n // 128
            for t in range(nt):
                psi = PA_BANKS[psr % 5]
                psr += 1
                for k in range(KC):
                    mm(P(psi, 512), hn(k, t * 128, 128), w1(k, 2560, 512), k == 0, k == KC - 1, W1K + HN, ["ps%d" % psi])
                if t == 0:
                    flush_pending()
                sc.add("act", lambda e, psi=psi, t=t: e.activation(
                    A.a16(o_vst + t * 65, [[260, 8], [1, 64]]),
                    bass.AP(ps[psi], 0, [[512, 128], [64, 8], [1, 64]]), AF.Copy), ["ps%d" % psi], ["vst"])
            dma(bass.AP(v_w, (col0 // 128) * 65, [[NT * 65, 128], [128 * NT * 65, 8], [1, nt * 65]]),
                A.a16(o_vst, [[260, 8], [1, nt * 65]]), ["vst"], ["v_d"])
            psi = PA_BANKS[psr % 5]
            psr += 1
            proj(psi, 3072, 8, np_=8)
            act(A.a32(o_zT + col0, [[1, n]], 0, 8), P(psi, n, 0, 8), AF.Identity, ["ps%d" % psi, "cf"], ["zT"],
                bias=small(C_CF, 1, 0, 8), scale=1.0)
            if own:
                for c in range(4):
                    psi = PA_BANKS[psr % 5]
                    psr += 1
                    proj(psi, 1536 + c * 128, 128)
                    i = headnorm_p1(psi, n, C_CB + 12 + c)
                    flush_pending()
                    kbuf = kcount % 2
                    kcount += 1

                    def fin(i=i, n=n, kbuf=kbuf, c=c, oc0=oc0):
                        headnorm_p2(i, n, small(C_QS), kbuf,
                                    lambda o16, kb_: dma(qT_d[c * 128:(c + 1) * 128, oc0:oc0 + n], o16,
                                                         ["kst%d" % kb_], ["qT_d"]))
                    hstate["pending"] = fin
                for c in range(4):
                    p_cc = PA_BANKS[psr % 5]
                    psr += 1
                    proj(p_cc, 512 + c * 128, 128)
                    if c == 0:
                        flush_pending()
                    act(A.a32(o_ccs, [[1, n]]), P(p_cc, n), AF.Identity, ["ps%d" % p_cc, "cb"], ["ccs"],
                        bias=small(C_CB + 4 + c), scale=1.0)
                    p_cv = PA_BANKS[psr % 5]
                    psr += 1
                    proj(p_cv, 1024 + c * 128, 128)
                    stt(A.a32(o_u + 2, [[1, n]]), P(p_cv, n), small(C_CB + 8 + c), A.a32(o_ccs, [[1, n]]),
                        ALU.add, ALU.mult, ["ps%d" % p_cv, "ccs", "cb", "ucar%d" % c], ["u"])
                    if oc0 == 0:
                        memset(A.a32(o_u, [[1, 2]]), 0.0, ["u"], eng="dve")
                        ts(A.a32(o_u + 2, [[1, n]]), A.a32(o_u + 2, [[1, n]]), small(C_FLAG), ALU.mult, ["u", "flag"], ["u"])
                    else:
                        cp(A.a32(o_u, [[1, 2]]), A.a32(o_ucar + c * 2, [[1, 2]]), ["ucar%d" % c], ["u"])
                    ts(A.a32(o_y, [[1, n]]), A.a32(o_u + 2, [[1, n]]), small(C_CVA + c * 3 + 2), ALU.mult, ["u", "cva"], ["y"])
                    stt(A.a32(o_y, [[1, n]]), A.a32(o_u + 1, [[1, n]]), small(C_CVA + c * 3 + 1), A.a32(o_y, [[1, n]]),
                        ALU.mult, ALU.add, ["u", "y", "cva"], ["y"])
                    stt(A.a32(o_y, [[1, n]]), A.a32(o_u, [[1, n]]), small(C_CVA + c * 3 + 0), A.a32(o_y, [[1, n]]),
                        ALU.mult, ALU.add, ["u", "y", "cva"], ["y"])
                    cp(A.a32(o_ucar + c * 2, [[1, 2]]), A.a32(o_u + n, [[1, 2]]), ["u"], ["ucar%d" % c])
                    p_cb = PA_BANKS[psr % 5]
                    psr += 1
                    proj(p_cb, c * 128, 128)
                    stt(A.a16(o_yaT + c * NOWN + oc0, [[1, n]]), P(p_cb, n), small(C_CB + c), A.a32(o_y, [[1, n]]),
                        ALU.add, ALU.mult, ["ps%d" % p_cb, "y", "cb"], ["yaT"])
        flush_pending()
        sc.barrier()
        A.off = persist_mark

        o_e = A.f32(2048)
        o_val = A.f32(2048)
        o_one8 = A.f32(2048)
        o_fqf = A.f32(NOWN)
        o_fr = A.f32(NOWN)
        o_fq16 = A.b16(3 * NOWN)
        o_diag = A.f32(8)

        def r8(off, n, c0=0):
            return A.a32(off + c0, [[1, n]], 0, 8)

        memset(r8(o_one8, 2048), 1.0, ["one8"], eng="dve")
        for pc in range(4):
            c0 = pc * 2048
            dma(r8(o_val, 2048), validT[:, c0:c0 + 2048], (), ["val"])
            act(r8(o_e, 2048), r8(o_zT, 2048, c0), AF.Exp, ["zT"], ["e"], scale=-1.0)
            act(r8(o_e, 2048), r8(o_e, 2048), AF.Ln, ["e", "tmpc"], ["e"], bias=small(C_TMP + 2, 1, 0, 8), scale=1.0)
            stt(r8(o_e, 2048), r8(o_e, 2048), -1.0, r8(o_val, 2048), ALU.mult, ALU.mult, ["e", "val"], ["e"])
            init = 0.0 if pc == 0 else r8(o_zT, 1, c0 - 1)
            sc.add("dve", lambda e, c0=c0, init=init: e.tensor_tensor_scan(
                r8(o_zT, 2048, c0), r8(o_one8, 2048), r8(o_e, 2048), init, ALU.mult, ALU.add),
                ["e", "one8", "zT"], ["zT"])
        cp(small(C_FREF, 1, 0, 8), r8(o_zT, 1, S - 1), ["zT"], ["fref"])
        ts(r8(o_fqf, NOWN), r8(o_zT, NOWN, HALO0), small(C_FREF, 1, 0, 8), ALU.subtract, ["zT", "fref"], ["fqf"])
        cp(A.a16(o_fq16, [[1, NOWN]], 0, 8), r8(o_fqf, NOWN), ["fqf"], ["fq16a"])
        tt(r8(o_fr, NOWN), r8(o_fqf, NOWN), A.a16(o_fq16, [[1, NOWN]], 0, 8), ALU.subtract, ["fqf", "fq16a"], ["fr"])
        cp(A.a16(o_fq16 + NOWN, [[1, NOWN]], 0, 8), r8(o_fr, NOWN), ["fr"], ["fq16b"])
        tt(r8(o_fqf, NOWN), r8(o_fr, NOWN), A.a16(o_fq16 + NOWN, [[1, NOWN]], 0, 8), ALU.subtract, ["fr", "fq16b"], ["fqf"])
        cp(A.a16(o_fq16 + 2 * NOWN, [[1, NOWN]], 0, 8), r8(o_fqf, NOWN), ["fqf"], ["fq16c"])
        dma(fq_d[:, :], A.a16(o_fq16, [[1, 3 * NOWN]], 0, 8), ["fq16a", "fq16b", "fq16c"], ["fq_d"])
        for t in range(NT):
            mm(P(6, 8, c0=t * 8), r8(o_zT, 128, t * 128), A.a32(o_cst + 256, [[1, 8]], 0, 8), True, True,
               ["zT", "cst"], ["ps6"])
        ts(A.a32(o_diag, [[1, 8]], 0, 8), A.a32(o_cst + 256, [[1, 8]], 0, 8), small(C_FREF, 1, 0, 8), ALU.mult,
           ["cst", "fref"], ["diag"])
        mm(P(7, 8), A.a32(o_cst + 128, [[1, 128]], 0, 8), A.a32(o_diag, [[1, 8]], 0, 8), True, True, ["diag", "cst"], ["ps7"])
        cp(small(C_FRB, 8), P(7, 8), ["ps7"], ["frb"])
        tt(A.a32(o_biasK, [[8, NT], [1, 8]]), A.a32(o_small + C_FRB, [[0, NT], [1, 8]]),
           bass.AP(ps[6], 0, [[512, 128], [8, NT], [1, 8]]), ALU.subtract, ["frb", "ps6"], ["biasK"])
        tt(A.a32(o_biasK, [[8, NT], [1, 8]]), A.a32(o_biasK, [[8, NT], [1, 8]]),
           A.a32(o_small + C_KMASK, [[1, NT], [0, 8]]), ALU.add, ["biasK", "kmask"], ["biasK"])
        sc.barrier()
        A.off = mark_ya
        o_ybT = A.b16(8 * NOWN)
        persist_mark = A.off

        o_wg = A.b16(KC * 2048)
        o_wo = A.b16(KC * 1024)
        mark_c = A.off
        stg["offs"] = [A.f32(1024) for _ in range(2)]
        o_k = [A.b16(S) for _ in range(2)]
        o_v = [A.b16(NT * 65) for _ in range(2)]
        o_q = [A.b16(NOWN) for _ in range(2)]
        NPT = 4
        o_pT = [A.b16(512) for _ in range(NPT)]
        o_rec = A.f32(512)
        o_osb = A.f32(512)
        o_tmp = A.f32(512)
        for b in range(2):
            memset(A.a16(o_k[b], [[1, S]], 64, 3), 1.0, ["k%d" % b], eng="pool")

        def load_head(h):
            b = h % 2
            KB, VB, QB = "k%d" % b, "v%d" % b, "q%d" % b
            for half in range(2):
                dma(A.a16(o_k[b] + half * 4096, [[1, 4096]], 0, 64),
                    kT_d[h * 64:(h + 1) * 64, half * 4096:(half + 1) * 4096], ["kT_d"], [KB])
            dma(A.a16(o_v[b], [[1, NT * 65]]), v_d[h * 128:(h + 1) * 128, :], ["v_d"], [VB])
            dma(A.a16(o_q[b], [[1, NOWN]], 0, 64), qT_d[h * 64:(h + 1) * 64, :], ["qT_d"], [QB])
            dma(A.a16(o_q[b], [[1, NOWN]], 64, 3),
                bass.AP(fq_w, h * 3 * NOWN, [[NOWN, 3], [1, NOWN]]), ["fq_d"], [QB])

        units = []
        blk = 0
        for h in range(8):
            for (q0, nq) in OWN_GROUPS:
                qt0 = 47 + q0 // 128
                nvis = 47 + (q0 + nq) // 128
                for kt in range(nvis):
                    units.append((h, q0, nq, kt, kt == 0, kt == nvis - 1, kt >= qt0, kt - qt0, blk))
                blk += 1

        def emit_S(idx):
            h, q0, nq, kt, first, last, partial, jm, blk = units[idx]
            b = h % 2
            KB, QB = "k%d" % b, "q%d" % b
            sb = 3 + idx % 3
            pb = idx % NPT
            mm(P(sb, nq), A.a16(o_k[b] + kt * 128, [[1, 128]], 0, 67), A.a16(o_q[b] + q0, [[1, nq]], 0, 67),
               True, not partial, [KB, QB], ["ps%d" % sb])
            if partial:
                mm(P(sb, nq), identb, A.a16(o_maskb + jm * 512, [[1, nq]]), False, True, ["maskb"], ["ps%d" % sb])
            act(A.a16(o_pT[pb], [[1, nq]]), P(sb, nq), AF.Exp, ["ps%d" % sb, "biasK"], ["pT%d" % pb],
                bias=A.a32(o_biasK + kt * 8 + h, [[1, 1]]), scale=1.0)

        def emit_PV(idx):
            h, q0, nq, kt, first, last, partial, jm, blk = units[idx]
            b = h % 2
            VB = "v%d" % b
            pb = idx % NPT
            ob = 6 + blk % 2
            OK_ = "ps%d" % ob
            mm(P(ob, nq, 0, 65), A.a16(o_v[b] + kt * 65, [[1, 65]]), A.a16(o_pT[pb], [[1, nq]]),
               first, last, [VB, "pT%d" % pb], [OK_])
            if last:
                ts(A.a32(o_rec, [[1, nq]], 64, 1), P(ob, nq, 64, 1), 1e-30, ALU.add, [OK_], ["rec"])
                recip(A.a32(o_rec, [[1, nq]], 64, 1), A.a32(o_rec, [[1, nq]], 64, 1), ["rec"], ["rec"])
                mm(P(2, nq, 0, 64), A.a32(o_cst + 128, [[1, 64]], 64, 1), A.a32(o_rec, [[1, nq]], 64, 1), True, True,
                   ["rec", "cst"], ["ps2"])
                cp(A.a32(o_osb, [[1, nq]], 0, 64), P(ob, nq, 0, 64), [OK_], ["osb"])
                tt(A.a32(o_tmp, [[1, nq]], 0, 64), A.a32(o_osb, [[1, nq]], 0, 64), P(2, nq, 0, 64), ALU.mult,
                   ["osb", "ps2"], ["tmp"])
                ts(A.a16(o_ybT + h * NOWN + q0, [[1, nq]], 0, 64), A.a32(o_tmp, [[1, nq]], 0, 64),
                   small(C_CBV + h, 1, 0, 64), ALU.add, ["tmp", "cbv"], ["ybT"])

        LA = 2
        load_head(0)
        load_head(1)
        for k in range(KC):
            load_w(o_wg + k * 2048, lambda c0, n, k=k: w_in[k * 128:(k + 1) * 128, 3080 + c0:3080 + c0 + n], 2048, "wg",
                   engs=("pool",))
            load_w(o_wo + k * 1024, lambda c0, n, k=k: w_o[k * 128:(k + 1) * 128, c0:c0 + n], 1024, "wo", engs=("pool",))
        for idx in range(len(units) + LA):
            if idx < len(units):
                u = units[idx]
                if u[3] == LA + 2 and u[1] == 0 and 1 <= u[0] and u[0] + 1 < 8:
                    load_head(u[0] + 1)
                emit_S(idx)
            if idx - LA >= 0:
                emit_PV(idx - LA)
        sc.barrier()
        A.off = mark_c

        o_wa = A.b16(4 * 1024)
        o_wb = A.b16(8 * 1024)
        o_xg = [A.f32(KC * 512)]
        o_sq = [A.f32(512) for _ in range(2)]
        o_acc = A.f32(512)
        o_hn = [A.b16(KC * 512)]
        o_inv = A.f32(512)
        o_rt = A.f32(512)
        o_sga = A.f32(512)
        o_sgb = A.f32(512)
        o_m1 = A.f32(512)
        o_m2 = A.f32(512)
        o_mg = A.b16(KC * 512)
        stg["offs"] = [o_xg[0], o_xg[0] + 1024, o_xg[0] + 2048]
        for c in range(4):
            load_w(o_wa + c * 1024, lambda c0, n, c=c: w_a[c * 128:(c + 1) * 128, c0:c0 + n], 1024, "wa", engs=("dve", "act"))
        for h in range(8):
            load_w(o_wb + h * 1024, lambda c0, n, h=h: w_b[h * 64:(h + 1) * 64, c0:c0 + n], 1024, "wb", np_=64, engs=("dve", "act"))
        sc.barrier()
        for j in range(16):
            for k in range(KC):
                mm(P(7, 1, c0=j), A.a16(o_wg + k * 2048 + j * 128, [[1, 128]]), A.a16(o_b1b + k, [[1, 1]]),
                   k == 0, k == KC - 1, ["wg", "b1b"], ["ps7"])
        cp(small(C_CBG, 16), P(7, 16), ["ps7"], ["cbg"])
        for gi2, (oc0, n) in enumerate(OWN_GROUPS):
            col0 = HALO0 + oc0
            norm_group(lambda k: xT[k * 128:(k + 1) * 128, col0:col0 + n], n, C_A1, 0, "c", nbuf=1)
            for e_ in range(KC):
                pa_, pb_ = [0, 6][e_ % 2], [1, 7][e_ % 2]
                for k in range(KC):
                    mm(P(pa_, n), A.a16(o_wg + k * 2048 + e_ * 128, [[1, 128]]), A.a16(o_hn[0] + k * 512, [[1, n]]),
                       k == 0, k == KC - 1, ["wg", "hn0"], ["ps%d" % pa_])
                act(A.a32(o_sga, [[1, n]]), P(pa_, n), AF.Sigmoid, ["ps%d" % pa_, "cbg"], ["sga"], bias=small(C_CBG + e_), scale=1.0)
                for k in range(KC):
                    mm(P(pb_, n), A.a16(o_wg + k * 2048 + 1024 + e_ * 128, [[1, 128]]), A.a16(o_hn[0] + k * 512, [[1, n]]),
                       k == 0, k == KC - 1, ["wg", "hn0"], ["ps%d" % pb_])
                act(A.a32(o_sgb, [[1, n]]), P(pb_, n), AF.Sigmoid, ["ps%d" % pb_, "cbg"], ["sgb"], bias=small(C_CBG + 8 + e_), scale=1.0)
                for c in range(4):
                    mm(P(4, n), A.a16(o_wa + c * 1024 + e_ * 128, [[1, 128]]), A.a16(o_yaT + c * NOWN + oc0, [[1, n]]),
                       c == 0, c == 3, ["wa", "yaT"], ["ps4"])
                tt(A.a32(o_m1, [[1, n]]), A.a32(o_sga, [[1, n]]), P(4, n), ALU.mult, ["sga", "ps4"], ["m1"])
                for h in range(8):
                    mm(P(5, n), A.a16(o_wb + h * 1024 + e_ * 128, [[1, 128]], 0, 64),
                       A.a16(o_ybT + h * NOWN + oc0, [[1, n]], 0, 64), h == 0, h == 7, ["wb", "ybT"], ["ps5"])
                tt(A.a32(o_m2, [[1, n]]), A.a32(o_sgb, [[1, n]]), P(5, n), ALU.mult, ["sgb", "ps5"], ["m2"])
                tt(A.a16(o_mg + e_ * 512, [[1, n]]), A.a32(o_m1, [[1, n]]), A.a32(o_m2, [[1, n]]), ALU.add,
                   ["m1", "m2"], ["mg%d" % e_], eng="pool")
            MG = ["mg%d" % i for i in range(KC)]
            for e_ in range(KC):
                psi = 2 + e_ % 2
                for k in range(KC):
                    mm(P(psi, n), A.a16(o_wo + k * 1024 + e_ * 128, [[1, 128]]), A.a16(o_mg + k * 512, [[1, n]]),
                       k == 0, k == KC - 1, ["wo"] + MG, ["ps%d" % psi])
                stt(A.a32(o_xg[0] + e_ * 512, [[1, n]]), P(psi, n), small(C_MOD + 16 + e_),
                    A.a32(o_xg[0] + e_ * 512, [[1, n]]), ALU.mult, ALU.add,
                    ["ps%d" % psi, "mod", "xg0_%d" % e_], ["xg0_%d" % e_])
                dma(x1_d[e_ * 128:(e_ + 1) * 128, oc0:oc0 + n], A.a32(o_xg[0] + e_ * 512, [[1, n]]),
                    ["xg0_%d" % e_], ["x1_d"])
        sc.barrier()
        A.off = small_mark

        NF = 2 * DFF
        o_wup = A.b16(KC * NF)
        o_wdn = A.b16(22 * 1024)
        o_xg = [A.f32(KC * 512)]
        o_sq = [A.f32(512) for _ in range(2)]
        o_acc = A.f32(512)
        o_hn = [A.b16(KC * 512)]
        o_inv = A.f32(512)
        o_rt = A.f32(512)
        o_yg = A.f32(512)
        o_yv = A.f32(512)
        o_sg = A.f32(512)
        o_act = A.b16(22 * 512)
        o_ws = A.f32(44)
        stg["offs"] = [o_xg[0], o_xg[0] + 1024, o_xg[0] + 2048]
        for k in range(KC):
            load_w(o_wup + k * NF, lambda c0, n, k=k: w_up[k * 128:(k + 1) * 128, c0:c0 + n], NF, "wup", engs=("dve", "act"))
        for c in range(22):
            load_w(o_wdn + c * 1024, lambda c0, n, c=c: w_dn[c * 128:(c + 1) * 128, c0:c0 + n], 1024, "wdn", engs=("dve", "act"))
        sc.barrier()
        for fc in range(44):
            for k in range(KC):
                mm(P(7, 1, c0=fc), A.a16(o_wup + k * NF + fc * 128, [[1, 128]]), A.a16(o_b2b + k, [[1, 1]]),
                   k == 0, k == KC - 1, ["wup", "b2b"], ["ps7"])
        cp(small(C_CB2, 44), P(7, 44), ["ps7"], ["cb2"])
        tt(A.a32(o_ws, [[1, 44]]), A.a32(o_small + C_CVF, [[3, 44]]), A.a32(o_small + C_CVF + 1, [[3, 44]]), ALU.add,
           ["cvf"], ["ws"])
        tt(A.a32(o_ws, [[1, 44]]), A.a32(o_ws, [[1, 44]]), A.a32(o_small + C_CVF + 2, [[3, 44]]), ALU.add, ["ws", "cvf"], ["ws"])
        tt(small(C_CBW, 44), small(C_CB2, 44), A.a32(o_ws, [[1, 44]]), ALU.mult, ["ws", "cb2"], ["cbw"])
        ts(small(C_TMP + 3), small(C_FLAG), -1.0, ALU.add, ["flag"], ["fm1"])
        o_tc = A.f32(2)
        pcount = 0
        for g in range(5):
            w0c = 126 + 510 * g
            n = min(512, NOWN - w0c)
            m = n - 2
            norm_group(lambda k: x1_d[k * 128:(k + 1) * 128, w0c:w0c + n], n, C_A2, 0, "d", nbuf=1)
            for c in range(22):
                for which, fc in ((0, c), (1, 22 + c)):
                    psi = [0, 1, 6, 7, 3][pcount % 5]
                    pcount += 1
                    oy = o_yg if which == 0 else o_yv
                    YK = "yg" if which == 0 else "yv"
                    PK = "ps%d" % psi
                    for k in range(KC):
                        mm(P(psi, n), A.a16(o_wup + k * NF + fc * 128, [[1, 128]]), A.a16(o_hn[0] + k * 512, [[1, n]]),
                           k == 0, k == KC - 1, ["wup", "hn0"], [PK])
                    act(A.a32(oy, [[1, m]]), P(psi, m, c0=2), AF.Identity, [PK, "cvf", "cbw"], [YK],
                        bias=small(C_CBW + fc), scale=small(C_CVF + fc * 3 + 2))
                    stt(A.a32(oy, [[1, m]]), P(psi, m, c0=1), small(C_CVF + fc * 3 + 1), A.a32(oy, [[1, m]]),
                        ALU.mult, ALU.add, [PK, YK, "cvf"], [YK])
                    stt(A.a32(oy, [[1, m]]), P(psi, m, c0=0), small(C_CVF + fc * 3 + 0), A.a32(oy, [[1, m]]),
                        ALU.mult, ALU.add, [PK, YK, "cvf"], [YK])
                    if g == 0:
                        ts(A.a32(o_tc, [[1, 2]]), P(psi, 2), small(C_CB2 + fc), ALU.add, [PK, "cb2", "fm1"], ["tc"],
                           s2=small(C_TMP + 3), op1=ALU.mult)
                        stt(A.a32(oy, [[1, 2]]), A.a32(o_tc, [[1, 2]]), small(C_CVF + fc * 3 + 0), A.a32(oy, [[1, 2]]),
                            ALU.mult, ALU.add, ["tc", YK, "cvf"], [YK])
                        stt(A.a32(oy, [[1, 1]]), A.a32(o_tc + 1, [[1, 1]]), small(C_CVF + fc * 3 + 1), A.a32(oy, [[1, 1]]),
                            ALU.mult, ALU.add, ["tc", YK, "cvf"], [YK])
                act(A.a32(o_sg, [[1, m]]), A.a32(o_yg, [[1, m]]), AF.Silu, ["yg"], ["sg"])
                tt(A.a16(o_act + c * 512, [[1, m]]), A.a32(o_sg, [[1, m]]), A.a32(o_yv, [[1, m]]), ALU.mult,
                   ["sg", "yv"], ["act%d" % c], eng="pool")
            ACTK = ["act%d" % c for c in range(22)]
            for e_ in range(KC):
                psi = 4 + e_ % 2
                for c in range(22):
                    mm(P(psi, m), A.a16(o_wdn + c * 1024 + e_ * 128, [[1, 128]]), A.a16(o_act + c * 512, [[1, m]]),
                       c == 0, c == 21, ["wdn"] + ACTK, ["ps%d" % psi])
                stt(A.a32(o_xg[0] + e_ * 512 + 2, [[1, m]]), P(psi, m), small(C_MOD + 40 + e_),
                    A.a32(o_xg[0] + e_ * 512 + 2, [[1, m]]), ALU.mult, ALU.add,
                    ["ps%d" % psi, "mod", "xg0_%d" % e_], ["xg0_%d" % e_])
                dma(outT[e_ * 128:(e_ + 1) * 128, w0c + 2 - 128:w0c + n - 128], A.a32(o_xg[0] + e_ * 512 + 2, [[1, m]]),
                    ["xg0_%d" % e_], ["outT"], is_out=True)
        sc.finish()

        with nc.Block() as block:
            @block.tensor
            def _(e):
                sc.emit("pe", e)

            @block.scalar
            def _(e):
                sc.emit("act", e)

            @block.vector
            def _(e):
                sc.emit("dve", e)

            @block.gpsimd
            def _(e):
                sc.emit("pool", e)

            @block.sync
            def _(e):
                sc.emit("sp", e)
    return nc


_NC_CACHE = {}


def _consts():
    p = np.arange(128)
    tri = (p[:, None] <= p[None, :]).astype(np.float32)
    ones = np.ones((128, 128), np.float32)
    ident = np.eye(128, dtype=np.float32)
    blk = ((p[:, None] // 64) == (p[None, :] // 64)).astype(np.float32)
    cst = np.concatenate([tri, ones, ident, blk], axis=1)
    c = np.arange(512)
    masks = []
    for jm in range(4):
        masks.append(np.where(c[None, :] >= 128 * jm + p[:, None], 0.0, NEG).astype(np.float32))
    maskb = np.concatenate(masks + [ident], axis=1).astype(ml_dtypes.bfloat16)
    return cst, maskb


def prep_inputs(x, c, w_ada, b_ada, norm1_g, w_in, b_f, conv_a_w, q_norm_g, k_norm_g,
                w_branch_a, w_branch_b, w_out, norm2_g, w_up, conv_ffn_w, w_down):
    f = lambda a: np.ascontiguousarray(np.asarray(a, dtype=np.float32))
    x = f(x); c = f(c)
    cst, maskb = _consts()
    shared = {
        "w_ada": f(w_ada[0]),
        "b_adaT": f(np.asarray(b_ada[0]).reshape(48, 128).T),
        "g1T": f(np.asarray(norm1_g[0]).reshape(8, 128).T),
        "g2T": f(np.asarray(norm2_g[0]).reshape(8, 128).T),
        "w_in": f(w_in[0]),
        "bfT": f(np.asarray(b_f[0])[:, None]),
        "convA": f(np.asarray(conv_a_w[0]).reshape(3, 4, 128).transpose(2, 1, 0).reshape(128, 12)),
        "gqT": f(np.tile(np.asarray(q_norm_g[0]), 2)[:, None]),
        "gkT": f(np.tile(np.asarray(k_norm_g[0]), 2)[:, None]),
        "w_a": f(w_branch_a[0]),
        "w_b": f(w_branch_b[0]),
        "w_o": f(w_out[0]),
        "w_up": f(w_up[0]),
        "convF": f(np.asarray(conv_ffn_w[0]).reshape(3, 44, 128).transpose(2, 1, 0).reshape(128, 132)),
        "w_dn": f(w_down[0]),
        "cst": cst,
        "maskb": maskb,
    }
    in_maps = []
    pidx = np.arange(128)[:, None] + 128 * np.arange(NT)[None, :]
    for j in range(8):
        b, i = j // 4, j % 4
        own_end = 2048 * (i + 1)
        ws = own_end - S
        xTw = np.zeros((D, S), np.float32)
        t0 = max(ws, 0)
        xTw[:, t0 - ws:] = x[b, t0:own_end, :].T
        valid = ((np.arange(S) + ws) >= 0).astype(np.float32)
        m = dict(shared)
        m["xT"] = xTw
        m["validT"] = np.ascontiguousarray(np.broadcast_to(valid[None, :], (8, S)))
        m["kmask"] = np.where((pidx + ws) >= 0, 0.0, NEG).astype(np.float32)
        m["flag"] = np.full((128, 1), 1.0 if i > 0 else 0.0, np.float32)
        m["cT"] = f(c[b].reshape(8, 128).T)
        in_maps.append(m)
    return in_maps


def kernel(**inputs):
    in_maps = prep_inputs(**inputs)
    if "nc" not in _NC_CACHE:
        _NC_CACHE["nc"] = build_nc()
    nc = _NC_CACHE["nc"]
    res = run_bass_kernel_spmd(nc, in_maps, core_ids=list(range(8)))
    out = np.empty((2, S, D), np.float32)
    for j in range(8):
        b, i = j // 4, j % 4
        out[b, 2048 * i:2048 * (i + 1), :] = np.asarray(res.results[j]["outT"]).T
    return out
```

```python
import numpy as np
import ml_dtypes
import concourse.bass as bass
import concourse.mybir as mybir
from concourse.bass_utils import run_bass_kernel_spmd

F32 = mybir.dt.float32
BF16 = mybir.dt.bfloat16
AF = mybir.ActivationFunctionType
ALU = mybir.AluOpType

D = 1024
KC = 8
S = 8192
NT = 64
HALO0 = 47 * 128
NOWN = 2176
W1C = 3080
DFF = 2816
NEG = -30000.0
EPS = 1e-6
OWN_GROUPS = [(0, 128), (128, 512), (640, 512), (1152, 512), (1664, 512)]
PRE_GROUPS = [(i * 512, 512) for i in range(11)] + [(5632, 384)]

ENGS = ["pe", "act", "dve", "pool", "sp"]
NDSEM = 24


class Sched:
    def __init__(self, nc, sems, dsems):
        self.nc = nc
        self.sems = sems
        self.dsems = dsems
        self.streams = {e: [] for e in ENGS}
        self.cnt = {e: 0 for e in ENGS}
        self.lastw = {}
        self.readers = {}
        self.waited = {e: {} for e in ENGS}
        self.ndma = 0
        self.out_tokens = []

    def semof(self, tok):
        if tok[0] == "c":
            return (("c", tok[1]), tok[2])
        k = tok[1]
        return (("d", k % NDSEM), 16 * (k // NDSEM + 1))

    def _need(self, eng, sv, waits):
        key, val = sv
        if self.waited[eng].get(key, 0) >= val:
            return
        self.waited[eng][key] = val
        waits.append(sv)

    def add(self, eng, fn, reads=(), writes=(), dma=False, is_out=False):
        deps = set()
        for r in reads:
            t = self.lastw.get(r)
            if t is not None:
                deps.add(t)
        for w in writes:
            t = self.lastw.get(w)
            if t is not None:
                deps.add(t)
            for t in self.readers.get(w, ()):
                deps.add(t)
        waits = []
        for tok in deps:
            if tok[0] == "c" and tok[1] == eng and not dma:
                if eng == "pe":
                    continue
                if self.cnt[eng] - tok[2] > 3:
                    continue
            self._need(eng, self.semof(tok), waits)
        if dma:
            k = self.ndma
            self.ndma += 1
            if k >= NDSEM:
                self._need(eng, (("d", k % NDSEM), 16 * (k // NDSEM)), waits)
            tok = ("d", k)
            inc = (("d", k % NDSEM), 16)
            if is_out:
                self.out_tokens.append(tok)
        else:
            self.cnt[eng] += 1
            tok = ("c", eng, self.cnt[eng])
            inc = (("c", eng), 1)
        self.streams[eng].append((waits, fn, inc))
        for r in reads:
            self.readers.setdefault(r, []).append(tok)
        for w in writes:
            self.lastw[w] = tok
            self.readers[w] = []

    def barrier(self):
        for e in ENGS:
            waits = []
            for e2 in ENGS:
                if e2 != e and self.cnt[e2] > 0:
                    self._need(e, (("c", e2), self.cnt[e2]), waits)
            lo = max(0, self.ndma - NDSEM)
            for k in range(lo, self.ndma):
                self._need(e, self.semof(("d", k)), waits)
            if waits:
                self.streams[e].append((waits, None, None))
        self.lastw = {}
        self.readers = {}

    def finish(self):
        waits = []
        for t in self.out_tokens:
            self._need("sp", self.semof(t), waits)
        lo = max(0, self.ndma - NDSEM)
        for k in range(lo, self.ndma):
            self._need("sp", self.semof(("d", k)), waits)
        self.streams["sp"].append((waits, None, None))

    def _sem(self, key):
        return self.sems[key[1]] if key[0] == "c" else self.dsems[key[1]]

    def emit(self, eng, e):
        for waits, fn, inc in self.streams[eng]:
            for key, val in waits:
                e.wait_ge(self._sem(key), val)
            if fn is not None:
                ins = fn(e)
                if inc is not None:
                    ins.then_inc(self._sem(inc[0]), inc[1])


class Arena:
    def __init__(self, t16, t32, nbytes):
        self.t16, self.t32, self.nb = t16, t32, nbytes
        self.off = 0
        self.n16 = nbytes // 2
        self.n32 = nbytes // 4

    def alloc(self, nbytes):
        o = self.off
        self.off = (o + nbytes + 63) // 64 * 64
        assert self.off <= self.nb, ("arena overflow", self.off)
        return o

    def f32(self, cols):
        return self.alloc(cols * 4) // 4

    def b16(self, cols):
        return self.alloc(cols * 2) // 2

    def a32(self, off, dims, p0=0, np_=128):
        return bass.AP(self.t32, p0 * self.n32 + off, [[self.n32, np_]] + [list(d) for d in dims])

    def a16(self, off, dims, p0=0, np_=128):
        return bass.AP(self.t16, p0 * self.n16 + off, [[self.n16, np_]] + [list(d) for d in dims])


def build_nc():
    nc = bass.Bass("TRN2", target_bir_lowering=False)
    dt = nc.dram_tensor

    def din(name, shape, dtype=F32):
        return dt(name, list(shape), dtype, kind="ExternalInput").ap()

    xT = din("xT", [D, S])
    validT = din("validT", [8, S])
    kmask = din("kmask", [128, NT])
    flag = din("flag", [128, 1])
    cT = din("cT", [128, 8])
    w_ada = din("w_ada", [D, 6144])
    b_adaT = din("b_adaT", [128, 48])
    g1T = din("g1T", [128, 8])
    g2T = din("g2T", [128, 8])
    w_in = din("w_in", [D, 5128])
    bfT = din("bfT", [8, 1])
    convA = din("convA", [128, 12])
    gqT = din("gqT", [128, 1])
    gkT = din("gkT", [128, 1])
    w_a = din("w_a", [512, D])
    w_b = din("w_b", [512, D])
    w_o = din("w_o", [D, D])
    w_up = din("w_up", [D, 2 * DFF])
    convF = din("convF", [128, 132])
    w_dn = din("w_dn", [DFF, D])
    cst = din("cst", [128, 512])
    maskb_d = din("maskb", [128, 2048 + 128], BF16)
    outT = dt("outT", [D, 2048], F32, kind="ExternalOutput").ap()
    kT_d = dt("kT_d", [512, S], BF16).ap()
    v_w = dt("v_d", [8 * 128, NT * 65], BF16)
    v_d = v_w.ap()
    qT_d = dt("qT_d", [512, NOWN], BF16).ap()
    fq_w = dt("fq_d", [8, 3 * NOWN], BF16)
    fq_d = fq_w.ap()
    x1_d = dt("x1_d", [D, NOWN], F32).ap()

    NB = 210944
    import contextlib
    with contextlib.ExitStack() as es:
        arena_t = es.enter_context(nc.sbuf_tensor("arena", [128, NB // 2], BF16))
        arena32 = arena_t.bitcast(F32)
        ps = [es.enter_context(nc.psum_tensor("ps%d" % i, [128, 512], F32)) for i in range(8)]
        sems = {e: es.enter_context(nc.semaphore("s_" + e)) for e in ENGS}
        dsems = [es.enter_context(nc.semaphore("d%d" % i)) for i in range(NDSEM)]
        es.enter_context(nc.allow_low_precision("bf16 matmul operands, fp32 accumulate"))
        A = Arena(arena_t, arena32, NB)
        sc = Sched(nc, sems, dsems)

        def P(i, n=512, p0=0, np_=128, c0=0):
            return ps[i][p0:p0 + np_, c0:c0 + n]

        def mm(out, lhsT, rhs, start, stop, reads, writes):
            sc.add("pe", lambda e: e.matmul(out, lhsT, rhs, start=start, stop=stop), reads, writes)

        def act(out, in_, func, reads, writes, bias=None, scale=None):
            kw = {}
            if bias is not None:
                kw["bias"] = bias
            if scale is not None:
                kw["scale"] = scale
            sc.add("act", lambda e: e.activation(out, in_, func, **kw), reads, writes)

        def tt(out, in0, in1, op, reads, writes, eng="dve"):
            sc.add(eng, lambda e: e.tensor_tensor(out, in0, in1, op), reads, writes)

        def ts(out, in0, s1, op0, reads, writes, s2=None, op1=None, eng="dve"):
            if op1 is None:
                sc.add(eng, lambda e: e.tensor_scalar(out, in0, s1, None, op0), reads, writes)
            else:
                sc.add(eng, lambda e: e.tensor_scalar(out, in0, s1, s2, op0, op1), reads, writes)

        def stt(out, in0, scalar, in1, op0, op1, reads, writes):
            sc.add("dve", lambda e: e.scalar_tensor_tensor(out, in0, scalar, in1, op0, op1), reads, writes)

        def recip(out, in_, reads, writes):
            sc.add("dve", lambda e: e.reciprocal(out, in_), reads, writes)

        def cp(out, in_, reads, writes, eng="dve"):
            sc.add(eng, lambda e: e.tensor_copy(out, in_), reads, writes)

        def memset(ap, val, writes, eng="pool"):
            sc.add(eng, lambda e: e.memset(ap, val), (), writes)

        def dma(out, in_, reads, writes, q="sp", is_out=False):
            sc.add(q, lambda e: e.dma_start(out=out, in_=in_), reads, writes, dma=True, is_out=is_out)

        stg = {"offs": None, "i": 0}

        def load_w(dst_off, src_ap_fn, ncols, wkey, np_=128, engs=("act", "pool")):
            for c0 in range(0, ncols, 1024):
                n = min(1024, ncols - c0)
                i = stg["i"] % len(stg["offs"])
                stg["i"] += 1
                so = stg["offs"][i]
                dma(A.a32(so, [[1, n]], 0, np_), src_ap_fn(c0, n), (), ["stg%d" % i])
                eng = engs[stg["i"] % len(engs)]
                if eng in ("pool", "dve"):
                    cp(A.a16(dst_off + c0, [[1, n]], 0, np_), A.a32(so, [[1, n]], 0, np_), ["stg%d" % i], [wkey], eng=eng)
                else:
                    act(A.a16(dst_off + c0, [[1, n]], 0, np_), A.a32(so, [[1, n]], 0, np_), AF.Copy, ["stg%d" % i], [wkey])

        o_cst = A.f32(512)
        o_maskb = A.b16(2048 + 128)
        o_small = A.f32(512)
        o_biasK = A.f32(512)
        o_ucar = A.f32(8)
        o_b1b = A.b16(8)
        o_b2b = A.b16(8)
        small_mark = A.off
        o_yaT = A.b16(4 * NOWN)
        mark_ya = A.off
        o_zT = A.f32(S)
        persist_mark = A.off

        tri = A.a32(o_cst, [[1, 128]])
        ones = A.a32(o_cst + 128, [[1, 128]])
        ident = A.a32(o_cst + 256, [[1, 128]])
        blockones = A.a32(o_cst + 384, [[1, 128]])
        identb = A.a16(o_maskb + 2048, [[1, 128]])

        def small(c0, n=1, p0=0, np_=128):
            return A.a32(o_small + c0, [[1, n]], p0, np_)
        C_MOD = 0
        C_A1 = 48
        C_A2 = 56
        C_CB = 64
        C_CBV = 88
        C_CF = 96
        C_QS = 97
        C_G1 = 98
        C_G2 = 106
        C_BADA = 114
        C_CT = 162
        C_CVA = 170
        C_CVF = 182
        C_FLAG = 314
        C_KMASK = 315
        C_CBG = 379
        C_CB2 = 395
        C_CBW = 439
        C_BF = 483
        C_GQ = 484
        C_GK = 485
        C_TMP = 486
        C_FREF = 494
        C_FRB = 495

        dma(A.a32(o_cst, [[1, 512]]), cst[:, :], (), ["cst"])
        dma(A.a16(o_maskb, [[1, 2176]]), maskb_d[:, :], (), ["maskb"])
        dma(small(C_BADA, 48), b_adaT[:, :], (), ["bada"])
        dma(small(C_CT, 8), cT[:, :], (), ["ct"])
        dma(small(C_G1, 8), g1T[:, :], (), ["g1"])
        dma(small(C_G2, 8), g2T[:, :], (), ["g2"])
        dma(small(C_CVA, 12), convA[:, :], (), ["cva"])
        dma(small(C_CVF, 132), convF[:, :], (), ["cvf"])
        dma(small(C_FLAG, 1), flag[:, :], (), ["flag"])
        dma(small(C_KMASK, 64), kmask[:, :], (), ["kmask"])
        dma(small(C_BF, 1, 0, 8), bfT[:, :], (), ["bf"])
        dma(small(C_GQ, 1), gqT[:, :], (), ["gq"])
        dma(small(C_GK, 1), gkT[:, :], (), ["gk"])

        o_w1 = A.b16(KC * W1C)
        o_scb = A.b16(8)
        o_wp = [A.b16(8 * 1024) for _ in range(2)]
        stg["offs"] = [A.f32(1024) for _ in range(4)]
        scb = A.a16(o_scb, [[1, 8]])
        act(scb, small(C_CT, 8), AF.Silu, ["ct"], ["scb"])
        for pj in range(2):
            b = pj % 2
            for k in range(KC):
                load_w(o_wp[b] + k * 1024,
                       lambda c0, n, k=k, pj=pj: w_ada[k * 128:(k + 1) * 128, pj * 1024 + c0:pj * 1024 + c0 + n],
                       1024, "wp%d_%d" % (b, k), engs=("dve", "act"))
            for jj in range(8):
                j = pj * 8 + jj
                for k in range(KC):
                    mm(P(7, 1, c0=j), A.a16(o_wp[b] + k * 1024 + jj * 128, [[1, 128]]),
                       A.a16(o_scb + k, [[1, 1]]), k == 0, k == KC - 1,
                       ["wp%d_%d" % (b, k), "scb"], ["ps7"])
        tt(small(C_MOD, 16), P(7, 16), small(C_BADA, 16), ALU.add, ["ps7", "bada"], ["mod"])
        stt(small(C_A1, 8), small(C_MOD + 8, 8), 1.0, small(C_G1, 8), ALU.add, ALU.mult, ["mod", "g1"], ["A1t"])
        ts(small(C_A1, 8), small(C_A1, 8), 32.0, ALU.mult, ["A1t"], ["A1"])
        cp(A.a16(o_b1b, [[1, 8]]), small(C_MOD + 0, 8), ["mod"], ["b1b"])
        stt(small(C_QS), small(C_GQ), 8.0, small(C_GK), ALU.mult, ALU.mult, ["gq", "gk"], ["qs"])
        for k in range(KC):
            load_w(o_w1 + k * W1C, lambda c0, n, k=k: w_in[k * 128:(k + 1) * 128, c0:c0 + n], W1C, "w1_%d" % k,
                   engs=("dve", "act"))
        sc.barrier()
        A.off = persist_mark

        o_w1 = A.b16(KC * W1C)
        o_xg = [A.f32(KC * 512) for _ in range(3)]
        o_sq = [A.f32(512) for _ in range(2)]
        o_acc = A.f32(512)
        o_hn = [A.b16(KC * 512) for _ in range(2)]
        o_inv = A.f32(512)
        o_rt = A.f32(512)
        o_rt2 = A.f32(512)
        o_kb = [A.f32(512) for _ in range(2)]
        o_ksq = [A.f32(512) for _ in range(2)]
        o_invk = A.f32(512)
        o_kst = [A.b16(512) for _ in range(2)]
        o_vst = A.b16(8 * 4 * 65)
        o_ccs = A.f32(512)
        o_u = A.f32(516)
        o_y = A.f32(512)

        def w1(k, c0, n):
            return A.a16(o_w1 + k * W1C + c0, [[1, n]])

        W1K = ["w1_%d" % k for k in range(KC)]
        b1b = A.a16(o_b1b, [[1, 8]])
        for j in range(24):
            if 20 <= j < 24:
                continue
            for k in range(KC):
                mm(P(7, 1, c0=j), w1(k, j * 128, 128), A.a16(o_b1b + k, [[1, 1]]), k == 0, k == KC - 1,
                   W1K + ["b1b"], ["ps7"])
        for h in range(8):
            for k in range(KC):
                mm(P(7, 1, 0, 64, c0=24 + h), w1(k, 2560 + h * 64, 64), A.a16(o_b1b + k, [[1, 1]]),
                   k == 0, k == KC - 1, W1K + ["b1b"], ["ps7"])
        for k in range(KC):
            mm(P(7, 1, 0, 8, c0=32), w1(k, 3072, 8), A.a16(o_b1b + k, [[1, 1]]), k == 0, k == KC - 1,
               W1K + ["b1b"], ["ps7"])
        cp(small(C_CB, 20), P(7, 20), ["ps7"], ["cb"])
        cp(small(C_CBV, 8, 0, 64), P(7, 8, 0, 64, c0=24), ["ps7"], ["cbv"])
        tt(small(C_CF, 1, 0, 8), P(7, 1, 0, 8, c0=32), small(C_BF, 1, 0, 8), ALU.add, ["ps7", "bf"], ["cf"])
        memset(A.a32(o_u, [[1, 2]]), 0.0, ["u"])
        memset(A.a16(o_vst, [[1, 8 * 4 * 65]]), 1.0, ["vst"])
        memset(small(C_TMP), EPS * D, ["tmpc"], eng="dve")
        memset(small(C_TMP + 1), EPS * 64, ["tmpc"], eng="dve")
        memset(small(C_TMP + 2), 1.0, ["tmpc"], eng="dve")

        def load_group(src_fn, n, xb):
            for k in range(KC):
                dma(A.a32(o_xg[xb] + k * 512, [[1, n]]), src_fn(k), (), ["xg%d_%d" % (xb, k)])

        def norm_group(src_fn, n, Acol, gi, tag, nbuf=2, xb=None, hb=None, do_load=True):
            b = gi % nbuf if xb is None else xb
            hb = b if hb is None else hb
            if do_load:
                load_group(src_fn, n, b)
            for k in range(KC):
                if k == 0:
                    act(A.a32(o_acc, [[1, n]]), A.a32(o_xg[b] + k * 512, [[1, n]]), AF.Square,
                        ["xg%d_%d" % (b, k)], ["acc"])
                else:
                    sb = k % 2
                    act(A.a32(o_sq[sb], [[1, n]]), A.a32(o_xg[b] + k * 512, [[1, n]]), AF.Square,
                        ["xg%d_%d" % (b, k)], ["sq%d" % sb])
                    tt(A.a32(o_acc, [[1, n]]), A.a32(o_acc, [[1, n]]), A.a32(o_sq[sb], [[1, n]]), ALU.add,
                       ["acc", "sq%d" % sb], ["acc"], eng="dve")
            mm(P(2, n), ones, A.a32(o_acc, [[1, n]]), True, True, ["acc", "cst"], ["ps2"])
            act(A.a32(o_rt, [[1, n]]), P(2, n), AF.Ln, ["ps2", "tmpc"], ["rt"], bias=small(C_TMP), scale=1.0)
            act(A.a32(o_inv, [[1, n]]), A.a32(o_rt, [[1, n]]), AF.Exp, ["rt"], ["inv"], scale=-0.5)
            for k in range(KC):
                stt(A.a16(o_hn[hb] + k * 512, [[1, n]]), A.a32(o_xg[b] + k * 512, [[1, n]]),
                    small(Acol + k), A.a32(o_inv, [[1, n]]), ALU.mult, ALU.mult,
                    ["xg%d_%d" % (b, k), "inv", "A1", "A2"], ["hn%d" % hb])
            return b

        hstate = {"i": 0, "pending": None}

        def headnorm_p1(psi, n, cbcol):
            i = hstate["i"] % 2
            hstate["i"] += 1
            act(A.a32(o_kb[i], [[1, n]]), P(psi, n), AF.Identity, ["ps%d" % psi, "cb"], ["kb%d" % i], bias=small(cbcol), scale=1.0)
            act(A.a32(o_ksq[i], [[1, n]]), P(psi, n), AF.Square, ["ps%d" % psi, "cb"], ["ksq%d" % i], bias=small(cbcol), scale=1.0)
            return i

        def headnorm_p2(i, n, scale_ap, kbuf, dst_fn):
            mm(P(3, n), blockones, A.a32(o_ksq[i], [[1, n]]), True, True, ["ksq%d" % i, "cst"], ["ps3"])
            act(A.a32(o_rt2, [[1, n]]), P(3, n), AF.Ln, ["ps3", "tmpc"], ["rt2"], bias=small(C_TMP + 1), scale=1.0)
            act(A.a32(o_invk, [[1, n]]), A.a32(o_rt2, [[1, n]]), AF.Exp, ["rt2"], ["invk"], scale=-0.5)
            out16 = A.a16(o_kst[kbuf], [[1, n]])
            if scale_ap is None:
                tt(out16, A.a32(o_kb[i], [[1, n]]), A.a32(o_invk, [[1, n]]), ALU.mult, ["kb%d" % i, "invk"], ["kst%d" % kbuf])
            else:
                stt(out16, A.a32(o_kb[i], [[1, n]]), scale_ap, A.a32(o_invk, [[1, n]]), ALU.mult, ALU.mult,
                    ["kb%d" % i, "invk", "qs"], ["kst%d" % kbuf])
            dst_fn(out16, kbuf)

        def flush_pending():
            if hstate["pending"] is not None:
                hstate["pending"]()
                hstate["pending"] = None

        groups = ([(c, n, False, 0) for (c, n) in PRE_GROUPS] +
                  [(HALO0 + c, n, True, c) for (c, n) in OWN_GROUPS])
        psr = 0
        PA_BANKS = [0, 1, 4, 5, 6]
        kcount = 0

        def gsrc(gi):
            col0, n, own, oc0 = groups[gi]
            return (lambda k: xT[k * 128:(k + 1) * 128, col0:col0 + n]), n

        def gload(gi):
            f_, n_ = gsrc(gi)
            load_group(f_, n_, gi % 3)

        def stage1(gi):
            f_, n_ = gsrc(gi)
            norm_group(f_, n_, C_A1, gi, "a", xb=gi % 3, hb=gi % 2, do_load=False)

        gload(0)
        gload(1)
        stage1(0)
        for gi, (col0, n, own, oc0) in enumerate(groups):
            if gi + 2 < len(groups):
                gload(gi + 2)
            if gi + 1 < len(groups):
                stage1(gi + 1)
            b = gi % 2
            HN = ["hn%d" % b]

            def hn(k, c0=0, nn=None):
                return A.a16(o_hn[b] + k * 512 + c0, [[1, nn if nn is not None else n]])

            def proj(psi, wc0, wn, np_=128):
                for k in range(KC):
                    mm(P(psi, n, 0, np_), w1(k, wc0, wn), hn(k), k == 0, k == KC - 1, W1K + HN, ["ps%d" % psi])
            for c in range(4):
                psi = PA_BANKS[psr % 5]
                psr += 1
                proj(psi, 2048 + c * 128, 128)
                i = headnorm_p1(psi, n, C_CB + 16 + c)
                flush_pending()
                kbuf = kcount % 2
                kcount += 1

                def fin(i=i, n=n, kbuf=kbuf, c=c, col0=col0):
                    headnorm_p2(i, n, None, kbuf,
                                lambda o16, kb_: dma(kT_d[c * 128:(c + 1) * 128, col0:col0 + n], o16,
                                                     ["kst%d" % kb_], ["kT_d"]))
                hstate["pending"] = fin
            nt = n // 128
            for t in range(nt):
                psi = PA_BANKS[psr % 5]
                psr += 1
                for k in range(KC):
                    mm(P(psi, 512), hn(k, t * 128, 128), w1(k, 2560, 512), k == 0, k == KC - 1, W1K + HN, ["ps%d" % psi])
                if t == 0:
                    flush_pending()
                sc.add("act", lambda e, psi=psi, t=t: e.activation(
                    A.a16(o_vst + t * 65, [[260, 8], [1, 64]]),
                    bass.AP(ps[psi], 0, [[512, 128], [64, 8], [1, 64]]), AF.Copy), ["ps%d" % psi], ["vst"])
            dma(bass.AP(v_w, (col0 // 128) * 65, [[NT * 65, 128], [128 * NT * 65, 8], [1, nt * 65]]),
                A.a16(o_vst, [[260, 8], [1, nt * 65]]), ["vst"], ["v_d"])
            psi = PA_BANKS[psr % 5]
            psr += 1
            proj(psi, 3072, 8, np_=8)
            act(A.a32(o_zT + col0, [[1, n]], 0, 8), P(psi, n, 0, 8), AF.Identity, ["ps%d" % psi, "cf"], ["zT"],
                bias=small(C_CF, 1, 0, 8), scale=1.0)
            if own:
                for c in range(4):
                    psi = PA_BANKS[psr % 5]
                    psr += 1
                    proj(psi, 1536 + c * 128, 128)
                    i = headnorm_p1(psi, n, C_CB + 12 + c)
                    flush_pending()
                    kbuf = kcount % 2
                    kcount += 1

                    def fin(i=i, n=n, kbuf=kbuf, c=c, oc0=oc0):
                        headnorm_p2(i, n, small(C_QS), kbuf,
                                    lambda o16, kb_: dma(qT_d[c * 128:(c + 1) * 128, oc0:oc0 + n], o16,
                                                         ["kst%d" % kb_], ["qT_d"]))
                    hstate["pending"] = fin
                for c in range(4):
                    p_cc = PA_BANKS[psr % 5]
                    psr += 1
                    proj(p_cc, 512 + c * 128, 128)
                    if c == 0:
                        flush_pending()
                    act(A.a32(o_ccs, [[1, n]]), P(p_cc, n), AF.Identity, ["ps%d" % p_cc, "cb"], ["ccs"],
                        bias=small(C_CB + 4 + c), scale=1.0)
                    p_cv = PA_BANKS[psr % 5]
                    psr += 1
                    proj(p_cv, 1024 + c * 128, 128)
                    stt(A.a32(o_u + 2, [[1, n]]), P(p_cv, n), small(C_CB + 8 + c), A.a32(o_ccs, [[1, n]]),
                        ALU.add, ALU.mult, ["ps%d" % p_cv, "ccs", "cb", "ucar%d" % c], ["u"])
                    if oc0 == 0:
                        memset(A.a32(o_u, [[1, 2]]), 0.0, ["u"], eng="dve")
                        ts(A.a32(o_u + 2, [[1, n]]), A.a32(o_u + 2, [[1, n]]), small(C_FLAG), ALU.mult, ["u", "flag"], ["u"])
                    else:
                        cp(A.a32(o_u, [[1, 2]]), A.a32(o_ucar + c * 2, [[1, 2]]), ["ucar%d" % c], ["u"])
                    ts(A.a32(o_y, [[1, n]]), A.a32(o_u + 2, [[1, n]]), small(C_CVA + c * 3 + 2), ALU.mult, ["u", "cva"], ["y"])
                    stt(A.a32(o_y, [[1, n]]), A.a32(o_u + 1, [[1, n]]), small(C_CVA + c * 3 + 1), A.a32(o_y, [[1, n]]),
                        ALU.mult, ALU.add, ["u", "y", "cva"], ["y"])
                    stt(A.a32(o_y, [[1, n]]), A.a32(o_u, [[1, n]]), small(C_CVA + c * 3 + 0), A.a32(o_y, [[1, n]]),
                        ALU.mult, ALU.add, ["u", "y", "cva"], ["y"])
                    cp(A.a32(o_ucar + c * 2, [[1, 2]]), A.a32(o_u + n, [[1, 2]]), ["u"], ["ucar%d" % c])
                    p_cb = PA_BANKS[psr % 5]
                    psr += 1
                    proj(p_cb, c * 128, 128)
                    stt(A.a16(o_yaT + c * NOWN + oc0, [[1, n]]), P(p_cb, n), small(C_CB + c), A.a32(o_y, [[1, n]]),
                        ALU.add, ALU.mult, ["ps%d" % p_cb, "y", "cb"], ["yaT"])
        flush_pending()
        sc.barrier()
        A.off = persist_mark

        o_e = A.f32(2048)
        o_val = A.f32(2048)
        o_one8 = A.f32(2048)
        o_fqf = A.f32(NOWN)
        o_fr = A.f32(NOWN)
        o_fq16 = A.b16(3 * NOWN)
        o_diag = A.f32(8)

        def r8(off, n, c0=0):
            return A.a32(off + c0, [[1, n]], 0, 8)

        memset(r8(o_one8, 2048), 1.0, ["one8"], eng="dve")
        for pc in range(4):
            c0 = pc * 2048
            dma(r8(o_val, 2048), validT[:, c0:c0 + 2048], (), ["val"])
            act(r8(o_e, 2048), r8(o_zT, 2048, c0), AF.Exp, ["zT"], ["e"], scale=-1.0)
            act(r8(o_e, 2048), r8(o_e, 2048), AF.Ln, ["e", "tmpc"], ["e"], bias=small(C_TMP + 2, 1, 0, 8), scale=1.0)
            stt(r8(o_e, 2048), r8(o_e, 2048), -1.0, r8(o_val, 2048), ALU.mult, ALU.mult, ["e", "val"], ["e"])
            init = 0.0 if pc == 0 else r8(o_zT, 1, c0 - 1)
            sc.add("dve", lambda e, c0=c0, init=init: e.tensor_tensor_scan(
                r8(o_zT, 2048, c0), r8(o_one8, 2048), r8(o_e, 2048), init, ALU.mult, ALU.add),
                ["e", "one8", "zT"], ["zT"])
        cp(small(C_FREF, 1, 0, 8), r8(o_zT, 1, S - 1), ["zT"], ["fref"])
        ts(r8(o_fqf, NOWN), r8(o_zT, NOWN, HALO0), small(C_FREF, 1, 0, 8), ALU.subtract, ["zT", "fref"], ["fqf"])
        cp(A.a16(o_fq16, [[1, NOWN]], 0, 8), r8(o_fqf, NOWN), ["fqf"], ["fq16a"])
        tt(r8(o_fr, NOWN), r8(o_fqf, NOWN), A.a16(o_fq16, [[1, NOWN]], 0, 8), ALU.subtract, ["fqf", "fq16a"], ["fr"])
        cp(A.a16(o_fq16 + NOWN, [[1, NOWN]], 0, 8), r8(o_fr, NOWN), ["fr"], ["fq16b"])
        tt(r8(o_fqf, NOWN), r8(o_fr, NOWN), A.a16(o_fq16 + NOWN, [[1, NOWN]], 0, 8), ALU.subtract, ["fr", "fq16b"], ["fqf"])
        cp(A.a16(o_fq16 + 2 * NOWN, [[1, NOWN]], 0, 8), r8(o_fqf, NOWN), ["fqf"], ["fq16c"])
        dma(fq_d[:, :], A.a16(o_fq16, [[1, 3 * NOWN]], 0, 8), ["fq16a", "fq16b", "fq16c"], ["fq_d"])
        for t in range(NT):
            mm(P(6, 8, c0=t * 8), r8(o_zT, 128, t * 128), A.a32(o_cst + 256, [[1, 8]], 0, 8), True, True,
               ["zT", "cst"], ["ps6"])
        ts(A.a32(o_diag, [[1, 8]], 0, 8), A.a32(o_cst + 256, [[1, 8]], 0, 8), small(C_FREF, 1, 0, 8), ALU.mult,
           ["cst", "fref"], ["diag"])
        mm(P(7, 8), A.a32(o_cst + 128, [[1, 128]], 0, 8), A.a32(o_diag, [[1, 8]], 0, 8), True, True, ["diag", "cst"], ["ps7"])
        cp(small(C_FRB, 8), P(7, 8), ["ps7"], ["frb"])
        tt(A.a32(o_biasK, [[8, NT], [1, 8]]), A.a32(o_small + C_FRB, [[0, NT], [1, 8]]),
           bass.AP(ps[6], 0, [[512, 128], [8, NT], [1, 8]]), ALU.subtract, ["frb", "ps6"], ["biasK"])
        tt(A.a32(o_biasK, [[8, NT], [1, 8]]), A.a32(o_biasK, [[8, NT], [1, 8]]),
           A.a32(o_small + C_KMASK, [[1, NT], [0, 8]]), ALU.add, ["biasK", "kmask"], ["biasK"])
        sc.barrier()
        A.off = mark_ya
        o_ybT = A.b16(8 * NOWN)
        persist_mark = A.off

        o_wg = A.b16(KC * 2048)
        o_wo = A.b16(KC * 1024)
        mark_c = A.off
        stg["offs"] = [A.f32(1024) for _ in range(2)]
        o_k = [A.b16(S) for _ in range(2)]
        o_v = [A.b16(NT * 65) for _ in range(2)]
        o_q = [A.b16(NOWN) for _ in range(2)]
        NPT = 4
        o_pT = [A.b16(512) for _ in range(NPT)]
        o_rec = A.f32(512)
        o_osb = A.f32(512)
        o_tmp = A.f32(512)
        for b in range(2):
            memset(A.a16(o_k[b], [[1, S]], 64, 3), 1.0, ["k%d" % b], eng="pool")

        def load_head(h):
            b = h % 2
            KB, VB, QB = "k%d" % b, "v%d" % b, "q%d" % b
            for half in range(2):
                dma(A.a16(o_k[b] + half * 4096, [[1, 4096]], 0, 64),
                    kT_d[h * 64:(h + 1) * 64, half * 4096:(half + 1) * 4096], ["kT_d"], [KB])
            dma(A.a16(o_v[b], [[1, NT * 65]]), v_d[h * 128:(h + 1) * 128, :], ["v_d"], [VB])
            dma(A.a16(o_q[b], [[1, NOWN]], 0, 64), qT_d[h * 64:(h + 1) * 64, :], ["qT_d"], [QB])
            dma(A.a16(o_q[b], [[1, NOWN]], 64, 3),
                bass.AP(fq_w, h * 3 * NOWN, [[NOWN, 3], [1, NOWN]]), ["fq_d"], [QB])

        units = []
        blk = 0
        for h in range(8):
            for (q0, nq) in OWN_GROUPS:
                qt0 = 47 + q0 // 128
                nvis = 47 + (q0 + nq) // 128
                for kt in range(nvis):
                    units.append((h, q0, nq, kt, kt == 0, kt == nvis - 1, kt >= qt0, kt - qt0, blk))
                blk += 1

        def emit_S(idx):
            h, q0, nq, kt, first, last, partial, jm, blk = units[idx]
            b = h % 2
            KB, QB = "k%d" % b, "q%d" % b
            sb = 3 + idx % 3
            pb = idx % NPT
            mm(P(sb, nq), A.a16(o_k[b] + kt * 128, [[1, 128]], 0, 67), A.a16(o_q[b] + q0, [[1, nq]], 0, 67),
               True, not partial, [KB, QB], ["ps%d" % sb])
            if partial:
                mm(P(sb, nq), identb, A.a16(o_maskb + jm * 512, [[1, nq]]), False, True, ["maskb"], ["ps%d" % sb])
            act(A.a16(o_pT[pb], [[1, nq]]), P(sb, nq), AF.Exp, ["ps%d" % sb, "biasK"], ["pT%d" % pb],
                bias=A.a32(o_biasK + kt * 8 + h, [[1, 1]]), scale=1.0)

        def emit_PV(idx):
            h, q0, nq, kt, first, last, partial, jm, blk = units[idx]
            b = h % 2
            VB = "v%d" % b
            pb = idx % NPT
            ob = 6 + blk % 2
            OK_ = "ps%d" % ob
            mm(P(ob, nq, 0, 65), A.a16(o_v[b] + kt * 65, [[1, 65]]), A.a16(o_pT[pb], [[1, nq]]),
               first, last, [VB, "pT%d" % pb], [OK_])
            if last:
                ts(A.a32(o_rec, [[1, nq]], 64, 1), P(ob, nq, 64, 1), 1e-30, ALU.add, [OK_], ["rec"])
                recip(A.a32(o_rec, [[1, nq]], 64, 1), A.a32(o_rec, [[1, nq]], 64, 1), ["rec"], ["rec"])
                mm(P(2, nq, 0, 64), A.a32(o_cst + 128, [[1, 64]], 64, 1), A.a32(o_rec, [[1, nq]], 64, 1), True, True,
                   ["rec", "cst"], ["ps2"])
                cp(A.a32(o_osb, [[1, nq]], 0, 64), P(ob, nq, 0, 64), [OK_], ["osb"])
                tt(A.a32(o_tmp, [[1, nq]], 0, 64), A.a32(o_osb, [[1, nq]], 0, 64), P(2, nq, 0, 64), ALU.mult,
                   ["osb", "ps2"], ["tmp"])
                ts(A.a16(o_ybT + h * NOWN + q0, [[1, nq]], 0, 64), A.a32(o_tmp, [[1, nq]], 0, 64),
                   small(C_CBV + h, 1, 0, 64), ALU.add, ["tmp", "cbv"], ["ybT"])

        LA = 2
        o_wp2 = [A.b16(8 * 512) for _ in range(2)]

        def ada_load(blk_):
            b_ = blk_ % 2
            for k in range(KC):
                load_w(o_wp2[b_] + k * 512,
                       lambda c0, n, k=k, blk_=blk_: w_ada[k * 128:(k + 1) * 128,
                                                           2048 + blk_ * 512 + c0:2048 + blk_ * 512 + c0 + n],
                       512, "wq%d_%d" % (b_, k), engs=("pool",))

        def ada_mm(blk_):
            b_ = blk_ % 2
            for jj in range(4):
                j = 16 + blk_ * 4 + jj
                for k in range(KC):
                    mm(P(0, 1, c0=j), A.a16(o_wp2[b_] + k * 512 + jj * 128, [[1, 128]]),
                       A.a16(o_scb2 + k, [[1, 1]]), k == 0, k == KC - 1,
                       ["wq%d_%d" % (b_, k), "scb2"], ["ps0"])

        o_scb2 = A.b16(8)
        act(A.a16(o_scb2, [[1, 8]]), small(C_CT, 8), AF.Silu, ["ct"], ["scb2"])
        load_head(0)
        load_head(1)
        for k in range(KC):
            load_w(o_wg + k * 2048, lambda c0, n, k=k: w_in[k * 128:(k + 1) * 128, 3080 + c0:3080 + c0 + n], 2048, "wg",
                   engs=("pool",))
            load_w(o_wo + k * 1024, lambda c0, n, k=k: w_o[k * 128:(k + 1) * 128, c0:c0 + n], 1024, "wo", engs=("pool",))
        ada_load(0)
        ada_load(1)
        for idx in range(len(units) + LA):
            if idx >= 250 and idx % 250 == 0 and idx // 250 <= 8:
                blk_ = idx // 250 - 1
                ada_mm(blk_)
                if blk_ + 2 < 8:
                    ada_load(blk_ + 2)
            if idx < len(units):
                u = units[idx]
                if u[3] == LA + 2 and u[1] == 0 and 1 <= u[0] and u[0] + 1 < 8:
                    load_head(u[0] + 1)
                emit_S(idx)
            if idx - LA >= 0:
                emit_PV(idx - LA)
        tt(small(C_MOD + 16, 32), P(0, 32, c0=16), small(C_BADA + 16, 32), ALU.add, ["ps0", "bada"], ["mod2"])
        stt(small(C_A2, 8), small(C_MOD + 32, 8), 1.0, small(C_G2, 8), ALU.add, ALU.mult, ["mod2", "g2"], ["A2t"])
        ts(small(C_A2, 8), small(C_A2, 8), 32.0, ALU.mult, ["A2t"], ["A2"])
        cp(A.a16(o_b2b, [[1, 8]]), small(C_MOD + 24, 8), ["mod2"], ["b2b"])
        sc.barrier()
        A.off = mark_c

        o_wa = A.b16(4 * 1024)
        o_wb = A.b16(8 * 1024)
        o_xg = [A.f32(KC * 512)]
        o_sq = [A.f32(512) for _ in range(2)]
        o_acc = A.f32(512)
        o_hn = [A.b16(KC * 512)]
        o_inv = A.f32(512)
        o_rt = A.f32(512)
        o_sga = A.f32(512)
        o_sgb = A.f32(512)
        o_m1 = A.f32(512)
        o_m2 = A.f32(512)
        o_mg = A.b16(KC * 512)
        stg["offs"] = [o_xg[0], o_xg[0] + 1024, o_xg[0] + 2048]
        for c in range(4):
            load_w(o_wa + c * 1024, lambda c0, n, c=c: w_a[c * 128:(c + 1) * 128, c0:c0 + n], 1024, "wa", engs=("dve", "act"))
        for h in range(8):
            load_w(o_wb + h * 1024, lambda c0, n, h=h: w_b[h * 64:(h + 1) * 64, c0:c0 + n], 1024, "wb", np_=64, engs=("dve", "act"))
        sc.barrier()
        for j in range(16):
            for k in range(KC):
                mm(P(7, 1, c0=j), A.a16(o_wg + k * 2048 + j * 128, [[1, 128]]), A.a16(o_b1b + k, [[1, 1]]),
                   k == 0, k == KC - 1, ["wg", "b1b"], ["ps7"])
        cp(small(C_CBG, 16), P(7, 16), ["ps7"], ["cbg"])
        for gi2, (oc0, n) in enumerate(OWN_GROUPS):
            col0 = HALO0 + oc0
            norm_group(lambda k: xT[k * 128:(k + 1) * 128, col0:col0 + n], n, C_A1, 0, "c", nbuf=1)
            for e_ in range(KC):
                pa_, pb_ = [0, 6][e_ % 2], [1, 7][e_ % 2]
                for k in range(KC):
                    mm(P(pa_, n), A.a16(o_wg + k * 2048 + e_ * 128, [[1, 128]]), A.a16(o_hn[0] + k * 512, [[1, n]]),
                       k == 0, k == KC - 1, ["wg", "hn0"], ["ps%d" % pa_])
                act(A.a32(o_sga, [[1, n]]), P(pa_, n), AF.Sigmoid, ["ps%d" % pa_, "cbg"], ["sga"], bias=small(C_CBG + e_), scale=1.0)
                for k in range(KC):
                    mm(P(pb_, n), A.a16(o_wg + k * 2048 + 1024 + e_ * 128, [[1, 128]]), A.a16(o_hn[0] + k * 512, [[1, n]]),
                       k == 0, k == KC - 1, ["wg", "hn0"], ["ps%d" % pb_])
                act(A.a32(o_sgb, [[1, n]]), P(pb_, n), AF.Sigmoid, ["ps%d" % pb_, "cbg"], ["sgb"], bias=small(C_CBG + 8 + e_), scale=1.0)
                for c in range(4):
                    mm(P(4, n), A.a16(o_wa + c * 1024 + e_ * 128, [[1, 128]]), A.a16(o_yaT + c * NOWN + oc0, [[1, n]]),
                       c == 0, c == 3, ["wa", "yaT"], ["ps4"])
                tt(A.a32(o_m1, [[1, n]]), A.a32(o_sga, [[1, n]]), P(4, n), ALU.mult, ["sga", "ps4"], ["m1"])
                for h in range(8):
                    mm(P(5, n), A.a16(o_wb + h * 1024 + e_ * 128, [[1, 128]], 0, 64),
                       A.a16(o_ybT + h * NOWN + oc0, [[1, n]], 0, 64), h == 0, h == 7, ["wb", "ybT"], ["ps5"])
                tt(A.a32(o_m2, [[1, n]]), A.a32(o_sgb, [[1, n]]), P(5, n), ALU.mult, ["sgb", "ps5"], ["m2"])
                tt(A.a16(o_mg + e_ * 512, [[1, n]]), A.a32(o_m1, [[1, n]]), A.a32(o_m2, [[1, n]]), ALU.add,
                   ["m1", "m2"], ["mg%d" % e_], eng="pool")
            MG = ["mg%d" % i for i in range(KC)]
            for e_ in range(KC):
                psi = 2 + e_ % 2
                for k in range(KC):
                    mm(P(psi, n), A.a16(o_wo + k * 1024 + e_ * 128, [[1, 128]]), A.a16(o_mg + k * 512, [[1, n]]),
                       k == 0, k == KC - 1, ["wo"] + MG, ["ps%d" % psi])
                stt(A.a32(o_xg[0] + e_ * 512, [[1, n]]), P(psi, n), small(C_MOD + 16 + e_),
                    A.a32(o_xg[0] + e_ * 512, [[1, n]]), ALU.mult, ALU.add,
                    ["ps%d" % psi, "mod", "xg0_%d" % e_], ["xg0_%d" % e_])
                dma(x1_d[e_ * 128:(e_ + 1) * 128, oc0:oc0 + n], A.a32(o_xg[0] + e_ * 512, [[1, n]]),
                    ["xg0_%d" % e_], ["x1_d"])
        sc.barrier()
        A.off = small_mark

        NF = 2 * DFF
        o_wup = A.b16(KC * NF)
        o_wdn = A.b16(22 * 1024)
        o_xg = [A.f32(KC * 512)]
        o_sq = [A.f32(512) for _ in range(2)]
        o_acc = A.f32(512)
        o_hn = [A.b16(KC * 512)]
        o_inv = A.f32(512)
        o_rt = A.f32(512)
        o_yg = A.f32(512)
        o_yv = A.f32(512)
        o_sg = A.f32(512)
        o_act = A.b16(22 * 512)
        o_ws = A.f32(44)
        stg["offs"] = [o_xg[0], o_xg[0] + 1024, o_xg[0] + 2048]
        for k in range(KC):
            load_w(o_wup + k * NF, lambda c0, n, k=k: w_up[k * 128:(k + 1) * 128, c0:c0 + n], NF, "wup", engs=("dve", "act"))
        for c in range(22):
            load_w(o_wdn + c * 1024, lambda c0, n, c=c: w_dn[c * 128:(c + 1) * 128, c0:c0 + n], 1024, "wdn", engs=("dve", "act"))
        sc.barrier()
        for fc in range(44):
            for k in range(KC):
                mm(P(7, 1, c0=fc), A.a16(o_wup + k * NF + fc * 128, [[1, 128]]), A.a16(o_b2b + k, [[1, 1]]),
                   k == 0, k == KC - 1, ["wup", "b2b"], ["ps7"])
        cp(small(C_CB2, 44), P(7, 44), ["ps7"], ["cb2"])
        tt(A.a32(o_ws, [[1, 44]]), A.a32(o_small + C_CVF, [[3, 44]]), A.a32(o_small + C_CVF + 1, [[3, 44]]), ALU.add,
           ["cvf"], ["ws"])
        tt(A.a32(o_ws, [[1, 44]]), A.a32(o_ws, [[1, 44]]), A.a32(o_small + C_CVF + 2, [[3, 44]]), ALU.add, ["ws", "cvf"], ["ws"])
        tt(small(C_CBW, 44), small(C_CB2, 44), A.a32(o_ws, [[1, 44]]), ALU.mult, ["ws", "cb2"], ["cbw"])
        ts(small(C_TMP + 3), small(C_FLAG), -1.0, ALU.add, ["flag"], ["fm1"])
        o_tc = A.f32(2)
        pcount = 0
        for g in range(5):
            w0c = 126 + 510 * g
            n = min(512, NOWN - w0c)
            m = n - 2
            norm_group(lambda k: x1_d[k * 128:(k + 1) * 128, w0c:w0c + n], n, C_A2, 0, "d", nbuf=1)
            for c in range(22):
                for which, fc in ((0, c), (1, 22 + c)):
                    psi = [0, 1, 6, 7, 3][pcount % 5]
                    pcount += 1
                    oy = o_yg if which == 0 else o_yv
                    YK = "yg" if which == 0 else "yv"
                    PK = "ps%d" % psi
                    for k in range(KC):
                        mm(P(psi, n), A.a16(o_wup + k * NF + fc * 128, [[1, 128]]), A.a16(o_hn[0] + k * 512, [[1, n]]),
                           k == 0, k == KC - 1, ["wup", "hn0"], [PK])
                    act(A.a32(oy, [[1, m]]), P(psi, m, c0=2), AF.Identity, [PK, "cvf", "cbw"], [YK],
                        bias=small(C_CBW + fc), scale=small(C_CVF + fc * 3 + 2))
                    stt(A.a32(oy, [[1, m]]), P(psi, m, c0=1), small(C_CVF + fc * 3 + 1), A.a32(oy, [[1, m]]),
                        ALU.mult, ALU.add, [PK, YK, "cvf"], [YK])
                    stt(A.a32(oy, [[1, m]]), P(psi, m, c0=0), small(C_CVF + fc * 3 + 0), A.a32(oy, [[1, m]]),
                        ALU.mult, ALU.add, [PK, YK, "cvf"], [YK])
                    if g == 0:
                        ts(A.a32(o_tc, [[1, 2]]), P(psi, 2), small(C_CB2 + fc), ALU.add, [PK, "cb2", "fm1"], ["tc"],
                           s2=small(C_TMP + 3), op1=ALU.mult)
                        stt(A.a32(oy, [[1, 2]]), A.a32(o_tc, [[1, 2]]), small(C_CVF + fc * 3 + 0), A.a32(oy, [[1, 2]]),
                            ALU.mult, ALU.add, ["tc", YK, "cvf"], [YK])
                        stt(A.a32(oy, [[1, 1]]), A.a32(o_tc + 1, [[1, 1]]), small(C_CVF + fc * 3 + 1), A.a32(oy, [[1, 1]]),
                            ALU.mult, ALU.add, ["tc", YK, "cvf"], [YK])
                act(A.a32(o_sg, [[1, m]]), A.a32(o_yg, [[1, m]]), AF.Silu, ["yg"], ["sg"])
                tt(A.a16(o_act + c * 512, [[1, m]]), A.a32(o_sg, [[1, m]]), A.a32(o_yv, [[1, m]]), ALU.mult,
                   ["sg", "yv"], ["act%d" % c], eng="pool")
            ACTK = ["act%d" % c for c in range(22)]
            for e_ in range(KC):
                psi = 4 + e_ % 2
                for c in range(22):
                    mm(P(psi, m), A.a16(o_wdn + c * 1024 + e_ * 128, [[1, 128]]), A.a16(o_act + c * 512, [[1, m]]),
                       c == 0, c == 21, ["wdn"] + ACTK, ["ps%d" % psi])
                stt(A.a32(o_xg[0] + e_ * 512 + 2, [[1, m]]), P(psi, m), small(C_MOD + 40 + e_),
                    A.a32(o_xg[0] + e_ * 512 + 2, [[1, m]]), ALU.mult, ALU.add,
                    ["ps%d" % psi, "mod", "xg0_%d" % e_], ["xg0_%d" % e_])
                dma(outT[e_ * 128:(e_ + 1) * 128, w0c + 2 - 128:w0c + n - 128], A.a32(o_xg[0] + e_ * 512 + 2, [[1, m]]),
                    ["xg0_%d" % e_], ["outT"], is_out=True)
        sc.finish()

        with nc.Block() as block:
            @block.tensor
            def _(e):
                sc.emit("pe", e)

            @block.scalar
            def _(e):
                sc.emit("act", e)

            @block.vector
            def _(e):
                sc.emit("dve", e)

            @block.gpsimd
            def _(e):
                sc.emit("pool", e)

            @block.sync
            def _(e):
                sc.emit("sp", e)
    return nc


_NC_CACHE = {}


def _consts():
    p = np.arange(128)
    tri = (p[:, None] <= p[None, :]).astype(np.float32)
    ones = np.ones((128, 128), np.float32)
    ident = np.eye(128, dtype=np.float32)
    blk = ((p[:, None] // 64) == (p[None, :] // 64)).astype(np.float32)
    cst = np.concatenate([tri, ones, ident, blk], axis=1)
    c = np.arange(512)
    masks = []
    for jm in range(4):
        masks.append(np.where(c[None, :] >= 128 * jm + p[:, None], 0.0, NEG).astype(np.float32))
    maskb = np.concatenate(masks + [ident], axis=1).astype(ml_dtypes.bfloat16)
    return cst, maskb


def prep_inputs(x, c, w_ada, b_ada, norm1_g, w_in, b_f, conv_a_w, q_norm_g, k_norm_g,
                w_branch_a, w_branch_b, w_out, norm2_g, w_up, conv_ffn_w, w_down):
    f = lambda a: np.ascontiguousarray(np.asarray(a, dtype=np.float32))
    x = f(x); c = f(c)
    cst, maskb = _consts()
    shared = {
        "w_ada": f(w_ada[0]),
        "b_adaT": f(np.asarray(b_ada[0]).reshape(48, 128).T),
        "g1T": f(np.asarray(norm1_g[0]).reshape(8, 128).T),
        "g2T": f(np.asarray(norm2_g[0]).reshape(8, 128).T),
        "w_in": f(w_in[0]),
        "bfT": f(np.asarray(b_f[0])[:, None]),
        "convA": f(np.asarray(conv_a_w[0]).reshape(3, 4, 128).transpose(2, 1, 0).reshape(128, 12)),
        "gqT": f(np.tile(np.asarray(q_norm_g[0]), 2)[:, None]),
        "gkT": f(np.tile(np.asarray(k_norm_g[0]), 2)[:, None]),
        "w_a": f(w_branch_a[0]),
        "w_b": f(w_branch_b[0]),
        "w_o": f(w_out[0]),
        "w_up": f(w_up[0]),
        "convF": f(np.asarray(conv_ffn_w[0]).reshape(3, 44, 128).transpose(2, 1, 0).reshape(128, 132)),
        "w_dn": f(w_down[0]),
        "cst": cst,
        "maskb": maskb,
    }
    in_maps = []
    pidx = np.arange(128)[:, None] + 128 * np.arange(NT)[None, :]
    for j in range(8):
        b, i = j // 4, j % 4
        own_end = 2048 * (i + 1)
        ws = own_end - S
        xTw = np.zeros((D, S), np.float32)
        t0 = max(ws, 0)
        xTw[:, t0 - ws:] = x[b, t0:own_end, :].T
        valid = ((np.arange(S) + ws) >= 0).astype(np.float32)
        m = dict(shared)
        m["xT"] = xTw
        m["validT"] = np.ascontiguousarray(np.broadcast_to(valid[None, :], (8, S)))
        m["kmask"] = np.where((pidx + ws) >= 0, 0.0, NEG).astype(np.float32)
        m["flag"] = np.full((128, 1), 1.0 if i > 0 else 0.0, np.float32)
        m["cT"] = f(c[b].reshape(8, 128).T)
        in_maps.append(m)
    return in_maps


def kernel(**inputs):
    in_maps = prep_inputs(**inputs)
    if "nc" not in _NC_CACHE:
        _NC_CACHE["nc"] = build_nc()
    nc = _NC_CACHE["nc"]
    res = run_bass_kernel_spmd(nc, in_maps, core_ids=list(range(8)))
    out = np.empty((2, S, D), np.float32)
    for j in range(8):
        b, i = j // 4, j % 4
        out[b, 2048 * i:2048 * (i + 1), :] = np.asarray(res.results[j]["outT"]).T
    return out
```
